# Optimizing a Trainium2 kernel written in Bass

```python
import jax, jax.numpy as jnp
from jax import lax
import numpy as np

D_MODEL = 1024
BATCH = 8
SEQ = 2048
DEPTH = 1

N_META = 16
EPS = 1e-6
ATT_HEADS = 8
ATT_KV_HEADS = 2
ATT_GROUP = ATT_HEADS // ATT_KV_HEADS
HEAD_DIM = 64
WINDOW = 128
ATT_BLOCK = 128
ROPE_THETA = 10000.0
GLA_HEADS = 4
GLA_DK = 256
GLA_DV = 512
GLA_DK_HEAD = GLA_DK // GLA_HEADS
GLA_DV_HEAD = GLA_DV // GLA_HEADS
GLA_RANK = 16
GLA_TAU = 16.0
GLA_CHUNK = 16
D_FF = 2816
CONV_WIDTH = 3

ATT_Q = ATT_HEADS * HEAD_DIM
ATT_KV = ATT_KV_HEADS * HEAD_DIM
SPLIT_SIZES = (ATT_Q, ATT_KV, ATT_KV, GLA_DK, GLA_DK, GLA_DV, GLA_DV, GLA_RANK, D_MODEL, D_MODEL)
SPLIT_POINTS = tuple(int(s) for s in np.cumsum(SPLIT_SIZES)[:-1])
D_IN = int(sum(SPLIT_SIZES))

kernel_name = "hybrid_swa_sink_gla_convglu_block"


def rmsnorm(x, g):
    xf = x.astype(jnp.float32)
    y = xf * lax.rsqrt(jnp.mean(xf * xf, axis=-1, keepdims=True) + EPS) * g.astype(jnp.float32)
    return y.astype(x.dtype)


def rope(x, pos):
    half = x.shape[-1] // 2
    inv_freq = ROPE_THETA ** (-jnp.arange(half, dtype=jnp.float32) / half)
    ang = pos.astype(jnp.float32)[:, None] * inv_freq[None, :]
    cos = jnp.concatenate([jnp.cos(ang), jnp.cos(ang)], -1)[None, :, None, :]
    sin = jnp.concatenate([jnp.sin(ang), jnp.sin(ang)], -1)[None, :, None, :]
    xf = x.astype(jnp.float32)
    rot = jnp.concatenate([-xf[..., half:], xf[..., :half]], -1)
    return (xf * cos + rot * sin).astype(x.dtype)


def swa_attention(q, k, v, sinks):
    B, L = q.shape[:2]
    pad = (-L) % ATT_BLOCK
    T = L + pad
    NB = T // ATT_BLOCK
    padf = lambda t: jnp.pad(t, ((0, 0), (pad, 0), (0, 0), (0, 0)))
    qb = padf(q).reshape(B, NB, ATT_BLOCK, ATT_KV_HEADS, ATT_GROUP, HEAD_DIM)
    kb = padf(k).reshape(B, NB, ATT_BLOCK, ATT_KV_HEADS, HEAD_DIM)
    vb = padf(v).reshape(B, NB, ATT_BLOCK, ATT_KV_HEADS, HEAD_DIM)
    prev = lambda t: jnp.concatenate([jnp.zeros_like(t[:, :1]), t[:, :-1]], axis=1)
    kk = jnp.concatenate([prev(kb), kb], axis=2)
    vv = jnp.concatenate([prev(vb), vb], axis=2)
    s = jnp.einsum('bnqkgd,bnskd->bnkgqs', qb, kk,
                   preferred_element_type=jnp.float32) * (HEAD_DIM ** -0.5)
    pos = jnp.arange(T) - pad
    qpos = pos.reshape(NB, ATT_BLOCK)
    kpos = jnp.concatenate([qpos - ATT_BLOCK, qpos], axis=1)
    rel = qpos[:, :, None] - kpos[:, None, :]
    valid = (rel >= 0) & (rel < WINDOW) & (kpos[:, None, :] >= 0)
    s = jnp.where(valid[None, :, None, None], s, -jnp.inf)
    sink = sinks.astype(jnp.float32).reshape(ATT_KV_HEADS, ATT_GROUP)[None, None, :, :, None, None]
    sink = jnp.broadcast_to(sink, s.shape[:-1] + (1,))
    p = jax.nn.softmax(jnp.concatenate([s, sink], axis=-1), axis=-1)[..., :-1]
    o = jnp.einsum('bnkgqs,bnskd->bnqkgd', p.astype(v.dtype), vv)
    return o.reshape(B, T, ATT_Q)[:, pad:]


def gla(q, k, v, log_a):
    B, L, H, Dk = q.shape
    Dv = v.shape[-1]
    C = GLA_CHUNK
    N = L // C
    to_chunks = lambda t: t.astype(jnp.float32).reshape(B, N, C, H, t.shape[-1]).transpose(1, 0, 3, 2, 4)
    qc, kc, vc, gc = (to_chunks(t) for t in (q * (Dk ** -0.5), k, v, log_a))
    causal = jnp.tril(jnp.ones((C, C), dtype=bool))[:, :, None]

    def step(S, inp):
        qi, ki, vi, gi = inp
        b = jnp.cumsum(gi, axis=2)
        o_inter = jnp.einsum('bhcd,bhde->bhce', qi * jnp.exp(b), S)
        diff = b[:, :, :, None, :] - b[:, :, None, :, :]
        decay = jnp.exp(jnp.where(causal, diff, -jnp.inf))
        a = jnp.einsum('bhid,bhjd,bhijd->bhij', qi, ki, decay)
        o = o_inter + jnp.einsum('bhij,bhje->bhie', a, vi)
        b_last = b[:, :, -1:, :]
        S = jnp.exp(b_last[:, :, 0, :])[..., None] * S + \
            jnp.einsum('bhcd,bhce->bhde', ki * jnp.exp(b_last - b), vi)
        return S, o

    S0 = jnp.zeros((B, H, Dk, Dv), jnp.float32)
    _, o = lax.scan(step, S0, (qc, kc, vc, gc))
    return o.transpose(1, 0, 3, 2, 4).reshape(B, L, H, Dv).astype(v.dtype)


def causal_dwconv(a, w, b):
    y = lax.conv_general_dilated(a, w[:, None, :].astype(a.dtype), window_strides=(1,),
                                 padding=[(CONV_WIDTH - 1, 0)],
                                 dimension_numbers=('NWC', 'WIO', 'NWC'),
                                 feature_group_count=a.shape[-1])
    return y + b.astype(a.dtype)


def setup_inputs(seed: int = 0) -> dict:
    key = jax.random.key(seed)
    ks = jax.random.split(key, 20)
    nrm = lambda k, shape, scale: jax.random.normal(k, shape, jnp.float32) * scale
    gain = lambda k, shape: 1.0 + 0.01 * jax.random.normal(k, shape, jnp.float32)
    return {
        "x": nrm(ks[0], (BATCH, SEQ, D_MODEL), 1.0),
        "meta_tokens": nrm(ks[1], (N_META, D_MODEL), 1.0),
        "mix_norm": gain(ks[2], (DEPTH, D_MODEL)),
        "w_in": nrm(ks[3], (DEPTH, D_MODEL, D_IN), D_MODEL ** -0.5),
        "b_in": nrm(ks[4], (DEPTH, D_IN), 0.01),
        "w_alpha": nrm(ks[5], (DEPTH, GLA_RANK, GLA_DK), GLA_RANK ** -0.5),
        "b_alpha": nrm(ks[6], (DEPTH, GLA_DK), 0.1),
        "attn_sinks": nrm(ks[7], (DEPTH, ATT_HEADS), 0.5),
        "gla_head_norm": gain(ks[8], (DEPTH, GLA_DV_HEAD)),
        "w_proj_attn": nrm(ks[9], (DEPTH, ATT_Q, D_MODEL), ATT_Q ** -0.5),
        "w_proj_gla": nrm(ks[10], (DEPTH, GLA_DV, D_MODEL), GLA_DV ** -0.5),
        "w_out": nrm(ks[11], (DEPTH, D_MODEL, D_MODEL), D_MODEL ** -0.5),
        "ffn_norm": gain(ks[12], (DEPTH, D_MODEL)),
        "w_up": nrm(ks[13], (DEPTH, D_MODEL, 2 * D_FF), D_MODEL ** -0.5),
        "conv_w": nrm(ks[14], (DEPTH, CONV_WIDTH, D_FF), CONV_WIDTH ** -0.5),
        "conv_b": nrm(ks[15], (DEPTH, D_FF), 0.01),
        "w_down": nrm(ks[16], (DEPTH, D_FF, D_MODEL), D_FF ** -0.5),
        "final_norm": gain(ks[17], (D_MODEL,)),
    }


def reference(x, meta_tokens, mix_norm, w_in, b_in, w_alpha, b_alpha, attn_sinks, gla_head_norm,
              w_proj_attn, w_proj_gla, w_out, ffn_norm, w_up, conv_w, conv_b, w_down, final_norm):
    B = x.shape[0]
    meta = jnp.broadcast_to(meta_tokens.astype(x.dtype)[None], (B, N_META, D_MODEL))
    h = jnp.concatenate([meta, x], axis=1)
    L = h.shape[1]
    pos = jnp.arange(L)
    for l in range(DEPTH):
        u = rmsnorm(h, mix_norm[l])
        proj = u @ w_in[l] + b_in[l]
        aq, ak, av, gq, gk, gv, gr, g_low, gate_a, gate_b = jnp.split(proj, SPLIT_POINTS, axis=-1)
        aq = rope(aq.reshape(B, L, ATT_HEADS, HEAD_DIM), pos)
        ak = rope(ak.reshape(B, L, ATT_KV_HEADS, HEAD_DIM), pos)
        av = av.reshape(B, L, ATT_KV_HEADS, HEAD_DIM)
        y_att = swa_attention(aq, ak, av, attn_sinks[l])
        z = (g_low @ w_alpha[l] + b_alpha[l]).astype(jnp.float32)
        log_a = (jax.nn.log_sigmoid(z) / GLA_TAU).reshape(B, L, GLA_HEADS, GLA_DK_HEAD)
        y_gla = gla(gq.reshape(B, L, GLA_HEADS, GLA_DK_HEAD), gk.reshape(B, L, GLA_HEADS, GLA_DK_HEAD),
                    gv.reshape(B, L, GLA_HEADS, GLA_DV_HEAD), log_a)
        y_gla = rmsnorm(y_gla, gla_head_norm[l]).reshape(B, L, GLA_DV) * jax.nn.silu(gr)
        mixed = jax.nn.sigmoid(gate_a) * (y_att @ w_proj_attn[l]) + \
            jax.nn.sigmoid(gate_b) * (y_gla @ w_proj_gla[l])
        h = h + mixed @ w_out[l]
        u = rmsnorm(h, ffn_norm[l])
        a, v = jnp.split(u @ w_up[l], 2, axis=-1)
        h = h + (jax.nn.gelu(causal_dwconv(a, conv_w[l], conv_b[l])) * v) @ w_down[l]
    return rmsnorm(h, final_norm)[:, N_META:]
```

```python
import contextlib
import numpy as np
import concourse.bass as bass
import concourse.mybir as mybir
from concourse.bass_utils import run_bass_kernel_spmd

F32 = mybir.dt.float32
BF16 = mybir.dt.bfloat16
AF = mybir.ActivationFunctionType
ALU = mybir.AluOpType
AX = mybir.AxisListType

D = 1024
SEQ = 2048
NMETA = 16
L = SEQ + NMETA
NT = 17
EPS = 1e-6
D_IN = 4368
D_FF = 2816
NCH = D_FF // 128

ENGS = ("pe", "act", "dve", "pool", "sp")
SAME_ENG_DIST = 3


class Buf:
    __slots__ = ("name", "writer", "readers", "psum")

    def __init__(self, name):
        self.name = name
        self.writer = None
        self.readers = []
        self.psum = False


class DmaSem:
    __slots__ = ("name", "count", "last", "handle")

    def __init__(self, name):
        self.name = name
        self.count = 0
        self.last = None
        self.handle = None


class Op:
    __slots__ = ("eng", "fn", "deps", "dsem", "dval", "signal", "idx", "sigval", "name")

    def __init__(self, eng, fn, name):
        self.eng = eng
        self.fn = fn
        self.deps = []
        self.dsem = None
        self.dval = 0
        self.signal = False
        self.idx = 0
        self.sigval = 0
        self.name = name


class K:
    def __init__(self, nc):
        self.nc = nc
        self.ops = {e: [] for e in ENGS}
        self.dsems = []
        self.nbuf = 0

    def buf(self, name=None):
        self.nbuf += 1
        return Buf(name or f"b{self.nbuf}")

    def dsem(self, name=None):
        d = DmaSem(name or f"d{len(self.dsems)}")
        self.dsems.append(d)
        return d

    limit = None
    nops = 0

    def op(self, eng, fn, reads=(), writes=(), dsem=None, name="", extra=(), force=False):
        self.nops += 1
        if self.limit is not None and self.nops > self.limit and not force:
            return None
        o = Op(eng, fn, name)
        lst = self.ops[eng]
        o.idx = len(lst)
        deps = list(extra)
        for b in list(reads) + list(writes):
            if b.writer is not None:
                deps.append(b.writer)
        for b in writes:
            deps.extend(b.readers)
        for b in reads:
            if b.psum:
                deps.extend(r for r in b.readers if r.eng != eng)
        if dsem is not None:
            o.dsem = dsem
            dsem.count += 16
            o.dval = dsem.count
            if dsem.last is not None:
                deps.append(dsem.last)
            dsem.last = o
        seen = set()
        for d in deps:
            if id(d) in seen or d is o:
                continue
            seen.add(id(d))
            if d.dsem is None and d.eng == eng:
                if eng == "pe":
                    continue
                if o.idx - d.idx > SAME_ENG_DIST:
                    continue
            o.deps.append(d)
            if d.dsem is None:
                d.signal = True
        for b in reads:
            b.readers.append(o)
        for b in writes:
            b.writer = o
            b.readers = []
        lst.append(o)
        return o

    def barrier(self):
        lasts = []
        for e in ENGS:
            for o in reversed(self.ops[e]):
                if o.dsem is None and o.fn is not None:
                    lasts.append(o)
                    break
        dl = [d.last for d in self.dsems if d.last is not None]
        for e in ENGS:
            o = Op(e, None, "barrier")
            o.idx = len(self.ops[e])
            for d in lasts:
                if d.eng != e:
                    o.deps.append(d)
                    d.signal = True
            o.deps.extend(dl)
            self.ops[e].append(o)

    def generate(self, final_waits=()):
        nc = self.nc
        with contextlib.ExitStack() as st:
            esem = {e: st.enter_context(nc.semaphore(f"s_{e}")) for e in ENGS}
            for d in self.dsems:
                d.handle = st.enter_context(nc.semaphore(f"dm_{d.name}"))
            for e in ENGS:
                c = 0
                for o in self.ops[e]:
                    if o.dsem is None and o.signal:
                        c += 1
                        o.sigval = c
            block = st.enter_context(nc.Block())

            def gen(ename, eng):
                seen = {}
                for o in self.ops[ename]:
                    need = {}
                    for d in o.deps:
                        if d.dsem is not None:
                            key, val, h = ("d", id(d.dsem)), d.dval, d.dsem.handle
                        else:
                            key, val, h = ("e", d.eng), d.sigval, esem[d.eng]
                        if val > need.get(key, (0, None))[0]:
                            need[key] = (val, h)
                    for key, (val, h) in need.items():
                        if seen.get(key, 0) >= val:
                            continue
                        seen[key] = val
                        eng.wait_ge(h, val)
                    if o.fn is None:
                        continue
                    ins = o.fn(eng)
                    if o.dsem is not None:
                        ins.then_inc(o.dsem.handle, 16)
                    elif o.signal:
                        ins.then_inc(esem[ename], 1)
                if ename == "sp":
                    for d in final_waits:
                        eng.wait_ge(d.handle, d.count)

            @block.tensor
            def _(e):
                gen("pe", e)

            @block.scalar
            def _(e):
                gen("act", e)

            @block.vector
            def _(e):
                gen("dve", e)

            @block.gpsimd
            def _(e):
                gen("pool", e)

            @block.sync
            def _(e):
                gen("sp", e)


class R:
    __slots__ = ("ap", "b")

    def __init__(self, ap, b):
        self.ap = ap
        self.b = b

    def __getitem__(self, idx):
        return self.ap[idx]


def tok0(t):
    return 0 if t == 0 else NMETA + 128 * (t - 1)


def tsz(t):
    return NMETA if t == 0 else 128


def build(debug=None, stop=None, ntiles=NT, limit=None):
    nc = bass.Bass("TRN2", target_bir_lowering=False)
    k = K(nc)
    k.limit = limit

    def din(name, shape):
        return nc.dram_tensor(name, list(shape), F32, kind="ExternalInput").ap()

    x_d = din("x", [SEQ, D])
    meta_d = din("meta_tokens", [NMETA, D])
    mixn_d = din("mix_norm", [1, D])
    win_d = din("w_in", [1, D, D_IN])
    bin_d = din("b_in", [1, D_IN])
    walpha_d = din("w_alpha", [1, 16, 256])
    balpha_d = din("b_alpha", [1, 256])
    sinks_d = din("attn_sinks", [1, 8])
    ghn_d = din("gla_head_norm", [1, 128])
    wpa_d = din("w_proj_attn", [1, 512, D])
    wpg_d = din("w_proj_gla", [1, 512, D])
    wout_d = din("w_out", [1, D, D])
    ffnn_d = din("ffn_norm", [1, D])
    wup_d = din("w_up", [1, D, 2 * D_FF])
    convw_d = din("conv_w", [1, 3, D_FF])
    convb_d = din("conv_b", [1, D_FF])
    wdn_d = din("w_down", [1, D_FF, D])
    fnorm_d = din("final_norm", [D])
    ident_d = din("c_ident", [128, 128])
    masks_d = din("c_masks", [3, 128, 128])
    utri_d = din("c_utri", [128, 128])
    rope_d = din("c_rope", [128, NT, 128])
    y_d = nc.dram_tensor("y", [SEQ, D], F32, kind="ExternalOutput").ap()
    dbg_d = None
    if debug:
        dbg_d = nc.dram_tensor("dbg", [L, D], F32, kind="ExternalOutput").ap()

    SB_BYTES = 212736
    big = nc.alloc_sbuf_tensor("sbig", [128, SB_BYTES // 2], BF16)
    cur = [0]

    def sb(shape, dt, name=None):
        n = 1
        for s in shape[1:]:
            n *= s
        isz = 4 if dt == F32 else 2
        nbytes = (n * isz + 31) // 32 * 32
        off = cur[0]
        cur[0] += nbytes
        assert cur[0] <= SB_BYTES, f"SBUF overflow at {name}: {cur[0]}"
        ap = big[0:shape[0], off // 2: off // 2 + n * isz // 2]
        if dt == F32:
            ap = ap.bitcast(F32)
        if len(shape) == 3:
            ap = ap.rearrange("p (a b) -> p a b", a=shape[1])
        elif len(shape) == 4:
            ap = ap.rearrange("p (a b c) -> p a b c", a=shape[1], b=shape[2])
        return R(ap, k.buf(name))

    banks = []
    for i in range(8):
        t = nc.alloc_psum_tensor(f"ps{i}", [128, 512], F32)
        banks.append((t, k.buf(f"ps{i}")))
        banks[-1][1].psum = True
    pools = {"all": [list(range(8)), 0], "A": [[0, 1], 0], "B1": [[2, 3], 0], "B2e": [[4, 5], 0], "B2o": [[6, 7], 0], "C": [[0, 1, 2, 3], 0], "D": [[4, 5, 6, 7], 0]}
    pool_cur = ["all"]

    class PS:
        __slots__ = ("f", "h", "b")

    def psn(pool=None):
        pl = pools[pool or pool_cur[0]]
        t, b = banks[pl[0][pl[1] % len(pl[0])]]
        pl[1] += 1
        p = PS()
        p.f = t[:, :]
        p.h = t[:, :].bitcast(BF16)
        p.b = b
        return p

    def rolling(streams, can_start):
        idx = {n: 0 for n in streams}
        cur_g = {n: None for n in streams}
        done = {n: 0 for n in streams}
        while True:
            progressed = False
            alive = False
            for n, (pool, facs) in streams.items():
                if cur_g[n] is None:
                    if idx[n] >= len(facs):
                        continue
                    alive = True
                    if not can_start(n, idx[n], done):
                        continue
                    cur_g[n] = facs[idx[n]]()
                    idx[n] += 1
                alive = True
                pool_cur[0] = pool
                try:
                    next(cur_g[n])
                except StopIteration:
                    cur_g[n] = None
                    done[n] += 1
                progressed = True
            if not alive:
                break
            assert progressed, "rolling scheduler deadlock"
        pool_cur[0] = "all"

    def interleave(*gens):
        gens = [g for g in gens if g is not None]
        while gens:
            for g in list(gens):
                try:
                    pool_cur[0] = g[0]
                    next(g[1])
                except StopIteration:
                    gens.remove(g)
        pool_cur[0] = "all"

    H = sb([128, NT, D], F32, "H")
    Hb = [k.buf(f"H{t}") for t in range(NT)]
    IDENT = sb([128, 128], BF16, "ident")
    MASKS = sb([128, 3, 128], BF16, "masks")
    UTRI = sb([128, 128], F32, "utri")
    ONES = sb([128, 128], BF16, "ones")
    NEGH = sb([128, 8], F32, "negh")
    GAINT = sb([128, 2, 8], F32, "gainT")
    ESINK = sb([128, 8], F32, "esink")
    STAT = sb([128, 8], F32, "stat")
    DENS = sb([128, 8], F32, "dens")
    S4 = sb([128, 8], F32, "s4")
    persist_end = cur[0]

    dq = {"n": 0}
    def cdma(eng, out_ap, in_ap, wr, slow=False):
        d = k.dsem(f"c{dq['n']}")
        dq["n"] += 1
        if slow:
            return k.op(eng, lambda e: e.dma_start(out=out_ap, in_=in_ap, allow_slow_non_contiguous=True), writes=wr, dsem=d)
        return k.op(eng, lambda e: e.dma_start(out=out_ap, in_=in_ap), writes=wr, dsem=d)

    cdma("pool", IDENT[:, :], ident_d, [IDENT.b])
    cdma("pool", MASKS[:, :, :], masks_d.rearrange("m p q -> p m q"), [MASKS.b])
    cdma("sp", UTRI[:, :], utri_d, [UTRI.b])
    cdma("sp", GAINT[:, 0, :], mixn_d[0].rearrange("(c p) -> p c", p=128), [GAINT.b], slow=True)
    GAINT2 = k.buf("gaint2")
    cdma("sp", GAINT[:, 1, :], ffnn_d[0].rearrange("(c p) -> p c", p=128), [GAINT2], slow=True)
    cdma("sp", ESINK[:, :], sinks_d.partition_broadcast(128), [ESINK.b])
    k.op("dve", lambda e: e.memset(ONES[:, :], 1.0), writes=[ONES.b])
    k.op("dve", lambda e: e.memset(NEGH[:, :], -0.5), writes=[NEGH.b])
    k.op("act", lambda e: e.activation(out=ESINK[:, :], in_=ESINK[:, :], func=AF.Exp), reads=[ESINK.b], writes=[ESINK.b])

    xds = [k.dsem(f"x{i}") for i in range(4)]
    k.op("sp", lambda e: e.dma_start(out=H[0:NMETA, 0, :], in_=meta_d), writes=[Hb[0]], dsem=xds[0])
    for t in range(1, NT):
        k.op("sp", lambda e, t=t: e.dma_start(out=H[:, t, :], in_=x_d[(t - 1) * 128: t * 128, :]),
             writes=[Hb[t]], dsem=xds[t % 4])

    def early_exit():
        k.barrier()
        dd_ = k.dsem("early")
        for t in range(1, NT):
            k.op("sp", lambda e, t=t: e.dma_start(out=y_d[(t - 1) * 128:t * 128, :], in_=H[:, t, :]), reads=[Hb[t]], dsem=dd_, force=True)
        k.generate(final_waits=[dd_])
        return nc

    if stop == "setup":
        return early_exit()

    def stat(col, n=1):
        return STAT[:, col:col + n]

    act_rsqrt = [False]

    def rms_rstd(src_ap, P, junk_ap, col, rd, wr, nfeat, STAT=STAT):
        ss = STAT[:P, col:col + 1]
        rs = STAT[:P, col + 1:col + 2]
        k.op("act", lambda e: e.activation(out=junk_ap, in_=src_ap, func=AF.Square, accum_out=ss),
             reads=rd, writes=wr + [STAT.b])
        if act_rsqrt[0]:
            k.op("act", lambda e: e.activation(out=rs, in_=ss, func=AF.Ln, scale=1.0 / nfeat, bias=EPS), reads=[STAT.b], writes=[STAT.b])
            k.op("act", lambda e: e.activation(out=rs, in_=rs, func=AF.Exp, scale=-0.5), reads=[STAT.b], writes=[STAT.b])
            return rs
        k.op("dve", lambda e: e.tensor_scalar(out=rs, in0=ss, scalar1=1.0 / nfeat, scalar2=EPS, op0=ALU.mult, op1=ALU.add),
             reads=[STAT.b], writes=[STAT.b])
        k.op("pool", lambda e: e.tensor_tensor(out=rs, in0=rs, in1=NEGH[:P, 0:1], op=ALU.pow),
             reads=[STAT.b, NEGH.b], writes=[STAT.b])
        return rs

    def norm_pre(t, U, ST=STAT):
        P = tsz(t)
        rs = rms_rstd(H[:P, t, :], P, U[:P, :], 0, [Hb[t]], [U.b], D, STAT=ST)
        k.op("dve", lambda e: e.tensor_scalar_mul(out=U[:P, :], in0=H[:P, t, :], scalar1=rs),
             reads=[Hb[t], ST.b], writes=[U.b])

    def norm_T(t, gi, U, dstT_ap, dst_b):
        P = tsz(t)
        ps = psn()
        pv = ps.h.rearrange("p (c q) -> p c q", c=8)
        for c in range(8):
            k.op("pe", lambda e, c=c: e.transpose(out=pv[:, c, :P], in_=U[:P, c * 128:(c + 1) * 128], identity=IDENT[:P, :P]),
                 reads=[U.b, IDENT.b], writes=[ps.b])
        k.op("dve", lambda e: e.tensor_tensor(out=dstT_ap, in0=pv[:, :, :P],
                                              in1=GAINT[:, gi, :].unsqueeze(2).broadcast_to([128, 8, P]), op=ALU.mult),
             reads=[ps.b, GAINT.b, GAINT2], writes=[dst_b])

    def transposes(src, n, P, rd):
        ps = psn()
        pv = ps.h.rearrange("p (c q) -> p c q", c=8)
        for c in range(n):
            k.op("pe", lambda e, c=c: e.transpose(out=pv[:, c, :P], in_=src[:P, c * 128:(c + 1) * 128], identity=IDENT[:P, :P]),
                 reads=rd + [IDENT.b], writes=[ps.b])
        return ps, pv

    YA = sb([128, 4, L], BF16, "YA")
    YG = sb([128, 4, L], BF16, "YG")
    p1_end = cur[0]

    NA = 2320
    wina_off = cur[0]
    WINA = sb([128, 8, NA], BF16, "winA")
    BIASA = sb([1, NA], BF16, "biasA")
    WALPHA = sb([32, 256], BF16, "walpha")
    rope_off = cur[0]
    ROPE = sb([128, NT, 128], F32, "rope")
    SST = sb([128, 2, 128], F32, "S")
    SBF_ = [sb([128, 2, 128], BF16, f"Sbf{i}") for i in range(3)]
    k.op("dve", lambda e: e.memset(SST[:, :, :], 0.0), writes=[SST.b])
    k.op("pool", lambda e: e.memset(SBF_[0][:, :, :], 0.0), writes=[SBF_[0].b])

    u_off = cur[0]
    U_ = [sb([128, D], BF16, f"U{i}") for i in range(2)]
    UT_ = [sb([128, 8, 128], BF16, f"UT{i}") for i in range(2)]
    T1 = sb([128, 640], F32, "T1")
    T2 = sb([128, 640], F32, "T2")
    TH = R(T1.ap[:, 0:512], T1.b)
    QR = sb([128, 512], BF16, "QR")
    KR = sb([128, 128], BF16, "KR")
    GL = sb([128, 16], BF16, "GL")
    QT_ = [sb([128, 4, 128], BF16, f"QT{i}") for i in range(3)]
    KT_ = [sb([128, 128], BF16, f"KT{i}") for i in range(4)]
    VA_ = [sb([128, 2, 66], BF16, f"VA{i}") for i in range(4)]
    GLT_ = [sb([32, 128], BF16, f"GLT{i}") for i in range(3)]
    GQK_ = [sb([128, 512], BF16, f"GQK{i}") for i in range(3)]
    V_ = [sb([128, 512], BF16, f"V{i}") for i in range(3)]
    SG_ = [sb([128, 512], BF16, f"SG{i}") for i in range(3)]
    PT_ = [[sb([128, 4, 128], BF16, f"PT{g}{b}") for b in range(2)] for g in range(2)]
    YATT = sb([128, 512], BF16, "YATT")
    OA = sb([128, 8, 66], F32, "OA")
    SPt_ = [sb([128, 256], F32, f"SP{i}") for i in range(2)]
    ENB_ = [sb([128, 256], F32, f"ENB{i}") for i in range(2)]
    EBT_ = [sb([128, 2, 128], F32, f"EBT{i}") for i in range(2)]
    ENBT_ = [sb([128, 2, 128], F32, f"ENBT{i}") for i in range(2)]
    QET_ = [sb([128, 2, 128], BF16, f"QET{i}") for i in range(2)]
    KET_ = [sb([128, 2, 128], BF16, f"KET{i}") for i in range(2)]
    KE_ = [sb([128, 256], BF16, f"KE{i}") for i in range(2)]
    AT_ = [sb([128, 4, 128], BF16, f"AT{i}") for i in range(2)]
    OSQ_ = [sb([128, 512], F32, f"OSQ{i}") for i in range(1)] * 2
    YGT_ = [sb([128, 512], BF16, f"YGT{i}") for i in range(1)] * 2
    S4_ = [sb([128, 8], F32, f"S4{i}") for i in range(1)] * 2
    sdone = [False] * NT
    for i in range(4):
        k.op("pool", lambda e, i=i: e.memset(VA_[i][:, :, :], 1.0), writes=[VA_[i].b])
    for i in range(3):
        k.op("pool", lambda e, i=i: e.memset(GLT_[i][:, :], 1.0), writes=[GLT_[i].b])

    segs = [(0, 0, 512), (512, 512, 256), (768, 2304, 16), (784, 768, 512), (1296, 1280, 512), (1808, 1792, 512)]
    WAb = [k.buf(f"winA{i}") for i in range(5)]
    WAb1b = k.buf("winA1b")
    seg_buf = [WAb[0], WAb[1], WAb1b, WAb[2], WAb[3], WAb[4]]
    wds = [k.dsem(f"w{i}") for i in range(6)]
    BIASAb = [k.buf(f"biasA{i}") for i in range(6)]
    bias_tok = {0: [BIASAb[0]], 512: [BIASAb[1], BIASAb[2]], 784: [BIASAb[3]], 1296: [BIASAb[4]], 1808: [BIASAb[5]]}
    win_v = win_d[0].rearrange("(kc p) c -> p kc c", p=128)
    def load_seg(i):
        d0, s0, n = segs[i]
        k.op("pool", lambda e: e.dma_start(out=WINA[:, :, d0:d0 + n], in_=win_v[:, :, s0:s0 + n]),
             writes=[seg_buf[i]], dsem=wds[i])
        cdma("pool", BIASA[0:1, d0:d0 + n], bin_d[0:1, s0:s0 + n], [BIASAb[i]])

    load_seg(0)
    WALPHA2 = k.buf("walpha_b")
    cdma("pool", WALPHA[0:16, :], walpha_d[0], [WALPHA.b])
    cdma("pool", WALPHA[16:17, :], balpha_d[0:1, :], [WALPHA2])
    cdma("sp", ROPE[:, :, :], rope_d, [ROPE.b])

    def proj_group(UT, P, W, kc_n, c0, n, wb, bias):
        ps = psn()
        for kc in range(kc_n):
            k.op("pe", lambda e, kc=kc: e.matmul(ps.f[:P, :n], lhsT=UT[:, kc, :P], rhs=W[:, kc, c0:c0 + n],
                                               start=(kc == 0), stop=(bias is None and kc == kc_n - 1)),
                 reads=[UT.b] + (wb if isinstance(wb, list) else [wb]), writes=[ps.b])
        if bias is not None:
            btok = bias_tok[c0] if bias is BIASA else [bias.b]
            k.op("pe", lambda e: e.matmul(ps.f[:P, :n], lhsT=ONES[0:1, :P], rhs=bias[0:1, c0:c0 + n], start=False, stop=True),
                 reads=[ONES.b] + btok, writes=[ps.b])
        return ps

    def stage_a(t):
        P = tsz(t)
        U, UT = U_[t % 2], UT_[t % 2]
        QT, KT, VA, GLT, GQK, V, SG = QT_[t % 3], KT_[t % 4], VA_[t % 4], GLT_[t % 3], GQK_[t % 3], V_[t % 3], SG_[t % 3]
        if t + 1 < ntiles:
            norm_pre(t + 1, U_[(t + 1) % 2])
        ps = proj_group(UT, P, WINA, 8, 0, 512, WAb[0], BIASA)
        yield
        qin = ps.f[:P, 0:512].rearrange("p (g i d) -> p i g d", g=2, i=4)
        qin5 = ps.f[:P, 0:512].rearrange("p (g i w d) -> p i g w d", g=2, i=4, w=2)
        t1v = T1[:P, 0:512].rearrange("p (i g d) -> p i g d", i=4, g=2)
        t2v = T2[:P, 0:512].rearrange("p (i g w d) -> p i g w d", i=4, g=2, w=2)
        cosf = ROPE[:P, t, 0:64]
        sin = ROPE[:P, t, 64:96]
        nsin = ROPE[:P, t, 96:128]
        k.op("dve", lambda e: e.tensor_tensor(out=t1v, in0=qin, in1=cosf.unsqueeze(1).unsqueeze(1).broadcast_to([P, 4, 2, 64]), op=ALU.mult),
             reads=[ps.b, ROPE.b], writes=[T1.b])
        k.op("dve", lambda e: e.tensor_tensor(out=t2v[:, :, :, 0, :], in0=qin5[:, :, :, 1, :],
                                              in1=nsin.unsqueeze(1).unsqueeze(1).broadcast_to([P, 4, 2, 32]), op=ALU.mult),
             reads=[ps.b, ROPE.b], writes=[T2.b])
        k.op("dve", lambda e: e.tensor_tensor(out=t2v[:, :, :, 1, :], in0=qin5[:, :, :, 0, :],
                                              in1=sin.unsqueeze(1).unsqueeze(1).broadcast_to([P, 4, 2, 32]), op=ALU.mult),
             reads=[ps.b, ROPE.b], writes=[T2.b])
        ps1 = proj_group(UT, P, WINA, 8, 512, 272, [WAb[1], WAb1b], BIASA)
        yield
        kin = ps1.f[:P, 0:128].rearrange("p (g d) -> p g d", g=2)
        kin4 = ps1.f[:P, 0:128].rearrange("p (g w d) -> p g w d", g=2, w=2)
        t1k = T1[:P, 512:640].rearrange("p (g d) -> p g d", g=2)
        t2k = T2[:P, 512:640].rearrange("p (g w d) -> p g w d", g=2, w=2)
        k.op("dve", lambda e: e.tensor_tensor(out=t1k, in0=kin, in1=cosf.unsqueeze(1).broadcast_to([P, 2, 64]), op=ALU.mult),
             reads=[ps1.b, ROPE.b], writes=[T1.b])
        k.op("dve", lambda e: e.tensor_tensor(out=t2k[:, :, 0, :], in0=kin4[:, :, 1, :], in1=nsin.unsqueeze(1).broadcast_to([P, 2, 32]), op=ALU.mult),
             reads=[ps1.b, ROPE.b], writes=[T2.b])
        k.op("dve", lambda e: e.tensor_tensor(out=t2k[:, :, 1, :], in0=kin4[:, :, 0, :], in1=sin.unsqueeze(1).broadcast_to([P, 2, 32]), op=ALU.mult),
             reads=[ps1.b, ROPE.b], writes=[T2.b])
        k.op("dve", lambda e: e.tensor_copy(out=VA[:P, :, 0:64], in_=ps1.f[:P, 128:256].rearrange("p (g d) -> p g d", g=2)),
             reads=[ps1.b], writes=[VA.b])
        k.op("dve", lambda e: e.tensor_copy(out=GL[:P, :], in_=ps1.f[:P, 256:272]), reads=[ps1.b], writes=[GL.b])
        ps2 = proj_group(UT, P, WINA, 8, 784, 512, WAb[2], BIASA)
        yield
        k.op("pool", lambda e: e.tensor_tensor(out=QR[:P, :], in0=T1[:P, 0:512], in1=T2[:P, 0:512], op=ALU.add),
             reads=[T1.b, T2.b], writes=[QR.b])
        k.op("pool", lambda e: e.tensor_tensor(out=KR[:P, :], in0=T1[:P, 512:640], in1=T2[:P, 512:640], op=ALU.add),
             reads=[T1.b, T2.b], writes=[KR.b])
        k.op("act", lambda e: e.copy(out=GQK[:P, :], in_=ps2.f[:P, :]), reads=[ps2.b], writes=[GQK.b])
        ps3 = proj_group(UT, P, WINA, 8, 1296, 512, WAb[3], BIASA)
        yield
        k.op("act", lambda e: e.copy(out=V[:P, :], in_=ps3.f[:P, :]), reads=[ps3.b], writes=[V.b])
        ps4 = proj_group(UT, P, WINA, 8, 1808, 512, WAb[4], BIASA)
        yield
        k.op("act", lambda e: e.activation(out=TH[:P, :], in_=ps4.f[:P, :], func=AF.Exp, scale=-1.0), reads=[ps4.b], writes=[TH.b])
        pst = psn()
        ptv = pst.h.rearrange("p (c q) -> p c q", c=8)
        for c in range(4):
            k.op("pe", lambda e, c=c: e.transpose(out=ptv[:, c, :P], in_=QR[:P, c * 128:(c + 1) * 128], identity=IDENT[:P, :P]),
                 reads=[QR.b, IDENT.b], writes=[pst.b])
        k.op("pe", lambda e: e.transpose(out=ptv[:, 4, :P], in_=KR[:P, :], identity=IDENT[:P, :P]),
             reads=[KR.b, IDENT.b], writes=[pst.b])
        k.op("pe", lambda e: e.transpose(out=ptv[0:16, 5, :P], in_=GL[:P, :], identity=IDENT[:P, :P]),
             reads=[GL.b, IDENT.b], writes=[pst.b])
        yield
        k.op("dve", lambda e: e.tensor_scalar_add(out=TH[:P, :], in0=TH[:P, :], scalar1=1.0), reads=[TH.b], writes=[TH.b])
        k.op("dve", lambda e: e.reciprocal(out=TH[:P, :], in_=TH[:P, :]), reads=[TH.b], writes=[TH.b])
        k.op("dve", lambda e: e.tensor_tensor(out=SG[:P, :], in0=TH[:P, :], in1=ps4.f[:P, :], op=ALU.mult),
             reads=[TH.b, ps4.b], writes=[SG.b])
        k.op("act", lambda e: e.copy(out=QT[:, :, :P], in_=ptv[:, 0:4, :P]), reads=[pst.b], writes=[QT.b])
        k.op("act", lambda e: e.copy(out=KT[:, :P], in_=ptv[:, 4, :P]), reads=[pst.b], writes=[KT.b])
        k.op("act", lambda e: e.copy(out=GLT[0:16, :P], in_=ptv[0:16, 5, :P]), reads=[pst.b], writes=[GLT.b])
        if t + 1 < ntiles:
            P1 = tsz(t + 1)
            U1, UT1 = U_[(t + 1) % 2], UT_[(t + 1) % 2]
            psu = psn()
            puv = psu.h.rearrange("p (c q) -> p c q", c=8)
            for c in range(8):
                k.op("pe", lambda e, c=c: e.transpose(out=puv[:, c, :P1], in_=U1[:P1, c * 128:(c + 1) * 128], identity=IDENT[:P1, :P1]),
                     reads=[U1.b, IDENT.b], writes=[psu.b])
            yield
            k.op("dve", lambda e: e.tensor_tensor(out=UT1[:, :, :P1], in0=puv[:, :, :P1],
                                                  in1=GAINT[:, 0, :].unsqueeze(2).broadcast_to([128, 8, P1]), op=ALU.mult),
                 reads=[psu.b, GAINT.b, GAINT2], writes=[UT1.b])
        yield

    def stage_att(t):
        P = tsz(t)
        QT = QT_[t % 3]
        t0 = tok0(t)
        blocks = ([] if t == 0 else [(t - 1, 0)]) + [(t, 1)]
        combos = [(g, tk, kb) for g in range(2) for (tk, kb) in blocks]
        pss_l = []
        def emit_score(c):
            g, tk, kb = combos[c]
            Pk = tsz(tk)
            pss = psn()
            sv = pss.f[:Pk, 0:4 * P].rearrange("p (i q) -> p i q", i=4)
            KTk = KT_[tk % 4]
            k.op("pe", lambda e: e.matmul(sv, lhsT=KTk[g * 64:(g + 1) * 64, :Pk], rhs=QT[g * 64:(g + 1) * 64, :, :P], start=True, stop=True),
                 reads=[KTk.b, QT.b], writes=[pss.b])
            pss_l.append((pss, sv, Pk, PT_[g][kb], kb))
        def emit_exp(c):
            pss, sv, Pk, PT, kb = pss_l[c]
            k.op("act", lambda e: e.activation(out=PT[:Pk, :, :P], in_=sv, func=AF.Exp, scale=0.125), reads=[pss.b], writes=[PT.b])
        def emit_mask(c):
            pss, sv, Pk, PT, kb = pss_l[c]
            if kb == 1:
                mk = MASKS[:Pk, 0, :P]
            else:
                mk = MASKS[:Pk, 2, :P] if t == 1 else MASKS[:Pk, 1, :P]
            k.op("dve", lambda e: e.tensor_tensor(out=PT[:Pk, :, :P], in0=PT[:Pk, :, :P], in1=mk.unsqueeze(1).broadcast_to([Pk, 4, P]), op=ALU.mult),
                 reads=[PT.b, MASKS.b], writes=[PT.b])
        nc_ = len(combos)
        for step in range(nc_ + 2):
            if step < nc_:
                emit_score(step)
            if 0 <= step - 1 < nc_:
                emit_exp(step - 1)
            if 0 <= step - 2 < nc_:
                emit_mask(step - 2)
            yield
        pso = [psn(), psn()]
        for g in range(2):
            for i in range(4):
                for bi, (tk, kb) in enumerate(blocks):
                    Pk = tsz(tk)
                    PT = PT_[g][kb]
                    VAk = VA_[tk % 4]
                    k.op("pe", lambda e, g=g, i=i, Pk=Pk, PT=PT, VAk=VAk, bi=bi: e.matmul(
                        pso[g].f[:P, i * 128:i * 128 + 66], lhsT=PT[:Pk, i, :P], rhs=VAk[:Pk, g, :],
                        start=(bi == 0), stop=(bi == len(blocks) - 1)),
                        reads=[PT.b, VAk.b], writes=[pso[g].b])
        yield
        for g in range(2):
            k.op("act", lambda e, g=g: e.copy(out=OA[:P, g * 4:(g + 1) * 4, :], in_=pso[g].f[:P, :].rearrange("p (i c) -> p i c", i=4)[:, :, 0:66]),
                 reads=[pso[g].b], writes=[OA.b])
        yield
        DEN = DENS[:P, 0:8]
        oav = OA[:P, :, :]
        k.op("dve", lambda e: e.tensor_tensor(out=DEN, in0=oav[:, :, 64], in1=ESINK[:P, :], op=ALU.add),
             reads=[OA.b, ESINK.b], writes=[DENS.b])
        k.op("dve", lambda e: e.reciprocal(out=DEN, in_=DEN), reads=[DENS.b], writes=[DENS.b])
        k.op("dve", lambda e: e.tensor_tensor(
            out=YATT[:P, :].rearrange("p (h d) -> p h d", h=8), in0=oav[:, :, 0:64],
            in1=DEN.unsqueeze(2).broadcast_to([P, 8, 64]), op=ALU.mult),
            reads=[OA.b, DENS.b], writes=[YATT.b])
        yield
        psy, pyv = transposes(YATT, 4, P, [YATT.b])
        yield
        yield
        k.op("act", lambda e: e.copy(out=YA[:, :, t0:t0 + P], in_=pyv[:, 0:4, :P]), reads=[psy.b], writes=[YA.b])
        yield

    def stage_gla(t):
        P = tsz(t)
        GLT, GQK, V, SG = GLT_[t % 3], GQK_[t % 3], V_[t % 3], SG_[t % 3]
        j2 = t % 2
        SPt, ENB, EBT, ENBT, QET, KET, KE, AT, OSQ, YGT, S4 = (SPt_[j2], ENB_[j2], EBT_[j2], ENBT_[j2], QET_[j2], KET_[j2], KE_[j2],
                                                                 AT_[j2], OSQ_[j2], YGT_[j2], S4_[j2])
        t0 = tok0(t)
        psz = psn()
        k.op("pe", lambda e: e.matmul(psz.f[:P, 0:256], lhsT=GLT[0:17, :P], rhs=WALPHA[0:17, :], start=True, stop=True),
             reads=[GLT.b, WALPHA.b, WALPHA2], writes=[psz.b])
        psq, pqv = transposes(GQK, 4, P, [GQK.b])
        yield
        k.op("act", lambda e: e.activation(out=SPt[:P, :], in_=psz.f[:P, 0:256], func=AF.Exp, scale=-1.0), reads=[psz.b], writes=[SPt.b])
        k.op("act", lambda e: e.activation(out=SPt[:P, :], in_=SPt[:P, :], func=AF.Ln, bias=1.0), reads=[SPt.b], writes=[SPt.b])
        yield
        psb = psn()
        bTv = psb.f[:, 256:512].rearrange("p (c q) -> p c q", c=2)
        k.op("pe", lambda e: e.matmul(psb.f[:P, 0:256], lhsT=UTRI[:P, :P], rhs=SPt[:P, :], start=True, stop=True),
             reads=[UTRI.b, SPt.b], writes=[psb.b])
        for pr in range(2):
            k.op("pe", lambda e, pr=pr: e.matmul(bTv[:, pr, :P], lhsT=SPt[:P, pr * 128:(pr + 1) * 128], rhs=UTRI[:P, :P], start=True, stop=True),
                 reads=[UTRI.b, SPt.b], writes=[psb.b])
        yield
        k.op("act", lambda e: e.activation(out=ENB[:P, :], in_=psb.f[:P, 0:256], func=AF.Exp, scale=-1.0), reads=[psb.b], writes=[ENB.b])
        k.op("act", lambda e: e.activation(out=EBT[:, :, :P], in_=bTv[:, :, :P], func=AF.Exp), reads=[psb.b], writes=[EBT.b])
        k.op("act", lambda e: e.activation(out=ENBT[:, :, :P], in_=bTv[:, :, :P], func=AF.Exp, scale=-1.0), reads=[psb.b], writes=[ENBT.b])
        yield
        k.op("pool", lambda e: e.tensor_tensor(out=KE[:P, :], in0=GQK[:P, 256:512], in1=ENB[:P, :], op=ALU.mult),
             reads=[GQK.b, ENB.b], writes=[KE.b])
        k.op("dve", lambda e: e.scalar_tensor_tensor(out=QET[:, :, :P], in0=pqv[:, 0:2, :P], scalar=0.125, in1=EBT[:, :, :P], op0=ALU.mult, op1=ALU.mult),
             reads=[psq.b, EBT.b], writes=[QET.b])
        k.op("dve", lambda e: e.tensor_tensor(out=KET[:, :, :P], in0=pqv[:, 2:4, :P], in1=ENBT[:, :, :P], op=ALU.mult),
             reads=[psq.b, ENBT.b], writes=[KET.b])
        yield
        while t > 0 and not sdone[t - 1]:
            yield
        if t < NT - 1:
            psd = psn()
            dsv = psd.f[:, :].rearrange("p (h e) -> p h e", h=4)
            for h in range(4):
                k.op("pe", lambda e, h=h: e.matmul(dsv[:, h, :], lhsT=KE[:P, (h // 2) * 128:(h // 2 + 1) * 128], rhs=V[:P, h * 128:(h + 1) * 128], start=True, stop=True),
                     reads=[KE.b, V.b], writes=[psd.b])
            yield
            dsv2 = psd.f[:, :].rearrange("p (c w e) -> p w c e", c=2, w=2)
            SBn = SBF_[(t + 1) % 3]
            for par in range(2):
                r0 = par * 64
                k.op("dve", lambda e, par=par, r0=r0: e.tensor_tensor(out=SST[r0:r0 + 64, :, :], in0=dsv2[r0:r0 + 64, par, :, :], in1=SST[r0:r0 + 64, :, :], op=ALU.add),
                     reads=[psd.b, SST.b], writes=[SST.b])
            for par in range(2):
                r0 = par * 64
                k.op("dve", lambda e, r0=r0: e.tensor_tensor(out=SST[r0:r0 + 64, :, :], in0=SST[r0:r0 + 64, :, :],
                                                          in1=EBT[r0:r0 + 64, :, P - 1:P].broadcast_to([64, 2, 128]), op=ALU.mult),
                     reads=[SST.b, EBT.b], writes=[SST.b])
            k.op("dve", lambda e: e.tensor_copy(out=SBn[:, :, :], in_=SST[:, :, :]), reads=[SST.b], writes=[SBn.b])
        sdone[t] = True
        yield
        psa = [psn(), psn()]
        for h in range(4):
            r0 = (h % 2) * 64
            avh = psa[h % 2].f[:P, 0:2 * P].rearrange("p (c q) -> p c q", c=2)
            k.op("pe", lambda e, h=h, r0=r0, avh=avh: e.matmul(avh[:, h // 2, :], lhsT=KET[r0:r0 + 64, h // 2, :P], rhs=QET[r0:r0 + 64, h // 2, :P], start=True, stop=True),
                 reads=[KET.b, QET.b], writes=[psa[h % 2].b])
        yield
        atv = AT[:P, :, :P].rearrange("p (c w) q -> p w c q", w=2)
        for par in range(2):
            avp = psa[par].f[:P, 0:2 * P].rearrange("p (c q) -> p c q", c=2)
            k.op("dve", lambda e, par=par, avp=avp: e.tensor_tensor(out=atv[:, par, :, :], in0=avp, in1=MASKS[:P, 0, :P].unsqueeze(1).broadcast_to([P, 2, P]), op=ALU.mult),
                 reads=[psa[par].b, MASKS.b], writes=[AT.b])
        yield
        SBc = SBF_[t % 3]
        psg = psn()
        ogv = psg.f[:P, :].rearrange("p (h e) -> p h e", h=4)
        for h in range(4):
            r0 = (h % 2) * 64
            if t > 0:
                k.op("pe", lambda e, h=h, r0=r0: e.matmul(ogv[:, h, :], lhsT=QET[r0:r0 + 64, h // 2, :P], rhs=SBc[r0:r0 + 64, h // 2, :], start=True, stop=False),
                     reads=[QET.b, SBc.b], writes=[psg.b])
            k.op("pe", lambda e, h=h: e.matmul(ogv[:, h, :], lhsT=AT[:P, h, :P], rhs=V[:P, h * 128:(h + 1) * 128], start=(t == 0), stop=True),
                 reads=[AT.b, V.b], writes=[psg.b])
        yield
        SS4 = S4[:P, 0:4]
        RS4 = S4[:P, 4:8]
        k.op("act", lambda e: e.activation(out=OSQ[:P, :], in_=psg.f[:P, :], func=AF.Square), reads=[psg.b], writes=[OSQ.b])
        yield
        k.op("dve", lambda e: e.reduce_sum(out=SS4, in_=OSQ[:P, :].rearrange("p (h e) -> p h e", h=4), axis=AX.X),
             reads=[OSQ.b], writes=[S4.b])
        yield
        k.op("act", lambda e: e.activation(out=RS4, in_=SS4, func=AF.Ln, scale=1.0 / 128, bias=EPS), reads=[S4.b], writes=[S4.b])
        k.op("act", lambda e: e.activation(out=RS4, in_=RS4, func=AF.Exp, scale=-0.5), reads=[S4.b], writes=[S4.b])
        yield
        k.op("dve", lambda e: e.tensor_tensor(out=OSQ[:P, :].rearrange("p (h e) -> p h e", h=4), in0=ogv,
                                              in1=RS4.unsqueeze(2).broadcast_to([P, 4, 128]), op=ALU.mult),
             reads=[psg.b, S4.b, OSQ.b], writes=[OSQ.b])
        yield
        k.op("pool", lambda e: e.tensor_tensor(out=YGT[:P, :], in0=OSQ[:P, :], in1=SG[:P, :], op=ALU.mult),
             reads=[OSQ.b, SG.b], writes=[YGT.b])
        yield
        psy2, pyv2 = transposes(YGT, 4, P, [YGT.b])
        yield
        yield
        k.op("act", lambda e: e.copy(out=YG[:, :, t0:t0 + P], in_=pyv2[:, 0:4, :P]), reads=[psy2.b], writes=[YG.b])
        yield

    act_rsqrt[0] = True
    norm_pre(0, U_[0])
    for i in range(1, len(segs)):
        load_seg(i)
    pool_cur[0] = "A"
    norm_T(0, 0, U_[0], UT_[0][:, :, :tsz(0)], UT_[0].b)

    def can_start(n, i, done):
        if n == "A":
            if i < 3:
                return True
            m = i - 3
            return done["B1"] >= i - 2 and done["B2e"] >= m // 2 + 1 and done["B2o"] >= (m + 1) // 2
        if n == "B1":
            return done["A"] >= i + 1
        tile = 2 * i if n == "B2e" else 2 * i + 1
        return done["A"] >= tile + 1

    def sb_at(off, shape, dt, name):
        save = cur[0]
        cur[0] = off
        r = sb(shape, dt, name)
        cur[0] = save
        return r

    WINB = sb_at(wina_off, [128, 8, 2048], BF16, "winB")
    WPA = sb_at(u_off, [128, 4, D], BF16, "wpa")
    WPG = sb_at(rope_off, [128, 4, D], BF16, "wpg")
    WBb = [k.buf(f"winB{i}") for i in range(4)]
    wbd = [k.dsem(f"wb{i}") for i in range(4)]
    wina_tokens = WAb + [WAb1b]

    def prefetch_gen():
        for i in range(4):
            k.op("pool", lambda e, i=i: e.dma_start(out=WINB[:, :, i * 512:(i + 1) * 512], in_=win_v[:, :, 2320 + i * 512: 2320 + (i + 1) * 512]),
                 writes=[WBb[i]] + wina_tokens, dsem=wbd[i])
        cdma("pool", WPA[:, :, :], wpa_d[0].rearrange("(c p) n -> p c n", p=128), [WPA.b, U_[0].b, U_[1].b, UT_[0].b, UT_[1].b])
        cdma("pool", WPG[:, :, :], wpg_d[0].rearrange("(c p) n -> p c n", p=128), [WPG.b, ROPE.b])
        yield

    def can_start_w(n, i, done):
        if n == "W":
            return done["A"] >= ntiles
        return can_start(n, i, done)

    rolling({"A": ("A", [(lambda t=t: stage_a(t)) for t in range(ntiles)]),
             "B1": ("B1", [(lambda t=t: stage_att(t)) for t in range(ntiles)]),
             "B2e": ("B2e", [(lambda t=t: stage_gla(t)) for t in range(0, ntiles, 2)]),
             "B2o": ("B2o", [(lambda t=t: stage_gla(t)) for t in range(1, ntiles, 2)]),
             "W": ("A", [prefetch_gen])}, can_start_w)

    act_rsqrt[0] = False
    if stop == "p1a":
        return early_exit()
    k.barrier()

    cur[0] = wina_off + 32768
    BIASB = sb([1, 2048], BF16, "biasB")
    GHN = sb([128, 1], F32, "ghn")
    M1_ = [sb([128, 512], F32, f"M1{i}") for i in range(2)]
    assert cur[0] <= rope_off
    cur[0] = u_off + 8192
    WOUT = sb([128, 8, D], BF16, "wout")
    cdma("pool", BIASB[0:1, :], bin_d[0:1, 2320:4368], [BIASB.b])
    cdma("pool", WOUT[:, :, :], wout_d[0].rearrange("(c p) n -> p c n", p=128), [WOUT.b])
    cdma("sp", GHN[:, :], ghn_d[0].rearrange("(p o) -> p o", o=1), [GHN.b], slow=True)
    k.op("dve", lambda e: e.tensor_scalar_mul(out=WPG[:, :, :], in0=WPG[:, :, :], scalar1=GHN[:, 0:1]),
         reads=[WPG.b, GHN.b], writes=[WPG.b])

    UB_ = [sb([128, D], BF16, f"UB{i}") for i in range(2)]
    UTB_ = [sb([128, 8, 128], BF16, f"UTB{i}") for i in range(2)]
    TA_ = [sb([128, 2048], BF16, f"TA{i}") for i in range(2)]
    M2_ = [sb([128, 512], F32, f"M2{i}") for i in range(2)]
    MIX_ = [sb([128, D], BF16, f"MIX{i}") for i in range(2)]
    MIXT_ = [sb([128, 8, 128], BF16, f"MIXT{i}") for i in range(2)]

    def stage_gates(t):
        P = tsz(t)
        U, UT, TA = UB_[t % 2], UTB_[t % 2], TA_[t % 2]
        norm_T(t, 0, U, UT[:, :, :P], UT.b)
        yield
        if t + 1 < NT:
            norm_pre(t + 1, UB_[(t + 1) % 2])
        for i in range(4):
            ps = proj_group(UT, P, WINB, 8, i * 512, 512, WBb[i], BIASB)
            k.op("act", lambda e, ps=ps, i=i: e.activation(out=TA[:P, i * 512:(i + 1) * 512], in_=ps.f[:P, :], func=AF.Tanh, scale=0.5),
                 reads=[ps.b], writes=[TA.b])
            yield

    def stage_merge(t):
        P = tsz(t)
        TA, MIX, MIXT = TA_[t % 2], MIX_[t % 2], MIXT_[t % 2]
        t0 = tok0(t)
        for hf in range(2):
            M1, M2 = M1_[hf], M2_[hf]
            pa = psn()
            for c in range(4):
                k.op("pe", lambda e, c=c, pa=pa, hf=hf: e.matmul(pa.f[:P, :], lhsT=YA[:, c, t0:t0 + P], rhs=WPA[:, c, hf * 512:(hf + 1) * 512], start=(c == 0), stop=(c == 3)),
                     reads=[YA.b, WPA.b], writes=[pa.b])
            pg = psn()
            for c in range(4):
                k.op("pe", lambda e, c=c, pg=pg, hf=hf: e.matmul(pg.f[:P, :], lhsT=YG[:, c, t0:t0 + P], rhs=WPG[:, c, hf * 512:(hf + 1) * 512], start=(c == 0), stop=(c == 3)),
                     reads=[YG.b, WPG.b], writes=[pg.b])
            yield
            k.op("dve", lambda e, pa=pa, hf=hf, M1=M1: e.scalar_tensor_tensor(out=M1[:P, :], in0=TA[:P, hf * 512:(hf + 1) * 512], scalar=1.0, in1=pa.f[:P, :], op0=ALU.add, op1=ALU.mult),
                 reads=[TA.b, pa.b], writes=[M1.b])
            k.op("dve", lambda e, pg=pg, hf=hf, M2=M2: e.scalar_tensor_tensor(out=M2[:P, :], in0=TA[:P, 1024 + hf * 512:1024 + (hf + 1) * 512], scalar=1.0, in1=pg.f[:P, :], op0=ALU.add, op1=ALU.mult),
                 reads=[TA.b, pg.b], writes=[M2.b])
            k.op("pool", lambda e, hf=hf, M1=M1, M2=M2: e.tensor_tensor(out=MIX[:P, hf * 512:(hf + 1) * 512], in0=M1[:P, :], in1=M2[:P, :], op=ALU.add),
                 reads=[M1.b, M2.b], writes=[MIX.b])
            yield
        yield
        psm, pmv = transposes(MIX, 8, P, [MIX.b])
        k.op("act", lambda e: e.copy(out=MIXT[:, :, :P], in_=pmv[:, :, :P]), reads=[psm.b], writes=[MIXT.b])
        yield
        for hf in range(2):
            po = psn()
            for c in range(8):
                k.op("pe", lambda e, c=c, po=po, hf=hf: e.matmul(po.f[:P, :], lhsT=MIXT[:, c, :P], rhs=WOUT[:, c, hf * 512:(hf + 1) * 512], start=(c == 0), stop=(c == 7)),
                     reads=[MIXT.b, WOUT.b], writes=[po.b])
            k.op("dve", lambda e, po=po, hf=hf: e.scalar_tensor_tensor(out=H[:P, t, hf * 512:(hf + 1) * 512], in0=po.f[:P, :], scalar=0.5,
                                                                       in1=H[:P, t, hf * 512:(hf + 1) * 512], op0=ALU.mult, op1=ALU.add),
                 reads=[po.b, Hb[t]], writes=[Hb[t]])
            yield

    norm_pre(0, UB_[0])
    interleave(("C", stage_gates(0)))
    for t in range(NT):
        interleave(("C", stage_gates(t + 1)) if t + 1 < NT else None, ("D", stage_merge(t)))

    if stop == "p1b":
        return early_exit()
    k.barrier()

    if debug == "hmid":
        dd = k.dsem("dbg")
        k.op("sp", lambda e: e.dma_start(out=dbg_d[0:NMETA, :], in_=H[0:NMETA, 0, :]), reads=[Hb[0]], dsem=dd)
        for t in range(1, NT):
            k.op("sp", lambda e, t=t: e.dma_start(out=dbg_d[tok0(t):tok0(t) + 128, :], in_=H[:, t, :]), reads=[Hb[t]], dsem=dd)
        k.barrier()

    cur[0] = persist_end
    U2T = sb([128, 8, L], BF16, "u2t")
    U2_ = [sb([128, D], BF16, f"U2{i}") for i in range(2)]
    FNORM = sb([128, D], F32, "fnorm")
    STF = sb([128, 8], F32, "stf")
    CWB = sb([128, NCH, 4], F32, "cwb")
    HALO = sb([128, NCH, 2], F32, "halo")
    IDF = sb([4, 4], F32, "idf")
    quarters = [(0, 6), (6, 6), (12, 5), (17, 5)]
    WA_ = [sb([128, 8, 768], BF16, f"WA{i}") for i in range(2)]
    WV_ = [sb([128, 8, 768], BF16, f"WV{i}") for i in range(2)]
    WD_ = [sb([128, 6, D], BF16, f"WD{i}") for i in range(2)]
    qds = [[k.dsem(f"q{i}{j}") for j in range(3)] for i in range(2)]
    ASB_ = [sb([128, 514], F32, f"ASB{i}") for i in range(2)]
    Y_ = [sb([128, 512], F32, f"Yc{i}") for i in range(2)]
    mt_off = cur[0]
    MT_ = [sb([128, 6, 512], BF16, f"MT{i}") for i in range(2)]
    out_off = cur[0]
    OUT = sb([128, D], F32, "OUT")
    U2x = [R(big[0:128, out_off // 2 + j * D: out_off // 2 + (j + 1) * D], k.buf(f"u2x{j}")) for j in range(2)]
    U2s = U2_ + U2x
    STP = [sb([128, 8], F32, f"stp{j}") for j in range(4)]
    cw_ap = big[0:4, mt_off // 2: mt_off // 2 + D_FF * 2].bitcast(F32)
    assert D_FF * 4 <= 2 * 6 * 512 * 2
    MTB = [MT_[0].b, MT_[1].b]

    wup_v = wup_d[0].rearrange("(kc p) c -> p kc c", p=128)
    wdn_v = wdn_d[0].rearrange("(c p) n -> p c n", p=128)

    def load_quarter(qi):
        c0, nq = quarters[qi]
        sl = qi % 2
        k.op("pool", lambda e: e.dma_start(out=WA_[sl][:, :, 0:nq * 128], in_=wup_v[:, :, c0 * 128:(c0 + nq) * 128]), writes=[WA_[sl].b], dsem=qds[sl][0])
        k.op("pool", lambda e: e.dma_start(out=WV_[sl][:, :, 0:nq * 128], in_=wup_v[:, :, D_FF + c0 * 128:D_FF + (c0 + nq) * 128]), writes=[WV_[sl].b], dsem=qds[sl][1])
        k.op("pool", lambda e: e.dma_start(out=WD_[sl][:, 0:nq, :], in_=wdn_v[:, c0:c0 + nq, :]), writes=[WD_[sl].b], dsem=qds[sl][2])

    load_quarter(0)
    load_quarter(1)
    cdma("sp", FNORM[:, :], fnorm_d.rearrange("(o d) -> o d", o=1).partition_broadcast(128), [FNORM.b])
    cdma("sp", cw_ap[0:3, :], convw_d[0], MTB)
    cdma("sp", cw_ap[3:4, :], convb_d[0:1, :], MTB)
    cdma("sp", IDF[:, :], ident_d[0:4, 0:4], [IDF.b])
    k.op("dve", lambda e: e.memset(HALO[:, :, :], 0.0), writes=[HALO.b])
    psc = psn()
    cwv = psc.f[:, 0:NCH * 4].rearrange("p (c f) -> p c f", f=4)
    for c in range(NCH):
        k.op("pe", lambda e, c=c: e.transpose(out=cwv[:, c, :], in_=cw_ap[0:4, c * 128:(c + 1) * 128], identity=IDF[0:4, 0:4]),
             reads=MTB + [IDF.b], writes=[psc.b])
    k.op("dve", lambda e: e.tensor_copy(out=CWB[:, :, :], in_=cwv), reads=[psc.b], writes=[CWB.b])

    act_rsqrt[0] = True
    for j in range(min(3, NT)):
        norm_pre(j, U2s[j % 4], ST=STP[j % 4])
    for t in range(NT):
        if t + 3 < NT:
            norm_pre(t + 3, U2s[(t + 3) % 4], ST=STP[(t + 3) % 4])
        norm_T(t, 1, U2s[t % 4], U2T[:, :, tok0(t):tok0(t) + tsz(t)], U2T.b)
    act_rsqrt[0] = False
    groups = [(0, NMETA, [0])] + [(NMETA + 512 * g, 512, [1 + 4 * g + j for j in range(4)]) for g in range(4)]
    ods = [k.dsem(f"o{i}") for i in range(4)]
    it = [0]
    gcnt = [0]

    def ffn_up(qi, gi):
        c0, nq = quarters[qi]
        sl = qi % 2
        WA, WV = WA_[sl], WV_[sl]
        g0, N, tiles = groups[gi]
        MT = MT_[gi % 2]
        for ci in range(nq):
            c = c0 + ci
            j = it[0] % 2
            it[0] += 1
            ASB, Y = ASB_[j], Y_[j]
            psA = psn()
            for kc in range(8):
                k.op("pe", lambda e, kc=kc, psA=psA, ci=ci: e.matmul(psA.f[:, :N], lhsT=WA[:, kc, ci * 128:(ci + 1) * 128], rhs=U2T[:, kc, g0:g0 + N], start=(kc == 0), stop=(kc == 7)),
                     reads=[WA.b, U2T.b], writes=[psA.b])
            if gi == 0:
                k.op("act", lambda e, psA=psA, c=c: e.copy(out=HALO[:, c, :], in_=psA.f[:, N - 2:N]), reads=[psA.b], writes=[HALO.b])
                yield
                continue
            psV = psn()
            for kc in range(8):
                k.op("pe", lambda e, kc=kc, psV=psV, ci=ci: e.matmul(psV.f[:, :N], lhsT=WV[:, kc, ci * 128:(ci + 1) * 128], rhs=U2T[:, kc, g0:g0 + N], start=(kc == 0), stop=(kc == 7)),
                     reads=[WV.b, U2T.b], writes=[psV.b])
            k.op("act", lambda e, psA=psA, ASB=ASB: e.copy(out=ASB[:, 2:2 + N], in_=psA.f[:, :N]), reads=[psA.b], writes=[ASB.b])
            k.op("act", lambda e, ASB=ASB, c=c: e.copy(out=ASB[:, 0:2], in_=HALO[:, c, :]), reads=[HALO.b, ASB.b], writes=[ASB.b])
            k.op("act", lambda e, ASB=ASB, c=c: e.copy(out=HALO[:, c, :], in_=ASB[:, N:N + 2]), reads=[ASB.b, HALO.b], writes=[HALO.b])
            yield
            k.op("dve", lambda e, ASB=ASB, Y=Y, c=c: e.tensor_scalar(out=Y[:, :N], in0=ASB[:, 2:2 + N], scalar1=CWB[:, c, 2:3], scalar2=CWB[:, c, 3:4], op0=ALU.mult, op1=ALU.add),
                 reads=[ASB.b, CWB.b], writes=[Y.b])
            k.op("dve", lambda e, ASB=ASB, Y=Y, c=c: e.scalar_tensor_tensor(out=Y[:, :N], in0=ASB[:, 1:1 + N], scalar=CWB[:, c, 1:2], in1=Y[:, :N], op0=ALU.mult, op1=ALU.add),
                 reads=[ASB.b, CWB.b, Y.b], writes=[Y.b])
            k.op("dve", lambda e, ASB=ASB, Y=Y, c=c: e.scalar_tensor_tensor(out=Y[:, :N], in0=ASB[:, 0:N], scalar=CWB[:, c, 0:1], in1=Y[:, :N], op0=ALU.mult, op1=ALU.add),
                 reads=[ASB.b, CWB.b, Y.b], writes=[Y.b])
            k.op("act", lambda e, Y=Y: e.activation(out=Y[:, :N], in_=Y[:, :N], func=AF.Gelu_apprx_tanh), reads=[Y.b], writes=[Y.b])
            k.op("dve", lambda e, Y=Y, psV=psV, ci=ci: e.tensor_tensor(out=MT[:, ci, :N], in0=Y[:, :N], in1=psV.f[:, :N], op=ALU.mult),
                 reads=[Y.b, psV.b], writes=[MT.b])
            yield

    pend = []

    def final_norm(t):
        jk = U2_[t % 2]
        ss = STF[:, 0:1]
        rs = STF[:, 1:2]
        k.op("act", lambda e: e.activation(out=jk[:, :], in_=H[:, t, :], func=AF.Square, accum_out=ss),
             reads=[Hb[t]], writes=[jk.b, STF.b])
        yield
        k.op("dve", lambda e: e.tensor_scalar(out=rs, in0=ss, scalar1=1.0 / D, scalar2=EPS, op0=ALU.mult, op1=ALU.add),
             reads=[STF.b], writes=[STF.b])
        yield
        k.op("pool", lambda e: e.tensor_tensor(out=rs, in0=rs, in1=NEGH[:, 0:1], op=ALU.pow), reads=[STF.b, NEGH.b], writes=[STF.b])
        yield
        k.op("dve", lambda e: e.scalar_tensor_tensor(out=OUT[:, :], in0=H[:, t, :], scalar=rs, in1=FNORM[:, :], op0=ALU.mult, op1=ALU.mult),
             reads=[Hb[t], STF.b, FNORM.b], writes=[OUT.b, U2x[0].b, U2x[1].b])
        yield
        k.op("sp", lambda e: e.dma_start(out=y_d[(t - 1) * 128:t * 128, :], in_=OUT[:, :]), reads=[OUT.b], dsem=ods[t % 4])
        yield

    def advance_pending():
        for g in list(pend[:1]):
            try:
                next(g)
            except StopIteration:
                pend.remove(g)

    def ffn_down(qi, gi):
        c0, nq = quarters[qi]
        sl = qi % 2
        WD = WD_[sl]
        g0, N, tiles = groups[gi]
        MT = MT_[gi % 2]
        last_q = qi == len(quarters) - 1
        yield
        yield
        for jt, t in enumerate(tiles):
            for hf in range(2):
                pd = psn()
                for ci in range(nq):
                    k.op("pe", lambda e, ci=ci, pd=pd, jt=jt, hf=hf: e.matmul(pd.f[:, :], lhsT=MT[:, ci, jt * 128:(jt + 1) * 128], rhs=WD[:, ci, hf * 512:(hf + 1) * 512], start=(ci == 0), stop=(ci == nq - 1)),
                         reads=[MT.b, WD.b], writes=[pd.b])
                k.op("dve", lambda e, pd=pd, t=t, hf=hf: e.tensor_tensor(out=H[:, t, hf * 512:(hf + 1) * 512], in0=pd.f[:, :], in1=H[:, t, hf * 512:(hf + 1) * 512], op=ALU.add),
                     reads=[pd.b, Hb[t]], writes=[Hb[t]])
                if last_q:
                    advance_pending()
                yield
                if last_q:
                    advance_pending()
            if last_q:
                pend.append(final_norm(t))
        if last_q and gi == len(groups) - 1:
            while pend:
                advance_pending()
                yield

    seq = [(qi, gi) for qi in range(len(quarters)) for gi in range(1, len(groups))]
    prev = None
    for (qi, gi) in seq:
        if gi == 1:
            interleave(("C", ffn_up(qi, 0)))
        interleave(("C", ffn_up(qi, gi)), ("D", ffn_down(*prev)) if prev is not None else None)
        if prev is not None and prev[1] == len(groups) - 1 and prev[0] + 2 < len(quarters):
            load_quarter(prev[0] + 2)
        prev = (qi, gi)
    interleave(("D", ffn_down(*prev)))

    finals = list(ods)
    if debug:
        finals.append(dd)
    k.generate(final_waits=finals)
    return nc


def _consts():
    ident = np.eye(128, dtype=np.float32)
    kk = np.arange(128)[:, None]
    qq = np.arange(128)[None, :]
    m_cur = (kk <= qq).astype(np.float32)
    m_prev = (kk > qq).astype(np.float32)
    m_p01 = (kk > qq - 112).astype(np.float32)
    masks = np.stack([m_cur, m_prev, m_p01]).astype(np.float32)
    utri = (kk <= qq).astype(np.float32) * np.float32(-1.0 / 16.0)
    half = 32
    inv_freq = (10000.0 ** (-np.arange(half, dtype=np.float32) / half)).astype(np.float32)
    rope = np.zeros((128, NT, 128), np.float32)
    for t in range(NT):
        P = tsz(t)
        pos = (tok0(t) + np.arange(P)).astype(np.float32)
        ang = pos[:, None] * inv_freq[None, :]
        c = np.cos(ang).astype(np.float32)
        s = np.sin(ang).astype(np.float32)
        rope[:P, t, 0:32] = c
        rope[:P, t, 32:64] = c
        rope[:P, t, 64:96] = s
        rope[:P, t, 96:128] = -s
    return dict(c_ident=ident, c_masks=masks, c_utri=utri, c_rope=rope)


_CACHE = {}


def kernel(**inputs):
    debug = inputs.pop("_debug", None)
    key = debug
    if key not in _CACHE:
        _CACHE[key] = build(debug)
    nc = _CACHE[key]
    consts = _consts()
    x = np.ascontiguousarray(inputs["x"], dtype=np.float32)
    B = x.shape[0]
    shared = {n: np.ascontiguousarray(np.asarray(v, dtype=np.float32)) for n, v in inputs.items() if n != "x"}
    shared.update(consts)
    in_maps = []
    for b in range(B):
        m = dict(shared)
        m["x"] = x[b]
        in_maps.append(m)
    res = run_bass_kernel_spmd(nc, in_maps, core_ids=list(range(B)))
    out = np.stack([np.asarray(r["y"], dtype=np.float32) for r in res.results], axis=0)
    if debug:
        return out, np.stack([np.asarray(r["dbg"]) for r in res.results], axis=0)
    return out
```

```python
import contextlib
import numpy as np
import concourse.bass as bass
import concourse.mybir as mybir
from concourse.bass_utils import run_bass_kernel_spmd

F32 = mybir.dt.float32
BF16 = mybir.dt.bfloat16
AF = mybir.ActivationFunctionType
ALU = mybir.AluOpType
AX = mybir.AxisListType

D = 1024
SEQ = 2048
NMETA = 16
L = SEQ + NMETA
NT = 17
EPS = 1e-6
D_IN = 4368
D_FF = 2816
NCH = D_FF // 128

ENGS = ("pe", "act", "dve", "pool", "sp")
SAME_ENG_DIST = 3


class Buf:
    __slots__ = ("name", "writer", "readers", "psum")

    def __init__(self, name):
        self.name = name
        self.writer = None
        self.readers = []
        self.psum = False


class DmaSem:
    __slots__ = ("name", "count", "last", "handle")

    def __init__(self, name):
        self.name = name
        self.count = 0
        self.last = None
        self.handle = None


class Op:
    __slots__ = ("eng", "fn", "deps", "dsem", "dval", "signal", "idx", "sigval", "name")

    def __init__(self, eng, fn, name):
        self.eng = eng
        self.fn = fn
        self.deps = []
        self.dsem = None
        self.dval = 0
        self.signal = False
        self.idx = 0
        self.sigval = 0
        self.name = name


class K:
    def __init__(self, nc):
        self.nc = nc
        self.ops = {e: [] for e in ENGS}
        self.dsems = []
        self.nbuf = 0

    def buf(self, name=None):
        self.nbuf += 1
        return Buf(name or f"b{self.nbuf}")

    def dsem(self, name=None):
        d = DmaSem(name or f"d{len(self.dsems)}")
        self.dsems.append(d)
        return d

    limit = None
    nops = 0

    def op(self, eng, fn, reads=(), writes=(), dsem=None, name="", extra=(), force=False):
        self.nops += 1
        if self.limit is not None and self.nops > self.limit and not force:
            return None
        o = Op(eng, fn, name)
        lst = self.ops[eng]
        o.idx = len(lst)
        deps = list(extra)
        for b in list(reads) + list(writes):
            if b.writer is not None:
                deps.append(b.writer)
        for b in writes:
            deps.extend(b.readers)
        for b in reads:
            if b.psum:
                deps.extend(r for r in b.readers if r.eng != eng)
        if dsem is not None:
            o.dsem = dsem
            dsem.count += 16
            o.dval = dsem.count
            if dsem.last is not None:
                deps.append(dsem.last)
            dsem.last = o
        seen = set()
        for d in deps:
            if id(d) in seen or d is o:
                continue
            seen.add(id(d))
            if d.dsem is None and d.eng == eng:
                if eng == "pe":
                    continue
                if o.idx - d.idx > SAME_ENG_DIST:
                    continue
            o.deps.append(d)
            if d.dsem is None:
                d.signal = True
        for b in reads:
            b.readers.append(o)
        for b in writes:
            b.writer = o
            b.readers = []
        lst.append(o)
        return o

    def barrier(self):
        lasts = []
        for e in ENGS:
            for o in reversed(self.ops[e]):
                if o.dsem is None and o.fn is not None:
                    lasts.append(o)
                    break
        dl = [d.last for d in self.dsems if d.last is not None]
        for e in ENGS:
            o = Op(e, None, "barrier")
            o.idx = len(self.ops[e])
            for d in lasts:
                if d.eng != e:
                    o.deps.append(d)
                    d.signal = True
            o.deps.extend(dl)
            self.ops[e].append(o)

    def generate(self, final_waits=()):
        nc = self.nc
        with contextlib.ExitStack() as st:
            esem = {e: st.enter_context(nc.semaphore(f"s_{e}")) for e in ENGS}
            for d in self.dsems:
                d.handle = st.enter_context(nc.semaphore(f"dm_{d.name}"))
            for e in ENGS:
                c = 0
                for o in self.ops[e]:
                    if o.dsem is None and o.signal:
                        c += 1
                        o.sigval = c
            block = st.enter_context(nc.Block())

            def gen(ename, eng):
                seen = {}
                for o in self.ops[ename]:
                    need = {}
                    for d in o.deps:
                        if d.dsem is not None:
                            key, val, h = ("d", id(d.dsem)), d.dval, d.dsem.handle
                        else:
                            key, val, h = ("e", d.eng), d.sigval, esem[d.eng]
                        if val > need.get(key, (0, None))[0]:
                            need[key] = (val, h)
                    for key, (val, h) in need.items():
                        if seen.get(key, 0) >= val:
                            continue
                        seen[key] = val
                        eng.wait_ge(h, val)
                    if o.fn is None:
                        continue
                    ins = o.fn(eng)
                    if o.dsem is not None:
                        ins.then_inc(o.dsem.handle, 16)
                    elif o.signal:
                        ins.then_inc(esem[ename], 1)
                if ename == "sp":
                    for d in final_waits:
                        eng.wait_ge(d.handle, d.count)

            @block.tensor
            def _(e):
                gen("pe", e)

            @block.scalar
            def _(e):
                gen("act", e)

            @block.vector
            def _(e):
                gen("dve", e)

            @block.gpsimd
            def _(e):
                gen("pool", e)

            @block.sync
            def _(e):
                gen("sp", e)


class R:
    __slots__ = ("ap", "b")

    def __init__(self, ap, b):
        self.ap = ap
        self.b = b

    def __getitem__(self, idx):
        return self.ap[idx]


def tok0(t):
    return 0 if t == 0 else NMETA + 128 * (t - 1)


def tsz(t):
    return NMETA if t == 0 else 128


def build(debug=None, stop=None, ntiles=NT, limit=None):
    nc = bass.Bass("TRN2", target_bir_lowering=False)
    k = K(nc)
    k.limit = limit

    def din(name, shape):
        return nc.dram_tensor(name, list(shape), F32, kind="ExternalInput").ap()

    x_d = din("x", [SEQ, D])
    meta_d = din("meta_tokens", [NMETA, D])
    mixn_d = din("mix_norm", [1, D])
    win_d = din("w_in", [1, D, D_IN])
    bin_d = din("b_in", [1, D_IN])
    walpha_d = din("w_alpha", [1, 16, 256])
    balpha_d = din("b_alpha", [1, 256])
    sinks_d = din("attn_sinks", [1, 8])
    ghn_d = din("gla_head_norm", [1, 128])
    wpa_d = din("w_proj_attn", [1, 512, D])
    wpg_d = din("w_proj_gla", [1, 512, D])
    wout_d = din("w_out", [1, D, D])
    ffnn_d = din("ffn_norm", [1, D])
    wup_d = din("w_up", [1, D, 2 * D_FF])
    convw_d = din("conv_w", [1, 3, D_FF])
    convb_d = din("conv_b", [1, D_FF])
    wdn_d = din("w_down", [1, D_FF, D])
    fnorm_d = din("final_norm", [D])
    ident_d = din("c_ident", [128, 128])
    masks_d = din("c_masks", [3, 128, 128])
    utri_d = din("c_utri", [128, 128])
    rope_d = din("c_rope", [128, NT, 128])
    y_d = nc.dram_tensor("y", [SEQ, D], F32, kind="ExternalOutput").ap()
    dbg_d = None
    if debug:
        dbg_d = nc.dram_tensor("dbg", [L, D], F32, kind="ExternalOutput").ap()

    SB_BYTES = 212736
    big = nc.alloc_sbuf_tensor("sbig", [128, SB_BYTES // 2], BF16)
    cur = [0]

    def sb(shape, dt, name=None):
        n = 1
        for s in shape[1:]:
            n *= s
        isz = 4 if dt == F32 else 2
        nbytes = (n * isz + 31) // 32 * 32
        off = cur[0]
        cur[0] += nbytes
        assert cur[0] <= SB_BYTES, f"SBUF overflow at {name}: {cur[0]}"
        ap = big[0:shape[0], off // 2: off // 2 + n * isz // 2]
        if dt == F32:
            ap = ap.bitcast(F32)
        if len(shape) == 3:
            ap = ap.rearrange("p (a b) -> p a b", a=shape[1])
        elif len(shape) == 4:
            ap = ap.rearrange("p (a b c) -> p a b c", a=shape[1], b=shape[2])
        return R(ap, k.buf(name))

    banks = []
    for i in range(8):
        t = nc.alloc_psum_tensor(f"ps{i}", [128, 512], F32)
        banks.append((t, k.buf(f"ps{i}")))
        banks[-1][1].psum = True
    pools = {"all": [list(range(8)), 0], "A": [[0, 1], 0], "B1": [[2, 3], 0], "B2e": [[4, 5], 0], "B2o": [[6, 7], 0], "C": [[0, 1, 2, 3], 0], "D": [[4, 5, 6, 7], 0]}
    pool_cur = ["all"]

    class PS:
        __slots__ = ("f", "h", "b")

    def psn(pool=None):
        pl = pools[pool or pool_cur[0]]
        t, b = banks[pl[0][pl[1] % len(pl[0])]]
        pl[1] += 1
        p = PS()
        p.f = t[:, :]
        p.h = t[:, :].bitcast(BF16)
        p.b = b
        return p

    def rolling(streams, can_start):
        idx = {n: 0 for n in streams}
        cur_g = {n: None for n in streams}
        done = {n: 0 for n in streams}
        while True:
            progressed = False
            alive = False
            for n, (pool, facs) in streams.items():
                if cur_g[n] is None:
                    if idx[n] >= len(facs):
                        continue
                    alive = True
                    if not can_start(n, idx[n], done):
                        continue
                    cur_g[n] = facs[idx[n]]()
                    idx[n] += 1
                alive = True
                pool_cur[0] = pool
                try:
                    next(cur_g[n])
                except StopIteration:
                    cur_g[n] = None
                    done[n] += 1
                progressed = True
            if not alive:
                break
            assert progressed, "rolling scheduler deadlock"
        pool_cur[0] = "all"

    def interleave(*gens):
        gens = [g for g in gens if g is not None]
        while gens:
            for g in list(gens):
                try:
                    pool_cur[0] = g[0]
                    next(g[1])
                except StopIteration:
                    gens.remove(g)
        pool_cur[0] = "all"

    H = sb([128, NT, D], F32, "H")
    Hb = [k.buf(f"H{t}") for t in range(NT)]
    IDENT = sb([128, 128], BF16, "ident")
    MASKS = sb([128, 3, 128], BF16, "masks")
    UTRI = sb([128, 128], F32, "utri")
    ONES = sb([128, 128], BF16, "ones")
    NEGH = sb([128, 8], F32, "negh")
    GAINT = sb([128, 2, 8], F32, "gainT")
    ESINK = sb([128, 8], F32, "esink")
    STAT = sb([128, 8], F32, "stat")
    DENS = sb([128, 8], F32, "dens")
    S4 = sb([128, 8], F32, "s4")
    persist_end = cur[0]

    dq = {"n": 0}
    def cdma(eng, out_ap, in_ap, wr, slow=False):
        d = k.dsem(f"c{dq['n']}")
        dq["n"] += 1
        if slow:
            return k.op(eng, lambda e: e.dma_start(out=out_ap, in_=in_ap, allow_slow_non_contiguous=True), writes=wr, dsem=d)
        return k.op(eng, lambda e: e.dma_start(out=out_ap, in_=in_ap), writes=wr, dsem=d)

    cdma("pool", IDENT[:, :], ident_d, [IDENT.b])
    cdma("pool", MASKS[:, :, :], masks_d.rearrange("m p q -> p m q"), [MASKS.b])
    cdma("sp", UTRI[:, :], utri_d, [UTRI.b])
    cdma("sp", GAINT[:, 0, :], mixn_d[0].rearrange("(c p) -> p c", p=128), [GAINT.b], slow=True)
    GAINT2 = k.buf("gaint2")
    cdma("sp", GAINT[:, 1, :], ffnn_d[0].rearrange("(c p) -> p c", p=128), [GAINT2], slow=True)
    cdma("sp", ESINK[:, :], sinks_d.partition_broadcast(128), [ESINK.b])
    k.op("dve", lambda e: e.memset(ONES[:, :], 1.0), writes=[ONES.b])
    k.op("dve", lambda e: e.memset(NEGH[:, :], -0.5), writes=[NEGH.b])
    k.op("act", lambda e: e.activation(out=ESINK[:, :], in_=ESINK[:, :], func=AF.Exp), reads=[ESINK.b], writes=[ESINK.b])

    xds = [k.dsem(f"x{i}") for i in range(4)]
    k.op("sp", lambda e: e.dma_start(out=H[0:NMETA, 0, :], in_=meta_d), writes=[Hb[0]], dsem=xds[0])
    def load_x(t, extra=()):
        k.op("sp", lambda e: e.dma_start(out=H[:, t, :], in_=x_d[(t - 1) * 128: t * 128, :]),
             writes=[Hb[t]], dsem=xds[t % 4], extra=extra)

    for t in range(1, 3):
        load_x(t)

    def early_exit():
        k.barrier()
        dd_ = k.dsem("early")
        for t in range(1, NT):
            k.op("sp", lambda e, t=t: e.dma_start(out=y_d[(t - 1) * 128:t * 128, :], in_=H[:, t, :]), reads=[Hb[t]], dsem=dd_, force=True)
        k.generate(final_waits=[dd_])
        return nc

    if stop == "setup":
        return early_exit()

    def stat(col, n=1):
        return STAT[:, col:col + n]

    act_rsqrt = [False]

    def rms_rstd(src_ap, P, junk_ap, col, rd, wr, nfeat, STAT=STAT):
        ss = STAT[:P, col:col + 1]
        rs = STAT[:P, col + 1:col + 2]
        k.op("act", lambda e: e.activation(out=junk_ap, in_=src_ap, func=AF.Square, accum_out=ss),
             reads=rd, writes=wr + [STAT.b])
        if act_rsqrt[0]:
            k.op("act", lambda e: e.activation(out=rs, in_=ss, func=AF.Ln, scale=1.0 / nfeat, bias=EPS), reads=[STAT.b], writes=[STAT.b])
            k.op("act", lambda e: e.activation(out=rs, in_=rs, func=AF.Exp, scale=-0.5), reads=[STAT.b], writes=[STAT.b])
            return rs
        k.op("dve", lambda e: e.tensor_scalar(out=rs, in0=ss, scalar1=1.0 / nfeat, scalar2=EPS, op0=ALU.mult, op1=ALU.add),
             reads=[STAT.b], writes=[STAT.b])
        k.op("pool", lambda e: e.tensor_tensor(out=rs, in0=rs, in1=NEGH[:P, 0:1], op=ALU.pow),
             reads=[STAT.b, NEGH.b], writes=[STAT.b])
        return rs

    def norm_pre(t, U, ST=STAT):
        P = tsz(t)
        rs = rms_rstd(H[:P, t, :], P, U[:P, :], 0, [Hb[t]], [U.b], D, STAT=ST)
        k.op("dve", lambda e: e.tensor_scalar_mul(out=U[:P, :], in0=H[:P, t, :], scalar1=rs),
             reads=[Hb[t], ST.b], writes=[U.b])

    def norm_T(t, gi, U, dstT_ap, dst_b):
        P = tsz(t)
        ps = psn()
        pv = ps.h.rearrange("p (c q) -> p c q", c=8)
        for c in range(8):
            k.op("pe", lambda e, c=c: e.transpose(out=pv[:, c, :P], in_=U[:P, c * 128:(c + 1) * 128], identity=IDENT[:P, :P]),
                 reads=[U.b, IDENT.b], writes=[ps.b])
        k.op("dve", lambda e: e.tensor_tensor(out=dstT_ap, in0=pv[:, :, :P],
                                              in1=GAINT[:, gi, :].unsqueeze(2).broadcast_to([128, 8, P]), op=ALU.mult),
             reads=[ps.b, GAINT.b, GAINT2], writes=[dst_b])

    def transposes(src, n, P, rd):
        ps = psn()
        pv = ps.h.rearrange("p (c q) -> p c q", c=8)
        for c in range(n):
            k.op("pe", lambda e, c=c: e.transpose(out=pv[:, c, :P], in_=src[:P, c * 128:(c + 1) * 128], identity=IDENT[:P, :P]),
                 reads=rd + [IDENT.b], writes=[ps.b])
        return ps, pv

    YA = sb([128, 4, L], BF16, "YA")
    YG = sb([128, 4, L], BF16, "YG")
    p1_end = cur[0]

    NA = 2320
    wina_off = cur[0]
    WINA = sb([128, 8, NA], BF16, "winA")
    BIASA = sb([1, NA], BF16, "biasA")
    WALPHA = sb([32, 256], BF16, "walpha")
    rope_off = cur[0]
    ROPE = sb([128, NT, 128], F32, "rope")
    SST = sb([128, 2, 128], F32, "S")
    SBF_ = [sb([128, 2, 128], BF16, f"Sbf{i}") for i in range(3)]
    k.op("dve", lambda e: e.memset(SST[:, :, :], 0.0), writes=[SST.b])
    k.op("pool", lambda e: e.memset(SBF_[0][:, :, :], 0.0), writes=[SBF_[0].b])

    u_off = cur[0]
    U_ = [sb([128, D], BF16, f"U{i}") for i in range(2)]
    UT_ = [sb([128, 8, 128], BF16, f"UT{i}") for i in range(2)]
    T1 = sb([128, 640], F32, "T1")
    T2 = sb([128, 640], F32, "T2")
    TH = R(T1.ap[:, 0:512], T1.b)
    QR = sb([128, 512], BF16, "QR")
    KR = sb([128, 128], BF16, "KR")
    GL = sb([128, 16], BF16, "GL")
    QT_ = [sb([128, 4, 128], BF16, f"QT{i}") for i in range(3)]
    KT_ = [sb([128, 128], BF16, f"KT{i}") for i in range(4)]
    VA_ = [sb([128, 2, 66], BF16, f"VA{i}") for i in range(4)]
    GLT_ = [sb([32, 128], BF16, f"GLT{i}") for i in range(3)]
    GQK_ = [sb([128, 512], BF16, f"GQK{i}") for i in range(3)]
    V_ = [sb([128, 512], BF16, f"V{i}") for i in range(3)]
    SG_ = [sb([128, 512], BF16, f"SG{i}") for i in range(3)]
    PT_ = [[sb([128, 4, 128], BF16, f"PT{g}{b}") for b in range(2)] for g in range(2)]
    YATT = sb([128, 512], BF16, "YATT")
    OA = sb([128, 8, 66], F32, "OA")
    SPt_ = [sb([128, 256], F32, f"SP{i}") for i in range(2)]
    ENB_ = [sb([128, 256], F32, f"ENB{i}") for i in range(2)]
    EBT_ = [sb([128, 2, 128], F32, f"EBT{i}") for i in range(2)]
    ENBT_ = [sb([128, 2, 128], F32, f"ENBT{i}") for i in range(2)]
    QET_ = [sb([128, 2, 128], BF16, f"QET{i}") for i in range(2)]
    KET_ = [sb([128, 2, 128], BF16, f"KET{i}") for i in range(2)]
    KE_ = [sb([128, 256], BF16, f"KE{i}") for i in range(2)]
    AT_ = [sb([128, 4, 128], BF16, f"AT{i}") for i in range(2)]
    OSQ_ = [sb([128, 512], F32, f"OSQ{i}") for i in range(1)] * 2
    YGT_ = [sb([128, 512], BF16, f"YGT{i}") for i in range(1)] * 2
    S4_ = [sb([128, 8], F32, f"S4{i}") for i in range(1)] * 2
    sdone = [False] * NT
    for i in range(4):
        k.op("pool", lambda e, i=i: e.memset(VA_[i][:, :, :], 1.0), writes=[VA_[i].b])
    for i in range(3):
        k.op("pool", lambda e, i=i: e.memset(GLT_[i][:, :], 1.0), writes=[GLT_[i].b])

    segs = [(0, 0, 512), (512, 512, 256), (768, 2304, 16), (784, 768, 512), (1296, 1280, 512), (1808, 1792, 512)]
    WAb = [k.buf(f"winA{i}") for i in range(5)]
    WAb1b = k.buf("winA1b")
    seg_buf = [WAb[0], WAb[1], WAb1b, WAb[2], WAb[3], WAb[4]]
    wds = [k.dsem(f"w{i}") for i in range(6)]
    BIASAb = [k.buf(f"biasA{i}") for i in range(6)]
    bias_tok = {0: [BIASAb[0]], 512: [BIASAb[1], BIASAb[2]], 784: [BIASAb[3]], 1296: [BIASAb[4]], 1808: [BIASAb[5]]}
    win_v = win_d[0].rearrange("(kc p) c -> p kc c", p=128)
    seg_ops = []

    def load_seg(i):
        d0, s0, n = segs[i]
        seg_ops.append(k.op("pool", lambda e: e.dma_start(out=WINA[:, :, d0:d0 + n], in_=win_v[:, :, s0:s0 + n]),
                            writes=[seg_buf[i]], dsem=wds[i]))
        cdma("pool", BIASA[0:1, d0:d0 + n], bin_d[0:1, s0:s0 + n], [BIASAb[i]])

    load_seg(0)
    WALPHA2 = k.buf("walpha_b")
    cdma("pool", WALPHA[0:16, :], walpha_d[0], [WALPHA.b])
    cdma("pool", WALPHA[16:17, :], balpha_d[0:1, :], [WALPHA2])
    cdma("sp", ROPE[:, :, :], rope_d, [ROPE.b])

    def proj_group(UT, P, W, kc_n, c0, n, wb, bias):
        ps = psn()
        for kc in range(kc_n):
            k.op("pe", lambda e, kc=kc: e.matmul(ps.f[:P, :n], lhsT=UT[:, kc, :P], rhs=W[:, kc, c0:c0 + n],
                                               start=(kc == 0), stop=(bias is None and kc == kc_n - 1)),
                 reads=[UT.b] + (wb if isinstance(wb, list) else [wb]), writes=[ps.b])
        if bias is not None:
            btok = bias_tok[c0] if bias is BIASA else [bias.b]
            k.op("pe", lambda e: e.matmul(ps.f[:P, :n], lhsT=ONES[0:1, :P], rhs=bias[0:1, c0:c0 + n], start=False, stop=True),
                 reads=[ONES.b] + btok, writes=[ps.b])
        return ps

    def stage_a(t):
        P = tsz(t)
        U, UT = U_[t % 2], UT_[t % 2]
        QT, KT, VA, GLT, GQK, V, SG = QT_[t % 3], KT_[t % 4], VA_[t % 4], GLT_[t % 3], GQK_[t % 3], V_[t % 3], SG_[t % 3]
        if t + 1 < ntiles:
            norm_pre(t + 1, U_[(t + 1) % 2])
        ps = proj_group(UT, P, WINA, 8, 0, 512, WAb[0], BIASA)
        yield
        qin = ps.f[:P, 0:512].rearrange("p (g i d) -> p i g d", g=2, i=4)
        qin5 = ps.f[:P, 0:512].rearrange("p (g i w d) -> p i g w d", g=2, i=4, w=2)
        t1v = T1[:P, 0:512].rearrange("p (i g d) -> p i g d", i=4, g=2)
        t2v = T2[:P, 0:512].rearrange("p (i g w d) -> p i g w d", i=4, g=2, w=2)
        cosf = ROPE[:P, t, 0:64]
        sin = ROPE[:P, t, 64:96]
        nsin = ROPE[:P, t, 96:128]
        k.op("dve", lambda e: e.tensor_tensor(out=t1v, in0=qin, in1=cosf.unsqueeze(1).unsqueeze(1).broadcast_to([P, 4, 2, 64]), op=ALU.mult),
             reads=[ps.b, ROPE.b], writes=[T1.b])
        k.op("dve", lambda e: e.tensor_tensor(out=t2v[:, :, :, 0, :], in0=qin5[:, :, :, 1, :],
                                              in1=nsin.unsqueeze(1).unsqueeze(1).broadcast_to([P, 4, 2, 32]), op=ALU.mult),
             reads=[ps.b, ROPE.b], writes=[T2.b])
        k.op("dve", lambda e: e.tensor_tensor(out=t2v[:, :, :, 1, :], in0=qin5[:, :, :, 0, :],
                                              in1=sin.unsqueeze(1).unsqueeze(1).broadcast_to([P, 4, 2, 32]), op=ALU.mult),
             reads=[ps.b, ROPE.b], writes=[T2.b])
        ps1 = proj_group(UT, P, WINA, 8, 512, 272, [WAb[1], WAb1b], BIASA)
        yield
        kin = ps1.f[:P, 0:128].rearrange("p (g d) -> p g d", g=2)
        kin4 = ps1.f[:P, 0:128].rearrange("p (g w d) -> p g w d", g=2, w=2)
        t1k = T1[:P, 512:640].rearrange("p (g d) -> p g d", g=2)
        t2k = T2[:P, 512:640].rearrange("p (g w d) -> p g w d", g=2, w=2)
        k.op("dve", lambda e: e.tensor_tensor(out=t1k, in0=kin, in1=cosf.unsqueeze(1).broadcast_to([P, 2, 64]), op=ALU.mult),
             reads=[ps1.b, ROPE.b], writes=[T1.b])
        k.op("dve", lambda e: e.tensor_tensor(out=t2k[:, :, 0, :], in0=kin4[:, :, 1, :], in1=nsin.unsqueeze(1).broadcast_to([P, 2, 32]), op=ALU.mult),
             reads=[ps1.b, ROPE.b], writes=[T2.b])
        k.op("dve", lambda e: e.tensor_tensor(out=t2k[:, :, 1, :], in0=kin4[:, :, 0, :], in1=sin.unsqueeze(1).broadcast_to([P, 2, 32]), op=ALU.mult),
             reads=[ps1.b, ROPE.b], writes=[T2.b])
        k.op("dve", lambda e: e.tensor_copy(out=VA[:P, :, 0:64], in_=ps1.f[:P, 128:256].rearrange("p (g d) -> p g d", g=2)),
             reads=[ps1.b], writes=[VA.b])
        k.op("dve", lambda e: e.tensor_copy(out=GL[:P, :], in_=ps1.f[:P, 256:272]), reads=[ps1.b], writes=[GL.b])
        ps2 = proj_group(UT, P, WINA, 8, 784, 512, WAb[2], BIASA)
        yield
        k.op("pool", lambda e: e.tensor_tensor(out=QR[:P, :], in0=T1[:P, 0:512], in1=T2[:P, 0:512], op=ALU.add),
             reads=[T1.b, T2.b], writes=[QR.b])
        k.op("pool", lambda e: e.tensor_tensor(out=KR[:P, :], in0=T1[:P, 512:640], in1=T2[:P, 512:640], op=ALU.add),
             reads=[T1.b, T2.b], writes=[KR.b])
        k.op("act", lambda e: e.copy(out=GQK[:P, :], in_=ps2.f[:P, :]), reads=[ps2.b], writes=[GQK.b])
        ps3 = proj_group(UT, P, WINA, 8, 1296, 512, WAb[3], BIASA)
        yield
        k.op("act", lambda e: e.copy(out=V[:P, :], in_=ps3.f[:P, :]), reads=[ps3.b], writes=[V.b])
        ps4 = proj_group(UT, P, WINA, 8, 1808, 512, WAb[4], BIASA)
        yield
        k.op("act", lambda e: e.activation(out=TH[:P, :], in_=ps4.f[:P, :], func=AF.Exp, scale=-1.0), reads=[ps4.b], writes=[TH.b])
        pst = psn()
        ptv = pst.h.rearrange("p (c q) -> p c q", c=8)
        for c in range(4):
            k.op("pe", lambda e, c=c: e.transpose(out=ptv[:, c, :P], in_=QR[:P, c * 128:(c + 1) * 128], identity=IDENT[:P, :P]),
                 reads=[QR.b, IDENT.b], writes=[pst.b])
        k.op("pe", lambda e: e.transpose(out=ptv[:, 4, :P], in_=KR[:P, :], identity=IDENT[:P, :P]),
             reads=[KR.b, IDENT.b], writes=[pst.b])
        k.op("pe", lambda e: e.transpose(out=ptv[0:16, 5, :P], in_=GL[:P, :], identity=IDENT[:P, :P]),
             reads=[GL.b, IDENT.b], writes=[pst.b])
        yield
        k.op("dve", lambda e: e.tensor_scalar_add(out=TH[:P, :], in0=TH[:P, :], scalar1=1.0), reads=[TH.b], writes=[TH.b])
        k.op("dve", lambda e: e.reciprocal(out=TH[:P, :], in_=TH[:P, :]), reads=[TH.b], writes=[TH.b])
        k.op("dve", lambda e: e.tensor_tensor(out=SG[:P, :], in0=TH[:P, :], in1=ps4.f[:P, :], op=ALU.mult),
             reads=[TH.b, ps4.b], writes=[SG.b])
        k.op("act", lambda e: e.copy(out=QT[:, :, :P], in_=ptv[:, 0:4, :P]), reads=[pst.b], writes=[QT.b])
        k.op("act", lambda e: e.copy(out=KT[:, :P], in_=ptv[:, 4, :P]), reads=[pst.b], writes=[KT.b])
        k.op("act", lambda e: e.copy(out=GLT[0:16, :P], in_=ptv[0:16, 5, :P]), reads=[pst.b], writes=[GLT.b])
        if t + 1 < ntiles:
            P1 = tsz(t + 1)
            U1, UT1 = U_[(t + 1) % 2], UT_[(t + 1) % 2]
            psu = psn()
            puv = psu.h.rearrange("p (c q) -> p c q", c=8)
            for c in range(8):
                k.op("pe", lambda e, c=c: e.transpose(out=puv[:, c, :P1], in_=U1[:P1, c * 128:(c + 1) * 128], identity=IDENT[:P1, :P1]),
                     reads=[U1.b, IDENT.b], writes=[psu.b])
            yield
            k.op("dve", lambda e: e.tensor_tensor(out=UT1[:, :, :P1], in0=puv[:, :, :P1],
                                                  in1=GAINT[:, 0, :].unsqueeze(2).broadcast_to([128, 8, P1]), op=ALU.mult),
                 reads=[psu.b, GAINT.b, GAINT2], writes=[UT1.b])
        yield

    def stage_att(t):
        P = tsz(t)
        QT = QT_[t % 3]
        t0 = tok0(t)
        blocks = ([] if t == 0 else [(t - 1, 0)]) + [(t, 1)]
        combos = [(g, tk, kb) for g in range(2) for (tk, kb) in blocks]
        pss_l = []
        def emit_score(c):
            g, tk, kb = combos[c]
            Pk = tsz(tk)
            pss = psn()
            sv = pss.f[:Pk, 0:4 * P].rearrange("p (i q) -> p i q", i=4)
            KTk = KT_[tk % 4]
            k.op("pe", lambda e: e.matmul(sv, lhsT=KTk[g * 64:(g + 1) * 64, :Pk], rhs=QT[g * 64:(g + 1) * 64, :, :P], start=True, stop=True),
                 reads=[KTk.b, QT.b], writes=[pss.b])
            pss_l.append((pss, sv, Pk, PT_[g][kb], kb))
        def emit_exp(c):
            pss, sv, Pk, PT, kb = pss_l[c]
            k.op("act", lambda e: e.activation(out=PT[:Pk, :, :P], in_=sv, func=AF.Exp, scale=0.125), reads=[pss.b], writes=[PT.b])
        def emit_mask(c):
            pss, sv, Pk, PT, kb = pss_l[c]
            if kb == 1:
                mk = MASKS[:Pk, 0, :P]
            else:
                mk = MASKS[:Pk, 2, :P] if t == 1 else MASKS[:Pk, 1, :P]
            k.op("dve", lambda e: e.tensor_tensor(out=PT[:Pk, :, :P], in0=PT[:Pk, :, :P], in1=mk.unsqueeze(1).broadcast_to([Pk, 4, P]), op=ALU.mult),
                 reads=[PT.b, MASKS.b], writes=[PT.b])
        nc_ = len(combos)
        for step in range(nc_ + 2):
            if step < nc_:
                emit_score(step)
            if 0 <= step - 1 < nc_:
                emit_exp(step - 1)
            if 0 <= step - 2 < nc_:
                emit_mask(step - 2)
            yield
        pso = [psn(), psn()]
        for g in range(2):
            for i in range(4):
                for bi, (tk, kb) in enumerate(blocks):
                    Pk = tsz(tk)
                    PT = PT_[g][kb]
                    VAk = VA_[tk % 4]
                    k.op("pe", lambda e, g=g, i=i, Pk=Pk, PT=PT, VAk=VAk, bi=bi: e.matmul(
                        pso[g].f[:P, i * 128:i * 128 + 66], lhsT=PT[:Pk, i, :P], rhs=VAk[:Pk, g, :],
                        start=(bi == 0), stop=(bi == len(blocks) - 1)),
                        reads=[PT.b, VAk.b], writes=[pso[g].b])
        yield
        for g in range(2):
            k.op("act", lambda e, g=g: e.copy(out=OA[:P, g * 4:(g + 1) * 4, :], in_=pso[g].f[:P, :].rearrange("p (i c) -> p i c", i=4)[:, :, 0:66]),
                 reads=[pso[g].b], writes=[OA.b])
        yield
        DEN = DENS[:P, 0:8]
        oav = OA[:P, :, :]
        k.op("dve", lambda e: e.tensor_tensor(out=DEN, in0=oav[:, :, 64], in1=ESINK[:P, :], op=ALU.add),
             reads=[OA.b, ESINK.b], writes=[DENS.b])
        k.op("dve", lambda e: e.reciprocal(out=DEN, in_=DEN), reads=[DENS.b], writes=[DENS.b])
        k.op("dve", lambda e: e.tensor_tensor(
            out=YATT[:P, :].rearrange("p (h d) -> p h d", h=8), in0=oav[:, :, 0:64],
            in1=DEN.unsqueeze(2).broadcast_to([P, 8, 64]), op=ALU.mult),
            reads=[OA.b, DENS.b], writes=[YATT.b])
        yield
        psy, pyv = transposes(YATT, 4, P, [YATT.b])
        yield
        yield
        k.op("act", lambda e: e.copy(out=YA[:, :, t0:t0 + P], in_=pyv[:, 0:4, :P]), reads=[psy.b], writes=[YA.b])
        yield

    def stage_gla(t):
        P = tsz(t)
        GLT, GQK, V, SG = GLT_[t % 3], GQK_[t % 3], V_[t % 3], SG_[t % 3]
        j2 = t % 2
        SPt, ENB, EBT, ENBT, QET, KET, KE, AT, OSQ, YGT, S4 = (SPt_[j2], ENB_[j2], EBT_[j2], ENBT_[j2], QET_[j2], KET_[j2], KE_[j2],
                                                                 AT_[j2], OSQ_[j2], YGT_[j2], S4_[j2])
        t0 = tok0(t)
        psz = psn()
        k.op("pe", lambda e: e.matmul(psz.f[:P, 0:256], lhsT=GLT[0:17, :P], rhs=WALPHA[0:17, :], start=True, stop=True),
             reads=[GLT.b, WALPHA.b, WALPHA2], writes=[psz.b])
        psq, pqv = transposes(GQK, 4, P, [GQK.b])
        yield
        k.op("act", lambda e: e.activation(out=SPt[:P, :], in_=psz.f[:P, 0:256], func=AF.Exp, scale=-1.0), reads=[psz.b], writes=[SPt.b])
        k.op("act", lambda e: e.activation(out=SPt[:P, :], in_=SPt[:P, :], func=AF.Ln, bias=1.0), reads=[SPt.b], writes=[SPt.b])
        yield
        psb = psn()
        bTv = psb.f[:, 256:512].rearrange("p (c q) -> p c q", c=2)
        k.op("pe", lambda e: e.matmul(psb.f[:P, 0:256], lhsT=UTRI[:P, :P], rhs=SPt[:P, :], start=True, stop=True),
             reads=[UTRI.b, SPt.b], writes=[psb.b])
        for pr in range(2):
            k.op("pe", lambda e, pr=pr: e.matmul(bTv[:, pr, :P], lhsT=SPt[:P, pr * 128:(pr + 1) * 128], rhs=UTRI[:P, :P], start=True, stop=True),
                 reads=[UTRI.b, SPt.b], writes=[psb.b])
        yield
        k.op("act", lambda e: e.activation(out=ENB[:P, :], in_=psb.f[:P, 0:256], func=AF.Exp, scale=-1.0), reads=[psb.b], writes=[ENB.b])
        k.op("act", lambda e: e.activation(out=EBT[:, :, :P], in_=bTv[:, :, :P], func=AF.Exp), reads=[psb.b], writes=[EBT.b])
        k.op("act", lambda e: e.activation(out=ENBT[:, :, :P], in_=bTv[:, :, :P], func=AF.Exp, scale=-1.0), reads=[psb.b], writes=[ENBT.b])
        yield
        k.op("pool", lambda e: e.tensor_tensor(out=KE[:P, :], in0=GQK[:P, 256:512], in1=ENB[:P, :], op=ALU.mult),
             reads=[GQK.b, ENB.b], writes=[KE.b])
        k.op("dve", lambda e: e.scalar_tensor_tensor(out=QET[:, :, :P], in0=pqv[:, 0:2, :P], scalar=0.125, in1=EBT[:, :, :P], op0=ALU.mult, op1=ALU.mult),
             reads=[psq.b, EBT.b], writes=[QET.b])
        k.op("dve", lambda e: e.tensor_tensor(out=KET[:, :, :P], in0=pqv[:, 2:4, :P], in1=ENBT[:, :, :P], op=ALU.mult),
             reads=[psq.b, ENBT.b], writes=[KET.b])
        yield
        while t > 0 and not sdone[t - 1]:
            yield
        if t < NT - 1:
            psd = psn()
            dsv = psd.f[:, :].rearrange("p (h e) -> p h e", h=4)
            for h in range(4):
                k.op("pe", lambda e, h=h: e.matmul(dsv[:, h, :], lhsT=KE[:P, (h // 2) * 128:(h // 2 + 1) * 128], rhs=V[:P, h * 128:(h + 1) * 128], start=True, stop=True),
                     reads=[KE.b, V.b], writes=[psd.b])
            yield
            dsv2 = psd.f[:, :].rearrange("p (c w e) -> p w c e", c=2, w=2)
            SBn = SBF_[(t + 1) % 3]
            for par in range(2):
                r0 = par * 64
                k.op("dve", lambda e, par=par, r0=r0: e.tensor_tensor(out=SST[r0:r0 + 64, :, :], in0=dsv2[r0:r0 + 64, par, :, :], in1=SST[r0:r0 + 64, :, :], op=ALU.add),
                     reads=[psd.b, SST.b], writes=[SST.b])
            for par in range(2):
                r0 = par * 64
                k.op("dve", lambda e, r0=r0: e.tensor_tensor(out=SST[r0:r0 + 64, :, :], in0=SST[r0:r0 + 64, :, :],
                                                          in1=EBT[r0:r0 + 64, :, P - 1:P].broadcast_to([64, 2, 128]), op=ALU.mult),
                     reads=[SST.b, EBT.b], writes=[SST.b])
            k.op("dve", lambda e: e.tensor_copy(out=SBn[:, :, :], in_=SST[:, :, :]), reads=[SST.b], writes=[SBn.b])
        sdone[t] = True
        yield
        psa = [psn(), psn()]
        for h in range(4):
            r0 = (h % 2) * 64
            avh = psa[h % 2].f[:P, 0:2 * P].rearrange("p (c q) -> p c q", c=2)
            k.op("pe", lambda e, h=h, r0=r0, avh=avh: e.matmul(avh[:, h // 2, :], lhsT=KET[r0:r0 + 64, h // 2, :P], rhs=QET[r0:r0 + 64, h // 2, :P], start=True, stop=True),
                 reads=[KET.b, QET.b], writes=[psa[h % 2].b])
        yield
        atv = AT[:P, :, :P].rearrange("p (c w) q -> p w c q", w=2)
        for par in range(2):
            avp = psa[par].f[:P, 0:2 * P].rearrange("p (c q) -> p c q", c=2)
            k.op("dve", lambda e, par=par, avp=avp: e.tensor_tensor(out=atv[:, par, :, :], in0=avp, in1=MASKS[:P, 0, :P].unsqueeze(1).broadcast_to([P, 2, P]), op=ALU.mult),
                 reads=[psa[par].b, MASKS.b], writes=[AT.b])
        yield
        SBc = SBF_[t % 3]
        psg = psn()
        ogv = psg.f[:P, :].rearrange("p (h e) -> p h e", h=4)
        for h in range(4):
            r0 = (h % 2) * 64
            if t > 0:
                k.op("pe", lambda e, h=h, r0=r0: e.matmul(ogv[:, h, :], lhsT=QET[r0:r0 + 64, h // 2, :P], rhs=SBc[r0:r0 + 64, h // 2, :], start=True, stop=False),
                     reads=[QET.b, SBc.b], writes=[psg.b])
            k.op("pe", lambda e, h=h: e.matmul(ogv[:, h, :], lhsT=AT[:P, h, :P], rhs=V[:P, h * 128:(h + 1) * 128], start=(t == 0), stop=True),
                 reads=[AT.b, V.b], writes=[psg.b])
        yield
        SS4 = S4[:P, 0:4]
        RS4 = S4[:P, 4:8]
        k.op("act", lambda e: e.activation(out=OSQ[:P, :], in_=psg.f[:P, :], func=AF.Square), reads=[psg.b], writes=[OSQ.b])
        yield
        k.op("dve", lambda e: e.reduce_sum(out=SS4, in_=OSQ[:P, :].rearrange("p (h e) -> p h e", h=4), axis=AX.X),
             reads=[OSQ.b], writes=[S4.b])
        yield
        k.op("act", lambda e: e.activation(out=RS4, in_=SS4, func=AF.Ln, scale=1.0 / 128, bias=EPS), reads=[S4.b], writes=[S4.b])
        k.op("act", lambda e: e.activation(out=RS4, in_=RS4, func=AF.Exp, scale=-0.5), reads=[S4.b], writes=[S4.b])
        yield
        k.op("dve", lambda e: e.tensor_tensor(out=OSQ[:P, :].rearrange("p (h e) -> p h e", h=4), in0=ogv,
                                              in1=RS4.unsqueeze(2).broadcast_to([P, 4, 128]), op=ALU.mult),
             reads=[psg.b, S4.b, OSQ.b], writes=[OSQ.b])
        yield
        k.op("pool", lambda e: e.tensor_tensor(out=YGT[:P, :], in0=OSQ[:P, :], in1=SG[:P, :], op=ALU.mult),
             reads=[OSQ.b, SG.b], writes=[YGT.b])
        yield
        psy2, pyv2 = transposes(YGT, 4, P, [YGT.b])
        yield
        yield
        k.op("act", lambda e: e.copy(out=YG[:, :, t0:t0 + P], in_=pyv2[:, 0:4, :P]), reads=[psy2.b], writes=[YG.b])
        yield

    act_rsqrt[0] = True
    norm_pre(0, U_[0])
    for i in range(1, len(segs)):
        load_seg(i)
    for t in range(3, NT):
        load_x(t, extra=[seg_ops[-1]])
    pool_cur[0] = "A"
    norm_T(0, 0, U_[0], UT_[0][:, :, :tsz(0)], UT_[0].b)

    def can_start(n, i, done):
        if n == "A":
            if i < 3:
                return True
            m = i - 3
            return done["B1"] >= i - 2 and done["B2e"] >= m // 2 + 1 and done["B2o"] >= (m + 1) // 2
        if n == "B1":
            return done["A"] >= i + 1
        tile = 2 * i if n == "B2e" else 2 * i + 1
        return done["A"] >= tile + 1

    def sb_at(off, shape, dt, name):
        save = cur[0]
        cur[0] = off
        r = sb(shape, dt, name)
        cur[0] = save
        return r

    WINB = sb_at(wina_off, [128, 8, 2048], BF16, "winB")
    WPA = sb_at(u_off, [128, 4, D], BF16, "wpa")
    WPG = sb_at(rope_off, [128, 4, D], BF16, "wpg")
    WBb = [k.buf(f"winB{i}") for i in range(4)]
    wbd = [k.dsem(f"wb{i}") for i in range(4)]
    wina_tokens = WAb + [WAb1b]

    def prefetch_gen():
        for i in range(4):
            k.op("pool", lambda e, i=i: e.dma_start(out=WINB[:, :, i * 512:(i + 1) * 512], in_=win_v[:, :, 2320 + i * 512: 2320 + (i + 1) * 512]),
                 writes=[WBb[i]] + wina_tokens, dsem=wbd[i])
        cdma("pool", WPA[:, :, :], wpa_d[0].rearrange("(c p) n -> p c n", p=128), [WPA.b, U_[0].b, U_[1].b, UT_[0].b, UT_[1].b])
        cdma("pool", WPG[:, :, :], wpg_d[0].rearrange("(c p) n -> p c n", p=128), [WPG.b, ROPE.b])
        yield

    def can_start_w(n, i, done):
        if n == "W":
            return done["A"] >= ntiles
        return can_start(n, i, done)

    rolling({"A": ("A", [(lambda t=t: stage_a(t)) for t in range(ntiles)]),
             "B1": ("B1", [(lambda t=t: stage_att(t)) for t in range(ntiles)]),
             "B2e": ("B2e", [(lambda t=t: stage_gla(t)) for t in range(0, ntiles, 2)]),
             "B2o": ("B2o", [(lambda t=t: stage_gla(t)) for t in range(1, ntiles, 2)]),
             "W": ("A", [prefetch_gen])}, can_start_w)

    act_rsqrt[0] = False
    if stop == "p1a":
        return early_exit()
    k.barrier()

    cur[0] = wina_off + 32768
    BIASB = sb([1, 2048], BF16, "biasB")
    GHN = sb([128, 1], F32, "ghn")
    M1_ = [sb([128, 512], F32, f"M1{i}") for i in range(2)]
    assert cur[0] <= rope_off
    cur[0] = u_off + 8192
    WOUT = sb([128, 8, D], BF16, "wout")
    cdma("pool", BIASB[0:1, :], bin_d[0:1, 2320:4368], [BIASB.b])
    cdma("pool", WOUT[:, :, :], wout_d[0].rearrange("(c p) n -> p c n", p=128), [WOUT.b])
    cdma("sp", GHN[:, :], ghn_d[0].rearrange("(p o) -> p o", o=1), [GHN.b], slow=True)
    k.op("dve", lambda e: e.tensor_scalar_mul(out=WPG[:, :, :], in0=WPG[:, :, :], scalar1=GHN[:, 0:1]),
         reads=[WPG.b, GHN.b], writes=[WPG.b])

    UB_ = [sb([128, D], BF16, f"UB{i}") for i in range(2)]
    UTB_ = [sb([128, 8, 128], BF16, f"UTB{i}") for i in range(2)]
    TA_ = [sb([128, 2048], BF16, f"TA{i}") for i in range(2)]
    M2_ = [sb([128, 512], F32, f"M2{i}") for i in range(2)]
    MIX_ = [sb([128, D], BF16, f"MIX{i}") for i in range(2)]
    MIXT_ = [sb([128, 8, 128], BF16, f"MIXT{i}") for i in range(2)]

    def stage_gates(t):
        P = tsz(t)
        U, UT, TA = UB_[t % 2], UTB_[t % 2], TA_[t % 2]
        norm_T(t, 0, U, UT[:, :, :P], UT.b)
        yield
        if t + 1 < NT:
            norm_pre(t + 1, UB_[(t + 1) % 2])
        for i in range(4):
            ps = proj_group(UT, P, WINB, 8, i * 512, 512, WBb[i], BIASB)
            k.op("act", lambda e, ps=ps, i=i: e.activation(out=TA[:P, i * 512:(i + 1) * 512], in_=ps.f[:P, :], func=AF.Tanh, scale=0.5),
                 reads=[ps.b], writes=[TA.b])
            yield

    def stage_merge(t):
        P = tsz(t)
        TA, MIX, MIXT = TA_[t % 2], MIX_[t % 2], MIXT_[t % 2]
        t0 = tok0(t)
        for hf in range(2):
            M1, M2 = M1_[hf], M2_[hf]
            pa = psn()
            for c in range(4):
                k.op("pe", lambda e, c=c, pa=pa, hf=hf: e.matmul(pa.f[:P, :], lhsT=YA[:, c, t0:t0 + P], rhs=WPA[:, c, hf * 512:(hf + 1) * 512], start=(c == 0), stop=(c == 3)),
                     reads=[YA.b, WPA.b], writes=[pa.b])
            pg = psn()
            for c in range(4):
                k.op("pe", lambda e, c=c, pg=pg, hf=hf: e.matmul(pg.f[:P, :], lhsT=YG[:, c, t0:t0 + P], rhs=WPG[:, c, hf * 512:(hf + 1) * 512], start=(c == 0), stop=(c == 3)),
                     reads=[YG.b, WPG.b], writes=[pg.b])
            yield
            k.op("dve", lambda e, pa=pa, hf=hf, M1=M1: e.scalar_tensor_tensor(out=M1[:P, :], in0=TA[:P, hf * 512:(hf + 1) * 512], scalar=1.0, in1=pa.f[:P, :], op0=ALU.add, op1=ALU.mult),
                 reads=[TA.b, pa.b], writes=[M1.b])
            k.op("dve", lambda e, pg=pg, hf=hf, M2=M2: e.scalar_tensor_tensor(out=M2[:P, :], in0=TA[:P, 1024 + hf * 512:1024 + (hf + 1) * 512], scalar=1.0, in1=pg.f[:P, :], op0=ALU.add, op1=ALU.mult),
                 reads=[TA.b, pg.b], writes=[M2.b])
            k.op("pool", lambda e, hf=hf, M1=M1, M2=M2: e.tensor_tensor(out=MIX[:P, hf * 512:(hf + 1) * 512], in0=M1[:P, :], in1=M2[:P, :], op=ALU.add),
                 reads=[M1.b, M2.b], writes=[MIX.b])
            yield
        yield
        psm, pmv = transposes(MIX, 8, P, [MIX.b])
        k.op("act", lambda e: e.copy(out=MIXT[:, :, :P], in_=pmv[:, :, :P]), reads=[psm.b], writes=[MIXT.b])
        yield
        for hf in range(2):
            po = psn()
            for c in range(8):
                k.op("pe", lambda e, c=c, po=po, hf=hf: e.matmul(po.f[:P, :], lhsT=MIXT[:, c, :P], rhs=WOUT[:, c, hf * 512:(hf + 1) * 512], start=(c == 0), stop=(c == 7)),
                     reads=[MIXT.b, WOUT.b], writes=[po.b])
            k.op("dve", lambda e, po=po, hf=hf: e.scalar_tensor_tensor(out=H[:P, t, hf * 512:(hf + 1) * 512], in0=po.f[:P, :], scalar=0.5,
                                                                       in1=H[:P, t, hf * 512:(hf + 1) * 512], op0=ALU.mult, op1=ALU.add),
                 reads=[po.b, Hb[t]], writes=[Hb[t]])
            yield

    norm_pre(0, UB_[0])
    interleave(("C", stage_gates(0)))
    for t in range(NT):
        interleave(("C", stage_gates(t + 1)) if t + 1 < NT else None, ("D", stage_merge(t)))

    if stop == "p1b":
        return early_exit()
    k.barrier()

    if debug == "hmid":
        dd = k.dsem("dbg")
        k.op("sp", lambda e: e.dma_start(out=dbg_d[0:NMETA, :], in_=H[0:NMETA, 0, :]), reads=[Hb[0]], dsem=dd)
        for t in range(1, NT):
            k.op("sp", lambda e, t=t: e.dma_start(out=dbg_d[tok0(t):tok0(t) + 128, :], in_=H[:, t, :]), reads=[Hb[t]], dsem=dd)
        k.barrier()

    cur[0] = persist_end
    U2T = sb([128, 8, L], BF16, "u2t")
    U2_ = [sb([128, D], BF16, f"U2{i}") for i in range(2)]
    FNORM = sb([128, D], F32, "fnorm")
    STF = sb([128, 8], F32, "stf")
    CWB = sb([128, NCH, 4], F32, "cwb")
    HALO = sb([128, NCH, 2], F32, "halo")
    IDF = sb([4, 4], F32, "idf")
    quarters = [(0, 6), (6, 6), (12, 5), (17, 5)]
    WA_ = [sb([128, 8, 768], BF16, f"WA{i}") for i in range(2)]
    WV_ = [sb([128, 8, 768], BF16, f"WV{i}") for i in range(2)]
    WD_ = [sb([128, 6, D], BF16, f"WD{i}") for i in range(2)]
    qds = [[k.dsem(f"q{i}{j}") for j in range(3)] for i in range(2)]
    ASB_ = [sb([128, 514], F32, f"ASB{i}") for i in range(2)]
    Y_ = [sb([128, 512], F32, f"Yc{i}") for i in range(2)]
    mt_off = cur[0]
    MT_ = [sb([128, 6, 512], BF16, f"MT{i}") for i in range(2)]
    out_off = cur[0]
    OUT = sb([128, D], F32, "OUT")
    U2x = [R(big[0:128, out_off // 2 + j * D: out_off // 2 + (j + 1) * D], k.buf(f"u2x{j}")) for j in range(2)]
    U2s = U2_ + U2x
    STP = [sb([128, 8], F32, f"stp{j}") for j in range(4)]
    cw_ap = big[0:4, mt_off // 2: mt_off // 2 + D_FF * 2].bitcast(F32)
    assert D_FF * 4 <= 2 * 6 * 512 * 2
    MTB = [MT_[0].b, MT_[1].b]

    wup_v = wup_d[0].rearrange("(kc p) c -> p kc c", p=128)
    wdn_v = wdn_d[0].rearrange("(c p) n -> p c n", p=128)

    def load_quarter(qi):
        c0, nq = quarters[qi]
        sl = qi % 2
        k.op("pool", lambda e: e.dma_start(out=WA_[sl][:, :, 0:nq * 128], in_=wup_v[:, :, c0 * 128:(c0 + nq) * 128]), writes=[WA_[sl].b], dsem=qds[sl][0])
        k.op("pool", lambda e: e.dma_start(out=WV_[sl][:, :, 0:nq * 128], in_=wup_v[:, :, D_FF + c0 * 128:D_FF + (c0 + nq) * 128]), writes=[WV_[sl].b], dsem=qds[sl][1])
        k.op("pool", lambda e: e.dma_start(out=WD_[sl][:, 0:nq, :], in_=wdn_v[:, c0:c0 + nq, :]), writes=[WD_[sl].b], dsem=qds[sl][2])

    load_quarter(0)
    load_quarter(1)
    cdma("sp", FNORM[:, :], fnorm_d.rearrange("(o d) -> o d", o=1).partition_broadcast(128), [FNORM.b])
    cdma("sp", cw_ap[0:3, :], convw_d[0], MTB)
    cdma("sp", cw_ap[3:4, :], convb_d[0:1, :], MTB)
    cdma("sp", IDF[:, :], ident_d[0:4, 0:4], [IDF.b])
    k.op("dve", lambda e: e.memset(HALO[:, :, :], 0.0), writes=[HALO.b])
    psc = psn()
    cwv = psc.f[:, 0:NCH * 4].rearrange("p (c f) -> p c f", f=4)
    for c in range(NCH):
        k.op("pe", lambda e, c=c: e.transpose(out=cwv[:, c, :], in_=cw_ap[0:4, c * 128:(c + 1) * 128], identity=IDF[0:4, 0:4]),
             reads=MTB + [IDF.b], writes=[psc.b])
    k.op("dve", lambda e: e.tensor_copy(out=CWB[:, :, :], in_=cwv), reads=[psc.b], writes=[CWB.b])

    act_rsqrt[0] = True
    for j in range(min(3, NT)):
        norm_pre(j, U2s[j % 4], ST=STP[j % 4])
    for t in range(NT):
        if t + 3 < NT:
            norm_pre(t + 3, U2s[(t + 3) % 4], ST=STP[(t + 3) % 4])
        norm_T(t, 1, U2s[t % 4], U2T[:, :, tok0(t):tok0(t) + tsz(t)], U2T.b)
    act_rsqrt[0] = False
    groups = [(0, NMETA, [0])] + [(NMETA + 512 * g, 512, [1 + 4 * g + j for j in range(4)]) for g in range(4)]
    ods = [k.dsem(f"o{i}") for i in range(4)]
    it = [0]
    gcnt = [0]

    def ffn_up(qi, gi):
        c0, nq = quarters[qi]
        sl = qi % 2
        WA, WV = WA_[sl], WV_[sl]
        g0, N, tiles = groups[gi]
        MT = MT_[gi % 2]
        for ci in range(nq):
            c = c0 + ci
            j = it[0] % 2
            it[0] += 1
            ASB, Y = ASB_[j], Y_[j]
            psA = psn()
            for kc in range(8):
                k.op("pe", lambda e, kc=kc, psA=psA, ci=ci: e.matmul(psA.f[:, :N], lhsT=WA[:, kc, ci * 128:(ci + 1) * 128], rhs=U2T[:, kc, g0:g0 + N], start=(kc == 0), stop=(kc == 7)),
                     reads=[WA.b, U2T.b], writes=[psA.b])
            if gi == 0:
                k.op("act", lambda e, psA=psA, c=c: e.copy(out=HALO[:, c, :], in_=psA.f[:, N - 2:N]), reads=[psA.b], writes=[HALO.b])
                yield
                continue
            psV = psn()
            for kc in range(8):
                k.op("pe", lambda e, kc=kc, psV=psV, ci=ci: e.matmul(psV.f[:, :N], lhsT=WV[:, kc, ci * 128:(ci + 1) * 128], rhs=U2T[:, kc, g0:g0 + N], start=(kc == 0), stop=(kc == 7)),
                     reads=[WV.b, U2T.b], writes=[psV.b])
            k.op("act", lambda e, psA=psA, ASB=ASB: e.copy(out=ASB[:, 2:2 + N], in_=psA.f[:, :N]), reads=[psA.b], writes=[ASB.b])
            k.op("act", lambda e, ASB=ASB, c=c: e.copy(out=ASB[:, 0:2], in_=HALO[:, c, :]), reads=[HALO.b, ASB.b], writes=[ASB.b])
            k.op("act", lambda e, ASB=ASB, c=c: e.copy(out=HALO[:, c, :], in_=ASB[:, N:N + 2]), reads=[ASB.b, HALO.b], writes=[HALO.b])
            yield
            k.op("dve", lambda e, ASB=ASB, Y=Y, c=c: e.tensor_scalar(out=Y[:, :N], in0=ASB[:, 2:2 + N], scalar1=CWB[:, c, 2:3], scalar2=CWB[:, c, 3:4], op0=ALU.mult, op1=ALU.add),
                 reads=[ASB.b, CWB.b], writes=[Y.b])
            k.op("dve", lambda e, ASB=ASB, Y=Y, c=c: e.scalar_tensor_tensor(out=Y[:, :N], in0=ASB[:, 1:1 + N], scalar=CWB[:, c, 1:2], in1=Y[:, :N], op0=ALU.mult, op1=ALU.add),
                 reads=[ASB.b, CWB.b, Y.b], writes=[Y.b])
            k.op("dve", lambda e, ASB=ASB, Y=Y, c=c: e.scalar_tensor_tensor(out=Y[:, :N], in0=ASB[:, 0:N], scalar=CWB[:, c, 0:1], in1=Y[:, :N], op0=ALU.mult, op1=ALU.add),
                 reads=[ASB.b, CWB.b, Y.b], writes=[Y.b])
            k.op("act", lambda e, Y=Y: e.activation(out=Y[:, :N], in_=Y[:, :N], func=AF.Gelu_apprx_tanh), reads=[Y.b], writes=[Y.b])
            k.op("dve", lambda e, Y=Y, psV=psV, ci=ci: e.tensor_tensor(out=MT[:, ci, :N], in0=Y[:, :N], in1=psV.f[:, :N], op=ALU.mult),
                 reads=[Y.b, psV.b], writes=[MT.b])
            yield

    pend = []

    def final_norm(t):
        jk = U2_[t % 2]
        ss = STF[:, 0:1]
        rs = STF[:, 1:2]
        k.op("act", lambda e: e.activation(out=jk[:, :], in_=H[:, t, :], func=AF.Square, accum_out=ss),
             reads=[Hb[t]], writes=[jk.b, STF.b])
        yield
        k.op("dve", lambda e: e.tensor_scalar(out=rs, in0=ss, scalar1=1.0 / D, scalar2=EPS, op0=ALU.mult, op1=ALU.add),
             reads=[STF.b], writes=[STF.b])
        yield
        k.op("pool", lambda e: e.tensor_tensor(out=rs, in0=rs, in1=NEGH[:, 0:1], op=ALU.pow), reads=[STF.b, NEGH.b], writes=[STF.b])
        yield
        k.op("dve", lambda e: e.scalar_tensor_tensor(out=OUT[:, :], in0=H[:, t, :], scalar=rs, in1=FNORM[:, :], op0=ALU.mult, op1=ALU.mult),
             reads=[Hb[t], STF.b, FNORM.b], writes=[OUT.b, U2x[0].b, U2x[1].b])
        yield
        k.op("sp", lambda e: e.dma_start(out=y_d[(t - 1) * 128:t * 128, :], in_=OUT[:, :]), reads=[OUT.b], dsem=ods[t % 4])
        yield

    def advance_pending():
        for g in list(pend[:1]):
            try:
                next(g)
            except StopIteration:
                pend.remove(g)

    def ffn_down(qi, gi):
        c0, nq = quarters[qi]
        sl = qi % 2
        WD = WD_[sl]
        g0, N, tiles = groups[gi]
        MT = MT_[gi % 2]
        last_q = qi == len(quarters) - 1
        yield
        yield
        for jt, t in enumerate(tiles):
            for hf in range(2):
                pd = psn()
                for ci in range(nq):
                    k.op("pe", lambda e, ci=ci, pd=pd, jt=jt, hf=hf: e.matmul(pd.f[:, :], lhsT=MT[:, ci, jt * 128:(jt + 1) * 128], rhs=WD[:, ci, hf * 512:(hf + 1) * 512], start=(ci == 0), stop=(ci == nq - 1)),
                         reads=[MT.b, WD.b], writes=[pd.b])
                k.op("dve", lambda e, pd=pd, t=t, hf=hf: e.tensor_tensor(out=H[:, t, hf * 512:(hf + 1) * 512], in0=pd.f[:, :], in1=H[:, t, hf * 512:(hf + 1) * 512], op=ALU.add),
                     reads=[pd.b, Hb[t]], writes=[Hb[t]])
                if last_q:
                    advance_pending()
                yield
                if last_q:
                    advance_pending()
            if last_q:
                pend.append(final_norm(t))
        if last_q and gi == len(groups) - 1:
            while pend:
                advance_pending()
                yield

    seq = [(qi, gi) for qi in range(len(quarters)) for gi in range(1, len(groups))]
    prev = None
    for (qi, gi) in seq:
        if gi == 1:
            interleave(("C", ffn_up(qi, 0)))
        interleave(("C", ffn_up(qi, gi)), ("D", ffn_down(*prev)) if prev is not None else None)
        if prev is not None and prev[1] == len(groups) - 1 and prev[0] + 2 < len(quarters):
            load_quarter(prev[0] + 2)
        prev = (qi, gi)
    interleave(("D", ffn_down(*prev)))

    finals = list(ods)
    if debug:
        finals.append(dd)
    k.generate(final_waits=finals)
    return nc


def _consts():
    ident = np.eye(128, dtype=np.float32)
    kk = np.arange(128)[:, None]
    qq = np.arange(128)[None, :]
    m_cur = (kk <= qq).astype(np.float32)
    m_prev = (kk > qq).astype(np.float32)
    m_p01 = (kk > qq - 112).astype(np.float32)
    masks = np.stack([m_cur, m_prev, m_p01]).astype(np.float32)
    utri = (kk <= qq).astype(np.float32) * np.float32(-1.0 / 16.0)
    half = 32
    inv_freq = (10000.0 ** (-np.arange(half, dtype=np.float32) / half)).astype(np.float32)
    rope = np.zeros((128, NT, 128), np.float32)
    for t in range(NT):
        P = tsz(t)
        pos = (tok0(t) + np.arange(P)).astype(np.float32)
        ang = pos[:, None] * inv_freq[None, :]
        c = np.cos(ang).astype(np.float32)
        s = np.sin(ang).astype(np.float32)
        rope[:P, t, 0:32] = c
        rope[:P, t, 32:64] = c
        rope[:P, t, 64:96] = s
        rope[:P, t, 96:128] = -s
    return dict(c_ident=ident, c_masks=masks, c_utri=utri, c_rope=rope)


_CACHE = {}


def kernel(**inputs):
    debug = inputs.pop("_debug", None)
    key = debug
    if key not in _CACHE:
        _CACHE[key] = build(debug)
    nc = _CACHE[key]
    consts = _consts()
    x = np.ascontiguousarray(inputs["x"], dtype=np.float32)
    B = x.shape[0]
    shared = {n: np.ascontiguousarray(np.asarray(v, dtype=np.float32)) for n, v in inputs.items() if n != "x"}
    shared.update(consts)
    in_maps = []
    for b in range(B):
        m = dict(shared)
        m["x"] = x[b]
        in_maps.append(m)
    res = run_bass_kernel_spmd(nc, in_maps, core_ids=list(range(B)))
    out = np.stack([np.asarray(r["y"], dtype=np.float32) for r in res.results], axis=0)
    if debug:
        return out, np.stack([np.asarray(r["dbg"]) for r in res.results], axis=0)
    return out
```

```python
import contextlib
import numpy as np
import concourse.bass as bass
import concourse.mybir as mybir
from concourse.bass_utils import run_bass_kernel_spmd

F32 = mybir.dt.float32
BF16 = mybir.dt.bfloat16
AF = mybir.ActivationFunctionType
ALU = mybir.AluOpType
AX = mybir.AxisListType

D = 1024
SEQ = 2048
NMETA = 16
L = SEQ + NMETA
NT = 17
EPS = 1e-6
D_IN = 4368
D_FF = 2816
NCH = D_FF // 128

ENGS = ("pe", "act", "dve", "pool", "sp")
SAME_ENG_DIST = 3


class Buf:
    __slots__ = ("name", "writer", "readers", "psum")

    def __init__(self, name):
        self.name = name
        self.writer = None
        self.readers = []
        self.psum = False


class DmaSem:
    __slots__ = ("name", "count", "last", "handle")

    def __init__(self, name):
        self.name = name
        self.count = 0
        self.last = None
        self.handle = None


class Op:
    __slots__ = ("eng", "fn", "deps", "dsem", "dval", "signal", "idx", "sigval", "name")

    def __init__(self, eng, fn, name):
        self.eng = eng
        self.fn = fn
        self.deps = []
        self.dsem = None
        self.dval = 0
        self.signal = False
        self.idx = 0
        self.sigval = 0
        self.name = name


class K:
    def __init__(self, nc):
        self.nc = nc
        self.ops = {e: [] for e in ENGS}
        self.dsems = []
        self.nbuf = 0

    def buf(self, name=None):
        self.nbuf += 1
        return Buf(name or f"b{self.nbuf}")

    def dsem(self, name=None):
        d = DmaSem(name or f"d{len(self.dsems)}")
        self.dsems.append(d)
        return d

    limit = None
    nops = 0

    def op(self, eng, fn, reads=(), writes=(), dsem=None, name="", extra=(), force=False):
        self.nops += 1
        if self.limit is not None and self.nops > self.limit and not force:
            return None
        o = Op(eng, fn, name)
        lst = self.ops[eng]
        o.idx = len(lst)
        deps = list(extra)
        for b in list(reads) + list(writes):
            if b.writer is not None:
                deps.append(b.writer)
        for b in writes:
            deps.extend(b.readers)
        for b in reads:
            if b.psum:
                deps.extend(r for r in b.readers if r.eng != eng)
        if dsem is not None:
            o.dsem = dsem
            dsem.count += 16
            o.dval = dsem.count
            if dsem.last is not None:
                deps.append(dsem.last)
            dsem.last = o
        seen = set()
        for d in deps:
            if id(d) in seen or d is o:
                continue
            seen.add(id(d))
            if d.dsem is None and d.eng == eng:
                if eng == "pe":
                    continue
                if o.idx - d.idx > SAME_ENG_DIST:
                    continue
            o.deps.append(d)
            if d.dsem is None:
                d.signal = True
        for b in reads:
            b.readers.append(o)
        for b in writes:
            b.writer = o
            b.readers = []
        lst.append(o)
        return o

    def barrier(self):
        lasts = []
        for e in ENGS:
            for o in reversed(self.ops[e]):
                if o.dsem is None and o.fn is not None:
                    lasts.append(o)
                    break
        dl = [d.last for d in self.dsems if d.last is not None]
        for e in ENGS:
            o = Op(e, None, "barrier")
            o.idx = len(self.ops[e])
            for d in lasts:
                if d.eng != e:
                    o.deps.append(d)
                    d.signal = True
            o.deps.extend(dl)
            self.ops[e].append(o)

    def generate(self, final_waits=()):
        nc = self.nc
        with contextlib.ExitStack() as st:
            esem = {e: st.enter_context(nc.semaphore(f"s_{e}")) for e in ENGS}
            for d in self.dsems:
                d.handle = st.enter_context(nc.semaphore(f"dm_{d.name}"))
            for e in ENGS:
                c = 0
                for o in self.ops[e]:
                    if o.dsem is None and o.signal:
                        c += 1
                        o.sigval = c
            block = st.enter_context(nc.Block())

            def gen(ename, eng):
                seen = {}
                for o in self.ops[ename]:
                    need = {}
                    for d in o.deps:
                        if d.dsem is not None:
                            key, val, h = ("d", id(d.dsem)), d.dval, d.dsem.handle
                        else:
                            key, val, h = ("e", d.eng), d.sigval, esem[d.eng]
                        if val > need.get(key, (0, None))[0]:
                            need[key] = (val, h)
                    for key, (val, h) in need.items():
                        if seen.get(key, 0) >= val:
                            continue
                        seen[key] = val
                        eng.wait_ge(h, val)
                    if o.fn is None:
                        continue
                    ins = o.fn(eng)
                    if o.dsem is not None:
                        ins.then_inc(o.dsem.handle, 16)
                    elif o.signal:
                        ins.then_inc(esem[ename], 1)
                if ename == "sp":
                    for d in final_waits:
                        eng.wait_ge(d.handle, d.count)

            @block.tensor
            def _(e):
                gen("pe", e)

            @block.scalar
            def _(e):
                gen("act", e)

            @block.vector
            def _(e):
                gen("dve", e)

            @block.gpsimd
            def _(e):
                gen("pool", e)

            @block.sync
            def _(e):
                gen("sp", e)


class R:
    __slots__ = ("ap", "b")

    def __init__(self, ap, b):
        self.ap = ap
        self.b = b

    def __getitem__(self, idx):
        return self.ap[idx]


def tok0(t):
    return 0 if t == 0 else NMETA + 128 * (t - 1)


def tsz(t):
    return NMETA if t == 0 else 128


def build(debug=None, stop=None, ntiles=NT, limit=None):
    nc = bass.Bass("TRN2", target_bir_lowering=False)
    k = K(nc)
    k.limit = limit

    def din(name, shape):
        return nc.dram_tensor(name, list(shape), F32, kind="ExternalInput").ap()

    x_d = din("x", [SEQ, D])
    meta_d = din("meta_tokens", [NMETA, D])
    mixn_d = din("mix_norm", [1, D])
    win_d = din("w_in", [1, D, D_IN])
    bin_d = din("b_in", [1, D_IN])
    walpha_d = din("w_alpha", [1, 16, 256])
    balpha_d = din("b_alpha", [1, 256])
    sinks_d = din("attn_sinks", [1, 8])
    ghn_d = din("gla_head_norm", [1, 128])
    wpa_d = din("w_proj_attn", [1, 512, D])
    wpg_d = din("w_proj_gla", [1, 512, D])
    wout_d = din("w_out", [1, D, D])
    ffnn_d = din("ffn_norm", [1, D])
    wup_d = din("w_up", [1, D, 2 * D_FF])
    convw_d = din("conv_w", [1, 3, D_FF])
    convb_d = din("conv_b", [1, D_FF])
    wdn_d = din("w_down", [1, D_FF, D])
    fnorm_d = din("final_norm", [D])
    ident_d = din("c_ident", [128, 128])
    masks_d = din("c_masks", [3, 128, 128])
    utri_d = din("c_utri", [128, 128])
    rope_d = din("c_rope", [128, NT, 128])
    y_d = nc.dram_tensor("y", [SEQ, D], F32, kind="ExternalOutput").ap()
    dbg_d = None
    if debug:
        dbg_d = nc.dram_tensor("dbg", [L, D], F32, kind="ExternalOutput").ap()

    SB_BYTES = 212736
    big = nc.alloc_sbuf_tensor("sbig", [128, SB_BYTES // 2], BF16)
    cur = [0]

    def sb(shape, dt, name=None):
        n = 1
        for s in shape[1:]:
            n *= s
        isz = 4 if dt == F32 else 2
        nbytes = (n * isz + 31) // 32 * 32
        off = cur[0]
        cur[0] += nbytes
        assert cur[0] <= SB_BYTES, f"SBUF overflow at {name}: {cur[0]}"
        ap = big[0:shape[0], off // 2: off // 2 + n * isz // 2]
        if dt == F32:
            ap = ap.bitcast(F32)
        if len(shape) == 3:
            ap = ap.rearrange("p (a b) -> p a b", a=shape[1])
        elif len(shape) == 4:
            ap = ap.rearrange("p (a b c) -> p a b c", a=shape[1], b=shape[2])
        return R(ap, k.buf(name))

    banks = []
    for i in range(8):
        t = nc.alloc_psum_tensor(f"ps{i}", [128, 512], F32)
        banks.append((t, k.buf(f"ps{i}")))
        banks[-1][1].psum = True
    pools = {"all": [list(range(8)), 0], "A": [[0, 1], 0], "B1": [[2, 3], 0], "B2e": [[4, 5], 0], "B2o": [[6, 7], 0], "C": [[0, 1, 2, 3], 0], "D": [[4, 5, 6, 7], 0]}
    pool_cur = ["all"]

    class PS:
        __slots__ = ("f", "h", "b")

    def psn(pool=None):
        pl = pools[pool or pool_cur[0]]
        t, b = banks[pl[0][pl[1] % len(pl[0])]]
        pl[1] += 1
        p = PS()
        p.f = t[:, :]
        p.h = t[:, :].bitcast(BF16)
        p.b = b
        return p

    def rolling(streams, can_start):
        idx = {n: 0 for n in streams}
        cur_g = {n: None for n in streams}
        done = {n: 0 for n in streams}
        while True:
            progressed = False
            alive = False
            for n, (pool, facs) in streams.items():
                if cur_g[n] is None:
                    if idx[n] >= len(facs):
                        continue
                    alive = True
                    if not can_start(n, idx[n], done):
                        continue
                    cur_g[n] = facs[idx[n]]()
                    idx[n] += 1
                alive = True
                pool_cur[0] = pool
                try:
                    next(cur_g[n])
                except StopIteration:
                    cur_g[n] = None
                    done[n] += 1
                progressed = True
            if not alive:
                break
            assert progressed, "rolling scheduler deadlock"
        pool_cur[0] = "all"

    def interleave(*gens):
        gens = [g for g in gens if g is not None]
        while gens:
            for g in list(gens):
                try:
                    pool_cur[0] = g[0]
                    next(g[1])
                except StopIteration:
                    gens.remove(g)
        pool_cur[0] = "all"

    H = sb([128, NT, D], F32, "H")
    Hb = [k.buf(f"H{t}") for t in range(NT)]
    IDENT = sb([128, 128], BF16, "ident")
    MASKS = sb([128, 3, 128], BF16, "masks")
    UTRI = sb([128, 128], F32, "utri")
    ONES = sb([128, 128], BF16, "ones")
    NEGH = sb([128, 8], F32, "negh")
    GAINT = sb([128, 2, 8], F32, "gainT")
    ESINK = sb([128, 8], F32, "esink")
    STAT = sb([128, 8], F32, "stat")
    DENS = sb([128, 8], F32, "dens")
    S4 = sb([128, 8], F32, "s4")
    persist_end = cur[0]

    dq = {"n": 0}
    def cdma(eng, out_ap, in_ap, wr, slow=False):
        d = k.dsem(f"c{dq['n']}")
        dq["n"] += 1
        if slow:
            return k.op(eng, lambda e: e.dma_start(out=out_ap, in_=in_ap, allow_slow_non_contiguous=True), writes=wr, dsem=d)
        return k.op(eng, lambda e: e.dma_start(out=out_ap, in_=in_ap), writes=wr, dsem=d)

    cdma("pool", IDENT[:, :], ident_d, [IDENT.b])
    cdma("pool", MASKS[:, :, :], masks_d.rearrange("m p q -> p m q"), [MASKS.b])
    cdma("sp", UTRI[:, :], utri_d, [UTRI.b])
    cdma("sp", GAINT[:, 0, :], mixn_d[0].rearrange("(c p) -> p c", p=128), [GAINT.b], slow=True)
    GAINT2 = k.buf("gaint2")
    cdma("sp", GAINT[:, 1, :], ffnn_d[0].rearrange("(c p) -> p c", p=128), [GAINT2], slow=True)
    cdma("sp", ESINK[:, :], sinks_d.partition_broadcast(128), [ESINK.b])
    k.op("dve", lambda e: e.memset(ONES[:, :], 1.0), writes=[ONES.b])
    k.op("dve", lambda e: e.memset(NEGH[:, :], -0.5), writes=[NEGH.b])
    k.op("act", lambda e: e.activation(out=ESINK[:, :], in_=ESINK[:, :], func=AF.Exp), reads=[ESINK.b], writes=[ESINK.b])

    xds = [k.dsem(f"x{i}") for i in range(4)]
    k.op("sp", lambda e: e.dma_start(out=H[0:NMETA, 0, :], in_=meta_d), writes=[Hb[0]], dsem=xds[0])
    def load_x(t, extra=()):
        k.op("sp", lambda e: e.dma_start(out=H[:, t, :], in_=x_d[(t - 1) * 128: t * 128, :]),
             writes=[Hb[t]], dsem=xds[t % 4], extra=extra)

    for t in range(1, 3):
        load_x(t)

    def early_exit():
        k.barrier()
        dd_ = k.dsem("early")
        for t in range(1, NT):
            k.op("sp", lambda e, t=t: e.dma_start(out=y_d[(t - 1) * 128:t * 128, :], in_=H[:, t, :]), reads=[Hb[t]], dsem=dd_, force=True)
        k.generate(final_waits=[dd_])
        return nc

    if stop == "setup":
        return early_exit()

    def stat(col, n=1):
        return STAT[:, col:col + n]

    act_rsqrt = [False]

    def rms_rstd(src_ap, P, junk_ap, col, rd, wr, nfeat, STAT=STAT):
        ss = STAT[:P, col:col + 1]
        rs = STAT[:P, col + 1:col + 2]
        k.op("act", lambda e: e.activation(out=junk_ap, in_=src_ap, func=AF.Square, accum_out=ss),
             reads=rd, writes=wr + [STAT.b])
        if act_rsqrt[0]:
            k.op("act", lambda e: e.activation(out=rs, in_=ss, func=AF.Ln, scale=1.0 / nfeat, bias=EPS), reads=[STAT.b], writes=[STAT.b])
            k.op("act", lambda e: e.activation(out=rs, in_=rs, func=AF.Exp, scale=-0.5), reads=[STAT.b], writes=[STAT.b])
            return rs
        k.op("dve", lambda e: e.tensor_scalar(out=rs, in0=ss, scalar1=1.0 / nfeat, scalar2=EPS, op0=ALU.mult, op1=ALU.add),
             reads=[STAT.b], writes=[STAT.b])
        k.op("pool", lambda e: e.tensor_tensor(out=rs, in0=rs, in1=NEGH[:P, 0:1], op=ALU.pow),
             reads=[STAT.b, NEGH.b], writes=[STAT.b])
        return rs

    def norm_pre(t, U, ST=STAT):
        P = tsz(t)
        rs = rms_rstd(H[:P, t, :], P, U[:P, :], 0, [Hb[t]], [U.b], D, STAT=ST)
        k.op("dve", lambda e: e.tensor_scalar_mul(out=U[:P, :], in0=H[:P, t, :], scalar1=rs),
             reads=[Hb[t], ST.b], writes=[U.b])

    def norm_T(t, gi, U, dstT_ap, dst_b):
        P = tsz(t)
        ps = psn()
        pv = ps.h.rearrange("p (c q) -> p c q", c=8)
        for c in range(8):
            k.op("pe", lambda e, c=c: e.transpose(out=pv[:, c, :P], in_=U[:P, c * 128:(c + 1) * 128], identity=IDENT[:P, :P]),
                 reads=[U.b, IDENT.b], writes=[ps.b])
        k.op("dve", lambda e: e.tensor_tensor(out=dstT_ap, in0=pv[:, :, :P],
                                              in1=GAINT[:, gi, :].unsqueeze(2).broadcast_to([128, 8, P]), op=ALU.mult),
             reads=[ps.b, GAINT.b, GAINT2], writes=[dst_b])

    def transposes(src, n, P, rd):
        ps = psn()
        pv = ps.h.rearrange("p (c q) -> p c q", c=8)
        for c in range(n):
            k.op("pe", lambda e, c=c: e.transpose(out=pv[:, c, :P], in_=src[:P, c * 128:(c + 1) * 128], identity=IDENT[:P, :P]),
                 reads=rd + [IDENT.b], writes=[ps.b])
        return ps, pv

    YA = sb([128, 4, L], BF16, "YA")
    YG = sb([128, 4, L], BF16, "YG")
    p1_end = cur[0]

    NA = 2320
    wina_off = cur[0]
    WINA = sb([128, 8, NA], BF16, "winA")
    BIASA = sb([1, NA], BF16, "biasA")
    WALPHA = sb([32, 256], BF16, "walpha")
    rope_off = cur[0]
    ROPE = sb([128, NT, 128], F32, "rope")
    SST = sb([128, 2, 128], F32, "S")
    SBF_ = [sb([128, 2, 128], BF16, f"Sbf{i}") for i in range(3)]
    k.op("dve", lambda e: e.memset(SST[:, :, :], 0.0), writes=[SST.b])
    k.op("pool", lambda e: e.memset(SBF_[0][:, :, :], 0.0), writes=[SBF_[0].b])

    u_off = cur[0]
    U_ = [sb([128, D], BF16, f"U{i}") for i in range(2)]
    UT_ = [sb([128, 8, 128], BF16, f"UT{i}") for i in range(2)]
    T1 = sb([128, 640], F32, "T1")
    T2 = sb([128, 640], F32, "T2")
    TH = R(T1.ap[:, 0:512], T1.b)
    QR = sb([128, 512], BF16, "QR")
    KR = sb([128, 128], BF16, "KR")
    GL = sb([128, 16], BF16, "GL")
    QT_ = [sb([128, 4, 128], BF16, f"QT{i}") for i in range(3)]
    KT_ = [sb([128, 128], BF16, f"KT{i}") for i in range(4)]
    VA_ = [sb([128, 2, 66], BF16, f"VA{i}") for i in range(4)]
    GLT_ = [sb([32, 128], BF16, f"GLT{i}") for i in range(3)]
    GQK_ = [sb([128, 512], BF16, f"GQK{i}") for i in range(3)]
    V_ = [sb([128, 512], BF16, f"V{i}") for i in range(3)]
    SG_ = [sb([128, 512], BF16, f"SG{i}") for i in range(3)]
    PT_ = [[sb([128, 4, 128], BF16, f"PT{g}{b}") for b in range(2)] for g in range(2)]
    YATT = sb([128, 512], BF16, "YATT")
    OA = sb([128, 8, 66], F32, "OA")
    SPt_ = [sb([128, 256], F32, f"SP{i}") for i in range(2)]
    ENB_ = [sb([128, 256], F32, f"ENB{i}") for i in range(2)]
    EBT_ = [sb([128, 2, 128], F32, f"EBT{i}") for i in range(2)]
    ENBT_ = [sb([128, 2, 128], F32, f"ENBT{i}") for i in range(2)]
    QET_ = [sb([128, 2, 128], BF16, f"QET{i}") for i in range(2)]
    KET_ = [sb([128, 2, 128], BF16, f"KET{i}") for i in range(2)]
    KE_ = [sb([128, 256], BF16, f"KE{i}") for i in range(2)]
    AT_ = [sb([128, 4, 128], BF16, f"AT{i}") for i in range(2)]
    OSQ_ = [sb([128, 512], F32, f"OSQ{i}") for i in range(1)] * 2
    YGT_ = [sb([128, 512], BF16, f"YGT{i}") for i in range(1)] * 2
    S4_ = [sb([128, 8], F32, f"S4{i}") for i in range(1)] * 2
    sdone = [False] * NT
    for i in range(4):
        k.op("pool", lambda e, i=i: e.memset(VA_[i][:, :, :], 1.0), writes=[VA_[i].b])
    for i in range(3):
        k.op("pool", lambda e, i=i: e.memset(GLT_[i][:, :], 1.0), writes=[GLT_[i].b])

    segs = [(0, 0, 512), (512, 512, 256), (768, 2304, 16), (784, 768, 512), (1296, 1280, 512), (1808, 1792, 512)]
    WAb = [k.buf(f"winA{i}") for i in range(5)]
    WAb1b = k.buf("winA1b")
    seg_buf = [WAb[0], WAb[1], WAb1b, WAb[2], WAb[3], WAb[4]]
    wds = [k.dsem(f"w{i}") for i in range(6)]
    BIASAb = [k.buf(f"biasA{i}") for i in range(6)]
    bias_tok = {0: [BIASAb[0]], 512: [BIASAb[1], BIASAb[2]], 784: [BIASAb[3]], 1296: [BIASAb[4]], 1808: [BIASAb[5]]}
    win_v = win_d[0].rearrange("(kc p) c -> p kc c", p=128)
    seg_ops = []

    def load_seg(i):
        d0, s0, n = segs[i]
        seg_ops.append(k.op("pool", lambda e: e.dma_start(out=WINA[:, :, d0:d0 + n], in_=win_v[:, :, s0:s0 + n]),
                            writes=[seg_buf[i]], dsem=wds[i]))
        cdma("pool", BIASA[0:1, d0:d0 + n], bin_d[0:1, s0:s0 + n], [BIASAb[i]])

    load_seg(0)
    WALPHA2 = k.buf("walpha_b")
    cdma("pool", WALPHA[0:16, :], walpha_d[0], [WALPHA.b])
    cdma("pool", WALPHA[16:17, :], balpha_d[0:1, :], [WALPHA2])
    cdma("sp", ROPE[:, :, :], rope_d, [ROPE.b])

    def proj_group(UT, P, W, kc_n, c0, n, wb, bias):
        ps = psn()
        for kc in range(kc_n):
            k.op("pe", lambda e, kc=kc: e.matmul(ps.f[:P, :n], lhsT=UT[:, kc, :P], rhs=W[:, kc, c0:c0 + n],
                                               start=(kc == 0), stop=(bias is None and kc == kc_n - 1)),
                 reads=[UT.b] + (wb if isinstance(wb, list) else [wb]), writes=[ps.b])
        if bias is not None:
            btok = bias_tok[c0] if bias is BIASA else [bias.b]
            k.op("pe", lambda e: e.matmul(ps.f[:P, :n], lhsT=ONES[0:1, :P], rhs=bias[0:1, c0:c0 + n], start=False, stop=True),
                 reads=[ONES.b] + btok, writes=[ps.b])
        return ps

    def stage_a(t):
        P = tsz(t)
        U, UT = U_[t % 2], UT_[t % 2]
        QT, KT, VA, GLT, GQK, V, SG = QT_[t % 3], KT_[t % 4], VA_[t % 4], GLT_[t % 3], GQK_[t % 3], V_[t % 3], SG_[t % 3]
        if t + 1 < ntiles:
            norm_pre(t + 1, U_[(t + 1) % 2])
        ps = proj_group(UT, P, WINA, 8, 0, 512, WAb[0], BIASA)
        yield
        qin = ps.f[:P, 0:512].rearrange("p (g i d) -> p i g d", g=2, i=4)
        qin5 = ps.f[:P, 0:512].rearrange("p (g i w d) -> p i g w d", g=2, i=4, w=2)
        t1v = T1[:P, 0:512].rearrange("p (i g d) -> p i g d", i=4, g=2)
        t2v = T2[:P, 0:512].rearrange("p (i g w d) -> p i g w d", i=4, g=2, w=2)
        cosf = ROPE[:P, t, 0:64]
        sin = ROPE[:P, t, 64:96]
        nsin = ROPE[:P, t, 96:128]
        k.op("dve", lambda e: e.tensor_tensor(out=t1v, in0=qin, in1=cosf.unsqueeze(1).unsqueeze(1).broadcast_to([P, 4, 2, 64]), op=ALU.mult),
             reads=[ps.b, ROPE.b], writes=[T1.b])
        k.op("dve", lambda e: e.tensor_tensor(out=t2v[:, :, :, 0, :], in0=qin5[:, :, :, 1, :],
                                              in1=nsin.unsqueeze(1).unsqueeze(1).broadcast_to([P, 4, 2, 32]), op=ALU.mult),
             reads=[ps.b, ROPE.b], writes=[T2.b])
        k.op("dve", lambda e: e.tensor_tensor(out=t2v[:, :, :, 1, :], in0=qin5[:, :, :, 0, :],
                                              in1=sin.unsqueeze(1).unsqueeze(1).broadcast_to([P, 4, 2, 32]), op=ALU.mult),
             reads=[ps.b, ROPE.b], writes=[T2.b])
        ps1 = proj_group(UT, P, WINA, 8, 512, 272, [WAb[1], WAb1b], BIASA)
        yield
        kin = ps1.f[:P, 0:128].rearrange("p (g d) -> p g d", g=2)
        kin4 = ps1.f[:P, 0:128].rearrange("p (g w d) -> p g w d", g=2, w=2)
        t1k = T1[:P, 512:640].rearrange("p (g d) -> p g d", g=2)
        t2k = T2[:P, 512:640].rearrange("p (g w d) -> p g w d", g=2, w=2)
        k.op("dve", lambda e: e.tensor_tensor(out=t1k, in0=kin, in1=cosf.unsqueeze(1).broadcast_to([P, 2, 64]), op=ALU.mult),
             reads=[ps1.b, ROPE.b], writes=[T1.b])
        k.op("dve", lambda e: e.tensor_tensor(out=t2k[:, :, 0, :], in0=kin4[:, :, 1, :], in1=nsin.unsqueeze(1).broadcast_to([P, 2, 32]), op=ALU.mult),
             reads=[ps1.b, ROPE.b], writes=[T2.b])
        k.op("dve", lambda e: e.tensor_tensor(out=t2k[:, :, 1, :], in0=kin4[:, :, 0, :], in1=sin.unsqueeze(1).broadcast_to([P, 2, 32]), op=ALU.mult),
             reads=[ps1.b, ROPE.b], writes=[T2.b])
        k.op("dve", lambda e: e.tensor_copy(out=VA[:P, :, 0:64], in_=ps1.f[:P, 128:256].rearrange("p (g d) -> p g d", g=2)),
             reads=[ps1.b], writes=[VA.b])
        k.op("dve", lambda e: e.tensor_copy(out=GL[:P, :], in_=ps1.f[:P, 256:272]), reads=[ps1.b], writes=[GL.b])
        ps2 = proj_group(UT, P, WINA, 8, 784, 512, WAb[2], BIASA)
        yield
        k.op("pool", lambda e: e.tensor_tensor(out=QR[:P, :], in0=T1[:P, 0:512], in1=T2[:P, 0:512], op=ALU.add),
             reads=[T1.b, T2.b], writes=[QR.b])
        k.op("pool", lambda e: e.tensor_tensor(out=KR[:P, :], in0=T1[:P, 512:640], in1=T2[:P, 512:640], op=ALU.add),
             reads=[T1.b, T2.b], writes=[KR.b])
        k.op("act", lambda e: e.copy(out=GQK[:P, :], in_=ps2.f[:P, :]), reads=[ps2.b], writes=[GQK.b])
        ps3 = proj_group(UT, P, WINA, 8, 1296, 512, WAb[3], BIASA)
        yield
        k.op("act", lambda e: e.copy(out=V[:P, :], in_=ps3.f[:P, :]), reads=[ps3.b], writes=[V.b])
        ps4 = proj_group(UT, P, WINA, 8, 1808, 512, WAb[4], BIASA)
        yield
        k.op("act", lambda e: e.activation(out=TH[:P, :], in_=ps4.f[:P, :], func=AF.Exp, scale=-1.0), reads=[ps4.b], writes=[TH.b])
        pst = psn()
        ptv = pst.h.rearrange("p (c q) -> p c q", c=8)
        for c in range(4):
            k.op("pe", lambda e, c=c: e.transpose(out=ptv[:, c, :P], in_=QR[:P, c * 128:(c + 1) * 128], identity=IDENT[:P, :P]),
                 reads=[QR.b, IDENT.b], writes=[pst.b])
        k.op("pe", lambda e: e.transpose(out=ptv[:, 4, :P], in_=KR[:P, :], identity=IDENT[:P, :P]),
             reads=[KR.b, IDENT.b], writes=[pst.b])
        k.op("pe", lambda e: e.transpose(out=ptv[0:16, 5, :P], in_=GL[:P, :], identity=IDENT[:P, :P]),
             reads=[GL.b, IDENT.b], writes=[pst.b])
        yield
        k.op("dve", lambda e: e.tensor_scalar_add(out=TH[:P, :], in0=TH[:P, :], scalar1=1.0), reads=[TH.b], writes=[TH.b])
        k.op("dve", lambda e: e.reciprocal(out=TH[:P, :], in_=TH[:P, :]), reads=[TH.b], writes=[TH.b])
        k.op("dve", lambda e: e.tensor_tensor(out=SG[:P, :], in0=TH[:P, :], in1=ps4.f[:P, :], op=ALU.mult),
             reads=[TH.b, ps4.b], writes=[SG.b])
        k.op("act", lambda e: e.copy(out=QT[:, :, :P], in_=ptv[:, 0:4, :P]), reads=[pst.b], writes=[QT.b])
        k.op("act", lambda e: e.copy(out=KT[:, :P], in_=ptv[:, 4, :P]), reads=[pst.b], writes=[KT.b])
        k.op("act", lambda e: e.copy(out=GLT[0:16, :P], in_=ptv[0:16, 5, :P]), reads=[pst.b], writes=[GLT.b])
        if t + 1 < ntiles:
            P1 = tsz(t + 1)
            U1, UT1 = U_[(t + 1) % 2], UT_[(t + 1) % 2]
            psu = psn()
            puv = psu.h.rearrange("p (c q) -> p c q", c=8)
            for c in range(8):
                k.op("pe", lambda e, c=c: e.transpose(out=puv[:, c, :P1], in_=U1[:P1, c * 128:(c + 1) * 128], identity=IDENT[:P1, :P1]),
                     reads=[U1.b, IDENT.b], writes=[psu.b])
            yield
            k.op("dve", lambda e: e.tensor_tensor(out=UT1[:, :, :P1], in0=puv[:, :, :P1],
                                                  in1=GAINT[:, 0, :].unsqueeze(2).broadcast_to([128, 8, P1]), op=ALU.mult),
                 reads=[psu.b, GAINT.b, GAINT2], writes=[UT1.b])
        yield

    def stage_att(t):
        P = tsz(t)
        QT = QT_[t % 3]
        t0 = tok0(t)
        blocks = ([] if t == 0 else [(t - 1, 0)]) + [(t, 1)]
        combos = [(g, tk, kb) for g in range(2) for (tk, kb) in blocks]
        pss_l = []
        def emit_score(c):
            g, tk, kb = combos[c]
            Pk = tsz(tk)
            pss = psn()
            sv = pss.f[:Pk, 0:4 * P].rearrange("p (i q) -> p i q", i=4)
            KTk = KT_[tk % 4]
            k.op("pe", lambda e: e.matmul(sv, lhsT=KTk[g * 64:(g + 1) * 64, :Pk], rhs=QT[g * 64:(g + 1) * 64, :, :P], start=True, stop=True),
                 reads=[KTk.b, QT.b], writes=[pss.b])
            pss_l.append((pss, sv, Pk, PT_[g][kb], kb))
        def emit_exp(c):
            pss, sv, Pk, PT, kb = pss_l[c]
            k.op("act", lambda e: e.activation(out=PT[:Pk, :, :P], in_=sv, func=AF.Exp, scale=0.125), reads=[pss.b], writes=[PT.b])
        def emit_mask(c):
            pss, sv, Pk, PT, kb = pss_l[c]
            if kb == 1:
                mk = MASKS[:Pk, 0, :P]
            else:
                mk = MASKS[:Pk, 2, :P] if t == 1 else MASKS[:Pk, 1, :P]
            k.op("dve", lambda e: e.tensor_tensor(out=PT[:Pk, :, :P], in0=PT[:Pk, :, :P], in1=mk.unsqueeze(1).broadcast_to([Pk, 4, P]), op=ALU.mult),
                 reads=[PT.b, MASKS.b], writes=[PT.b])
        nc_ = len(combos)
        for c0_ in range(0, nc_, 2):
            cs = list(range(c0_, min(c0_ + 2, nc_)))
            if c0_ >= 2:
                for c in range(c0_ - 2, c0_):
                    emit_mask(c)
            for c in cs:
                emit_score(c)
            yield
            for c in cs:
                emit_exp(c)
            yield
        for c in range(max(0, nc_ - 2), nc_):
            emit_mask(c)
        yield
        pso = [psn(), psn()]
        for g in range(2):
            for i in range(4):
                for bi, (tk, kb) in enumerate(blocks):
                    Pk = tsz(tk)
                    PT = PT_[g][kb]
                    VAk = VA_[tk % 4]
                    k.op("pe", lambda e, g=g, i=i, Pk=Pk, PT=PT, VAk=VAk, bi=bi: e.matmul(
                        pso[g].f[:P, i * 128:i * 128 + 66], lhsT=PT[:Pk, i, :P], rhs=VAk[:Pk, g, :],
                        start=(bi == 0), stop=(bi == len(blocks) - 1)),
                        reads=[PT.b, VAk.b], writes=[pso[g].b])
        yield
        for g in range(2):
            k.op("act", lambda e, g=g: e.copy(out=OA[:P, g * 4:(g + 1) * 4, :], in_=pso[g].f[:P, :].rearrange("p (i c) -> p i c", i=4)[:, :, 0:66]),
                 reads=[pso[g].b], writes=[OA.b])
        yield
        DEN = DENS[:P, 0:8]
        oav = OA[:P, :, :]
        k.op("dve", lambda e: e.tensor_tensor(out=DEN, in0=oav[:, :, 64], in1=ESINK[:P, :], op=ALU.add),
             reads=[OA.b, ESINK.b], writes=[DENS.b])
        k.op("dve", lambda e: e.reciprocal(out=DEN, in_=DEN), reads=[DENS.b], writes=[DENS.b])
        k.op("dve", lambda e: e.tensor_tensor(
            out=YATT[:P, :].rearrange("p (h d) -> p h d", h=8), in0=oav[:, :, 0:64],
            in1=DEN.unsqueeze(2).broadcast_to([P, 8, 64]), op=ALU.mult),
            reads=[OA.b, DENS.b], writes=[YATT.b])
        yield
        psy, pyv = transposes(YATT, 4, P, [YATT.b])
        yield
        k.op("act", lambda e: e.copy(out=YA[:, :, t0:t0 + P], in_=pyv[:, 0:4, :P]), reads=[psy.b], writes=[YA.b])
        yield

    def stage_gla(t):
        P = tsz(t)
        GLT, GQK, V, SG = GLT_[t % 3], GQK_[t % 3], V_[t % 3], SG_[t % 3]
        j2 = t % 2
        SPt, ENB, EBT, ENBT, QET, KET, KE, AT, OSQ, YGT, S4 = (SPt_[j2], ENB_[j2], EBT_[j2], ENBT_[j2], QET_[j2], KET_[j2], KE_[j2],
                                                                 AT_[j2], OSQ_[j2], YGT_[j2], S4_[j2])
        t0 = tok0(t)
        psz = psn()
        k.op("pe", lambda e: e.matmul(psz.f[:P, 0:256], lhsT=GLT[0:17, :P], rhs=WALPHA[0:17, :], start=True, stop=True),
             reads=[GLT.b, WALPHA.b, WALPHA2], writes=[psz.b])
        psq, pqv = transposes(GQK, 4, P, [GQK.b])
        yield
        k.op("act", lambda e: e.activation(out=SPt[:P, :], in_=psz.f[:P, 0:256], func=AF.Exp, scale=-1.0), reads=[psz.b], writes=[SPt.b])
        k.op("act", lambda e: e.activation(out=SPt[:P, :], in_=SPt[:P, :], func=AF.Ln, bias=1.0), reads=[SPt.b], writes=[SPt.b])
        yield
        psb = psn()
        bTv = psb.f[:, 256:512].rearrange("p (c q) -> p c q", c=2)
        k.op("pe", lambda e: e.matmul(psb.f[:P, 0:256], lhsT=UTRI[:P, :P], rhs=SPt[:P, :], start=True, stop=True),
             reads=[UTRI.b, SPt.b], writes=[psb.b])
        for pr in range(2):
            k.op("pe", lambda e, pr=pr: e.matmul(bTv[:, pr, :P], lhsT=SPt[:P, pr * 128:(pr + 1) * 128], rhs=UTRI[:P, :P], start=True, stop=True),
                 reads=[UTRI.b, SPt.b], writes=[psb.b])
        yield
        k.op("act", lambda e: e.activation(out=ENB[:P, :], in_=psb.f[:P, 0:256], func=AF.Exp, scale=-1.0), reads=[psb.b], writes=[ENB.b])
        k.op("act", lambda e: e.activation(out=EBT[:, :, :P], in_=bTv[:, :, :P], func=AF.Exp), reads=[psb.b], writes=[EBT.b])
        k.op("act", lambda e: e.activation(out=ENBT[:, :, :P], in_=bTv[:, :, :P], func=AF.Exp, scale=-1.0), reads=[psb.b], writes=[ENBT.b])
        yield
        k.op("pool", lambda e: e.tensor_tensor(out=KE[:P, :], in0=GQK[:P, 256:512], in1=ENB[:P, :], op=ALU.mult),
             reads=[GQK.b, ENB.b], writes=[KE.b])
        k.op("dve", lambda e: e.scalar_tensor_tensor(out=QET[:, :, :P], in0=pqv[:, 0:2, :P], scalar=0.125, in1=EBT[:, :, :P], op0=ALU.mult, op1=ALU.mult),
             reads=[psq.b, EBT.b], writes=[QET.b])
        k.op("dve", lambda e: e.tensor_tensor(out=KET[:, :, :P], in0=pqv[:, 2:4, :P], in1=ENBT[:, :, :P], op=ALU.mult),
             reads=[psq.b, ENBT.b], writes=[KET.b])
        yield
        while t > 0 and not sdone[t - 1]:
            yield
        if t < NT - 1:
            psd = psn()
            dsv = psd.f[:, :].rearrange("p (h e) -> p h e", h=4)
            for h in range(4):
                k.op("pe", lambda e, h=h: e.matmul(dsv[:, h, :], lhsT=KE[:P, (h // 2) * 128:(h // 2 + 1) * 128], rhs=V[:P, h * 128:(h + 1) * 128], start=True, stop=True),
                     reads=[KE.b, V.b], writes=[psd.b])
            yield
            dsv2 = psd.f[:, :].rearrange("p (c w e) -> p w c e", c=2, w=2)
            SBn = SBF_[(t + 1) % 3]
            for par in range(2):
                r0 = par * 64
                k.op("dve", lambda e, par=par, r0=r0: e.tensor_tensor(out=SST[r0:r0 + 64, :, :], in0=dsv2[r0:r0 + 64, par, :, :], in1=SST[r0:r0 + 64, :, :], op=ALU.add),
                     reads=[psd.b, SST.b], writes=[SST.b])
            for par in range(2):
                r0 = par * 64
                k.op("dve", lambda e, r0=r0: e.tensor_tensor(out=SST[r0:r0 + 64, :, :], in0=SST[r0:r0 + 64, :, :],
                                                          in1=EBT[r0:r0 + 64, :, P - 1:P].broadcast_to([64, 2, 128]), op=ALU.mult),
                     reads=[SST.b, EBT.b], writes=[SST.b])
            k.op("dve", lambda e: e.tensor_copy(out=SBn[:, :, :], in_=SST[:, :, :]), reads=[SST.b], writes=[SBn.b])
        sdone[t] = True
        yield
        psa = [psn(), psn()]
        for h in range(4):
            r0 = (h % 2) * 64
            avh = psa[h % 2].f[:P, 0:2 * P].rearrange("p (c q) -> p c q", c=2)
            k.op("pe", lambda e, h=h, r0=r0, avh=avh: e.matmul(avh[:, h // 2, :], lhsT=KET[r0:r0 + 64, h // 2, :P], rhs=QET[r0:r0 + 64, h // 2, :P], start=True, stop=True),
                 reads=[KET.b, QET.b], writes=[psa[h % 2].b])
        yield
        atv = AT[:P, :, :P].rearrange("p (c w) q -> p w c q", w=2)
        for par in range(2):
            avp = psa[par].f[:P, 0:2 * P].rearrange("p (c q) -> p c q", c=2)
            k.op("dve", lambda e, par=par, avp=avp: e.tensor_tensor(out=atv[:, par, :, :], in0=avp, in1=MASKS[:P, 0, :P].unsqueeze(1).broadcast_to([P, 2, P]), op=ALU.mult),
                 reads=[psa[par].b, MASKS.b], writes=[AT.b])
        yield
        SBc = SBF_[t % 3]
        psg = psn()
        ogv = psg.f[:P, :].rearrange("p (h e) -> p h e", h=4)
        for h in range(4):
            r0 = (h % 2) * 64
            if t > 0:
                k.op("pe", lambda e, h=h, r0=r0: e.matmul(ogv[:, h, :], lhsT=QET[r0:r0 + 64, h // 2, :P], rhs=SBc[r0:r0 + 64, h // 2, :], start=True, stop=False),
                     reads=[QET.b, SBc.b], writes=[psg.b])
            k.op("pe", lambda e, h=h: e.matmul(ogv[:, h, :], lhsT=AT[:P, h, :P], rhs=V[:P, h * 128:(h + 1) * 128], start=(t == 0), stop=True),
                 reads=[AT.b, V.b], writes=[psg.b])
        yield
        SS4 = S4[:P, 0:4]
        RS4 = S4[:P, 4:8]
        k.op("act", lambda e: e.activation(out=OSQ[:P, :], in_=psg.f[:P, :], func=AF.Square), reads=[psg.b], writes=[OSQ.b])
        yield
        k.op("dve", lambda e: e.reduce_sum(out=SS4, in_=OSQ[:P, :].rearrange("p (h e) -> p h e", h=4), axis=AX.X),
             reads=[OSQ.b], writes=[S4.b])
        yield
        k.op("act", lambda e: e.activation(out=RS4, in_=SS4, func=AF.Ln, scale=1.0 / 128, bias=EPS), reads=[S4.b], writes=[S4.b])
        k.op("act", lambda e: e.activation(out=RS4, in_=RS4, func=AF.Exp, scale=-0.5), reads=[S4.b], writes=[S4.b])
        yield
        k.op("dve", lambda e: e.tensor_tensor(out=OSQ[:P, :].rearrange("p (h e) -> p h e", h=4), in0=ogv,
                                              in1=RS4.unsqueeze(2).broadcast_to([P, 4, 128]), op=ALU.mult),
             reads=[psg.b, S4.b, OSQ.b], writes=[OSQ.b])
        yield
        k.op("pool", lambda e: e.tensor_tensor(out=YGT[:P, :], in0=OSQ[:P, :], in1=SG[:P, :], op=ALU.mult),
             reads=[OSQ.b, SG.b], writes=[YGT.b])
        yield
        psy2, pyv2 = transposes(YGT, 4, P, [YGT.b])
        yield
        yield
        k.op("act", lambda e: e.copy(out=YG[:, :, t0:t0 + P], in_=pyv2[:, 0:4, :P]), reads=[psy2.b], writes=[YG.b])
        yield

    act_rsqrt[0] = True
    norm_pre(0, U_[0])
    for i in range(1, len(segs)):
        load_seg(i)
    for t in range(3, NT):
        load_x(t, extra=[seg_ops[-1]])
    pool_cur[0] = "A"
    norm_T(0, 0, U_[0], UT_[0][:, :, :tsz(0)], UT_[0].b)

    def can_start(n, i, done):
        if n == "A":
            if i < 3:
                return True
            m = i - 3
            return done["B1"] >= i - 2 and done["B2e"] >= m // 2 + 1 and done["B2o"] >= (m + 1) // 2
        if n == "B1":
            return done["A"] >= i + 1
        tile = 2 * i if n == "B2e" else 2 * i + 1
        return done["A"] >= tile + 1

    def sb_at(off, shape, dt, name):
        save = cur[0]
        cur[0] = off
        r = sb(shape, dt, name)
        cur[0] = save
        return r

    WINB = sb_at(wina_off, [128, 8, 2048], BF16, "winB")
    WPA = sb_at(u_off, [128, 4, D], BF16, "wpa")
    WPG = sb_at(rope_off, [128, 4, D], BF16, "wpg")
    WBb = [k.buf(f"winB{i}") for i in range(4)]
    wbd = [k.dsem(f"wb{i}") for i in range(4)]
    wina_tokens = WAb + [WAb1b]

    def prefetch_gen():
        for i in range(4):
            k.op("pool", lambda e, i=i: e.dma_start(out=WINB[:, :, i * 512:(i + 1) * 512], in_=win_v[:, :, 2320 + i * 512: 2320 + (i + 1) * 512]),
                 writes=[WBb[i]] + wina_tokens, dsem=wbd[i])
        cdma("pool", WPA[:, :, :], wpa_d[0].rearrange("(c p) n -> p c n", p=128), [WPA.b, U_[0].b, U_[1].b, UT_[0].b, UT_[1].b])
        cdma("pool", WPG[:, :, :], wpg_d[0].rearrange("(c p) n -> p c n", p=128), [WPG.b, ROPE.b])
        yield

    def can_start_w(n, i, done):
        if n == "W":
            return done["A"] >= ntiles
        return can_start(n, i, done)

    rolling({"A": ("A", [(lambda t=t: stage_a(t)) for t in range(ntiles)]),
             "B1": ("B1", [(lambda t=t: stage_att(t)) for t in range(ntiles)]),
             "B2e": ("B2e", [(lambda t=t: stage_gla(t)) for t in range(0, ntiles, 2)]),
             "B2o": ("B2o", [(lambda t=t: stage_gla(t)) for t in range(1, ntiles, 2)]),
             "W": ("A", [prefetch_gen])}, can_start_w)

    act_rsqrt[0] = False
    if stop == "p1a":
        return early_exit()
    k.barrier()

    cur[0] = wina_off + 32768
    BIASB = sb([1, 2048], BF16, "biasB")
    GHN = sb([128, 1], F32, "ghn")
    M1_ = [sb([128, 512], F32, f"M1{i}") for i in range(2)]
    assert cur[0] <= rope_off
    cur[0] = u_off + 8192
    WOUT = sb([128, 8, D], BF16, "wout")
    cdma("pool", BIASB[0:1, :], bin_d[0:1, 2320:4368], [BIASB.b])
    cdma("pool", WOUT[:, :, :], wout_d[0].rearrange("(c p) n -> p c n", p=128), [WOUT.b])
    cdma("sp", GHN[:, :], ghn_d[0].rearrange("(p o) -> p o", o=1), [GHN.b], slow=True)
    k.op("dve", lambda e: e.tensor_scalar_mul(out=WPG[:, :, :], in0=WPG[:, :, :], scalar1=GHN[:, 0:1]),
         reads=[WPG.b, GHN.b], writes=[WPG.b])

    UB_ = [sb([128, D], BF16, f"UB{i}") for i in range(2)]
    UTB_ = [sb([128, 8, 128], BF16, f"UTB{i}") for i in range(2)]
    TA_ = [sb([128, 2048], BF16, f"TA{i}") for i in range(2)]
    M2_ = [sb([128, 512], F32, f"M2{i}") for i in range(2)]
    MIX_ = [sb([128, D], BF16, f"MIX{i}") for i in range(2)]
    MIXT_ = [sb([128, 8, 128], BF16, f"MIXT{i}") for i in range(2)]

    def stage_gates(t):
        P = tsz(t)
        U, UT, TA = UB_[t % 2], UTB_[t % 2], TA_[t % 2]
        norm_T(t, 0, U, UT[:, :, :P], UT.b)
        yield
        if t + 1 < NT:
            norm_pre(t + 1, UB_[(t + 1) % 2])
        for i in range(4):
            ps = proj_group(UT, P, WINB, 8, i * 512, 512, WBb[i], BIASB)
            k.op("act", lambda e, ps=ps, i=i: e.activation(out=TA[:P, i * 512:(i + 1) * 512], in_=ps.f[:P, :], func=AF.Tanh, scale=0.5),
                 reads=[ps.b], writes=[TA.b])
            yield

    def stage_merge(t):
        P = tsz(t)
        TA, MIX, MIXT = TA_[t % 2], MIX_[t % 2], MIXT_[t % 2]
        t0 = tok0(t)
        for hf in range(2):
            M1, M2 = M1_[hf], M2_[hf]
            pa = psn()
            for c in range(4):
                k.op("pe", lambda e, c=c, pa=pa, hf=hf: e.matmul(pa.f[:P, :], lhsT=YA[:, c, t0:t0 + P], rhs=WPA[:, c, hf * 512:(hf + 1) * 512], start=(c == 0), stop=(c == 3)),
                     reads=[YA.b, WPA.b], writes=[pa.b])
            pg = psn()
            for c in range(4):
                k.op("pe", lambda e, c=c, pg=pg, hf=hf: e.matmul(pg.f[:P, :], lhsT=YG[:, c, t0:t0 + P], rhs=WPG[:, c, hf * 512:(hf + 1) * 512], start=(c == 0), stop=(c == 3)),
                     reads=[YG.b, WPG.b], writes=[pg.b])
            yield
            k.op("dve", lambda e, pa=pa, hf=hf, M1=M1: e.scalar_tensor_tensor(out=M1[:P, :], in0=TA[:P, hf * 512:(hf + 1) * 512], scalar=1.0, in1=pa.f[:P, :], op0=ALU.add, op1=ALU.mult),
                 reads=[TA.b, pa.b], writes=[M1.b])
            k.op("dve", lambda e, pg=pg, hf=hf, M2=M2: e.scalar_tensor_tensor(out=M2[:P, :], in0=TA[:P, 1024 + hf * 512:1024 + (hf + 1) * 512], scalar=1.0, in1=pg.f[:P, :], op0=ALU.add, op1=ALU.mult),
                 reads=[TA.b, pg.b], writes=[M2.b])
            k.op("pool", lambda e, hf=hf, M1=M1, M2=M2: e.tensor_tensor(out=MIX[:P, hf * 512:(hf + 1) * 512], in0=M1[:P, :], in1=M2[:P, :], op=ALU.add),
                 reads=[M1.b, M2.b], writes=[MIX.b])
            yield
        yield
        psm, pmv = transposes(MIX, 8, P, [MIX.b])
        k.op("act", lambda e: e.copy(out=MIXT[:, :, :P], in_=pmv[:, :, :P]), reads=[psm.b], writes=[MIXT.b])
        yield
        for hf in range(2):
            po = psn()
            for c in range(8):
                k.op("pe", lambda e, c=c, po=po, hf=hf: e.matmul(po.f[:P, :], lhsT=MIXT[:, c, :P], rhs=WOUT[:, c, hf * 512:(hf + 1) * 512], start=(c == 0), stop=(c == 7)),
                     reads=[MIXT.b, WOUT.b], writes=[po.b])
            k.op("dve", lambda e, po=po, hf=hf: e.scalar_tensor_tensor(out=H[:P, t, hf * 512:(hf + 1) * 512], in0=po.f[:P, :], scalar=0.5,
                                                                       in1=H[:P, t, hf * 512:(hf + 1) * 512], op0=ALU.mult, op1=ALU.add),
                 reads=[po.b, Hb[t]], writes=[Hb[t]])
            yield

    norm_pre(0, UB_[0])
    interleave(("C", stage_gates(0)))
    for t in range(NT):
        interleave(("C", stage_gates(t + 1)) if t + 1 < NT else None, ("D", stage_merge(t)))

    if stop == "p1b":
        return early_exit()
    k.barrier()

    if debug == "hmid":
        dd = k.dsem("dbg")
        k.op("sp", lambda e: e.dma_start(out=dbg_d[0:NMETA, :], in_=H[0:NMETA, 0, :]), reads=[Hb[0]], dsem=dd)
        for t in range(1, NT):
            k.op("sp", lambda e, t=t: e.dma_start(out=dbg_d[tok0(t):tok0(t) + 128, :], in_=H[:, t, :]), reads=[Hb[t]], dsem=dd)
        k.barrier()

    cur[0] = persist_end
    U2T = sb([128, 8, L], BF16, "u2t")
    U2_ = [sb([128, D], BF16, f"U2{i}") for i in range(2)]
    FNORM = sb([128, D], F32, "fnorm")
    STF = sb([128, 8], F32, "stf")
    CWB = sb([128, NCH, 4], F32, "cwb")
    HALO = sb([128, NCH, 2], F32, "halo")
    IDF = sb([4, 4], F32, "idf")
    quarters = [(0, 6), (6, 6), (12, 5), (17, 5)]
    WA_ = [sb([128, 8, 768], BF16, f"WA{i}") for i in range(2)]
    WV_ = [sb([128, 8, 768], BF16, f"WV{i}") for i in range(2)]
    WD_ = [sb([128, 6, D], BF16, f"WD{i}") for i in range(2)]
    qds = [[k.dsem(f"q{i}{j}") for j in range(3)] for i in range(2)]
    ASB_ = [sb([128, 514], F32, f"ASB{i}") for i in range(2)]
    Y_ = [sb([128, 512], F32, f"Yc{i}") for i in range(2)]
    mt_off = cur[0]
    MT_ = [sb([128, 6, 512], BF16, f"MT{i}") for i in range(2)]
    out_off = cur[0]
    OUT = sb([128, D], F32, "OUT")
    U2x = [R(big[0:128, out_off // 2 + j * D: out_off // 2 + (j + 1) * D], k.buf(f"u2x{j}")) for j in range(2)]
    U2s = U2_ + U2x
    STP = [sb([128, 8], F32, f"stp{j}") for j in range(4)]
    cw_ap = big[0:4, mt_off // 2: mt_off // 2 + D_FF * 2].bitcast(F32)
    assert D_FF * 4 <= 2 * 6 * 512 * 2
    MTB = [MT_[0].b, MT_[1].b]

    wup_v = wup_d[0].rearrange("(kc p) c -> p kc c", p=128)
    wdn_v = wdn_d[0].rearrange("(c p) n -> p c n", p=128)

    def load_quarter(qi):
        c0, nq = quarters[qi]
        sl = qi % 2
        k.op("pool", lambda e: e.dma_start(out=WA_[sl][:, :, 0:nq * 128], in_=wup_v[:, :, c0 * 128:(c0 + nq) * 128]), writes=[WA_[sl].b], dsem=qds[sl][0])
        k.op("pool", lambda e: e.dma_start(out=WV_[sl][:, :, 0:nq * 128], in_=wup_v[:, :, D_FF + c0 * 128:D_FF + (c0 + nq) * 128]), writes=[WV_[sl].b], dsem=qds[sl][1])
        k.op("pool", lambda e: e.dma_start(out=WD_[sl][:, 0:nq, :], in_=wdn_v[:, c0:c0 + nq, :]), writes=[WD_[sl].b], dsem=qds[sl][2])

    load_quarter(0)
    load_quarter(1)
    cdma("sp", FNORM[:, :], fnorm_d.rearrange("(o d) -> o d", o=1).partition_broadcast(128), [FNORM.b])
    cdma("sp", cw_ap[0:3, :], convw_d[0], MTB)
    cdma("sp", cw_ap[3:4, :], convb_d[0:1, :], MTB)
    cdma("sp", IDF[:, :], ident_d[0:4, 0:4], [IDF.b])
    k.op("dve", lambda e: e.memset(HALO[:, :, :], 0.0), writes=[HALO.b])
    psc = psn()
    cwv = psc.f[:, 0:NCH * 4].rearrange("p (c f) -> p c f", f=4)
    for c in range(NCH):
        k.op("pe", lambda e, c=c: e.transpose(out=cwv[:, c, :], in_=cw_ap[0:4, c * 128:(c + 1) * 128], identity=IDF[0:4, 0:4]),
             reads=MTB + [IDF.b], writes=[psc.b])
    k.op("dve", lambda e: e.tensor_copy(out=CWB[:, :, :], in_=cwv), reads=[psc.b], writes=[CWB.b])

    act_rsqrt[0] = True
    for j in range(min(3, NT)):
        norm_pre(j, U2s[j % 4], ST=STP[j % 4])
    for t in range(NT):
        if t + 3 < NT:
            norm_pre(t + 3, U2s[(t + 3) % 4], ST=STP[(t + 3) % 4])
        norm_T(t, 1, U2s[t % 4], U2T[:, :, tok0(t):tok0(t) + tsz(t)], U2T.b)
    act_rsqrt[0] = False
    groups = [(0, NMETA, [0])] + [(NMETA + 512 * g, 512, [1 + 4 * g + j for j in range(4)]) for g in range(4)]
    ods = [k.dsem(f"o{i}") for i in range(4)]
    it = [0]
    gcnt = [0]

    def ffn_up(qi, gi):
        c0, nq = quarters[qi]
        sl = qi % 2
        WA, WV = WA_[sl], WV_[sl]
        g0, N, tiles = groups[gi]
        MT = MT_[gi % 2]
        for ci in range(nq):
            c = c0 + ci
            j = it[0] % 2
            it[0] += 1
            ASB, Y = ASB_[j], Y_[j]
            psA = psn()
            for kc in range(8):
                k.op("pe", lambda e, kc=kc, psA=psA, ci=ci: e.matmul(psA.f[:, :N], lhsT=WA[:, kc, ci * 128:(ci + 1) * 128], rhs=U2T[:, kc, g0:g0 + N], start=(kc == 0), stop=(kc == 7)),
                     reads=[WA.b, U2T.b], writes=[psA.b])
            if gi == 0:
                k.op("act", lambda e, psA=psA, c=c: e.copy(out=HALO[:, c, :], in_=psA.f[:, N - 2:N]), reads=[psA.b], writes=[HALO.b])
                yield
                continue
            psV = psn()
            for kc in range(8):
                k.op("pe", lambda e, kc=kc, psV=psV, ci=ci: e.matmul(psV.f[:, :N], lhsT=WV[:, kc, ci * 128:(ci + 1) * 128], rhs=U2T[:, kc, g0:g0 + N], start=(kc == 0), stop=(kc == 7)),
                     reads=[WV.b, U2T.b], writes=[psV.b])
            k.op("act", lambda e, psA=psA, ASB=ASB: e.copy(out=ASB[:, 2:2 + N], in_=psA.f[:, :N]), reads=[psA.b], writes=[ASB.b])
            k.op("act", lambda e, ASB=ASB, c=c: e.copy(out=ASB[:, 0:2], in_=HALO[:, c, :]), reads=[HALO.b, ASB.b], writes=[ASB.b])
            k.op("act", lambda e, ASB=ASB, c=c: e.copy(out=HALO[:, c, :], in_=ASB[:, N:N + 2]), reads=[ASB.b, HALO.b], writes=[HALO.b])
            yield
            k.op("dve", lambda e, ASB=ASB, Y=Y, c=c: e.tensor_scalar(out=Y[:, :N], in0=ASB[:, 2:2 + N], scalar1=CWB[:, c, 2:3], scalar2=CWB[:, c, 3:4], op0=ALU.mult, op1=ALU.add),
                 reads=[ASB.b, CWB.b], writes=[Y.b])
            k.op("dve", lambda e, ASB=ASB, Y=Y, c=c: e.scalar_tensor_tensor(out=Y[:, :N], in0=ASB[:, 1:1 + N], scalar=CWB[:, c, 1:2], in1=Y[:, :N], op0=ALU.mult, op1=ALU.add),
                 reads=[ASB.b, CWB.b, Y.b], writes=[Y.b])
            k.op("dve", lambda e, ASB=ASB, Y=Y, c=c: e.scalar_tensor_tensor(out=Y[:, :N], in0=ASB[:, 0:N], scalar=CWB[:, c, 0:1], in1=Y[:, :N], op0=ALU.mult, op1=ALU.add),
                 reads=[ASB.b, CWB.b, Y.b], writes=[Y.b])
            k.op("act", lambda e, Y=Y: e.activation(out=Y[:, :N], in_=Y[:, :N], func=AF.Gelu_apprx_tanh), reads=[Y.b], writes=[Y.b])
            k.op("dve", lambda e, Y=Y, psV=psV, ci=ci: e.tensor_tensor(out=MT[:, ci, :N], in0=Y[:, :N], in1=psV.f[:, :N], op=ALU.mult),
                 reads=[Y.b, psV.b], writes=[MT.b])
            yield

    pend = []

    def final_norm(t):
        jk = U2_[t % 2]
        ss = STF[:, 0:1]
        rs = STF[:, 1:2]
        k.op("act", lambda e: e.activation(out=jk[:, :], in_=H[:, t, :], func=AF.Square, accum_out=ss),
             reads=[Hb[t]], writes=[jk.b, STF.b])
        yield
        k.op("dve", lambda e: e.tensor_scalar(out=rs, in0=ss, scalar1=1.0 / D, scalar2=EPS, op0=ALU.mult, op1=ALU.add),
             reads=[STF.b], writes=[STF.b])
        yield
        k.op("pool", lambda e: e.tensor_tensor(out=rs, in0=rs, in1=NEGH[:, 0:1], op=ALU.pow), reads=[STF.b, NEGH.b], writes=[STF.b])
        yield
        k.op("dve", lambda e: e.scalar_tensor_tensor(out=OUT[:, :], in0=H[:, t, :], scalar=rs, in1=FNORM[:, :], op0=ALU.mult, op1=ALU.mult),
             reads=[Hb[t], STF.b, FNORM.b], writes=[OUT.b, U2x[0].b, U2x[1].b])
        yield
        k.op("sp", lambda e: e.dma_start(out=y_d[(t - 1) * 128:t * 128, :], in_=OUT[:, :]), reads=[OUT.b], dsem=ods[t % 4])
        yield

    def advance_pending():
        for g in list(pend[:1]):
            try:
                next(g)
            except StopIteration:
                pend.remove(g)

    def ffn_down(qi, gi):
        c0, nq = quarters[qi]
        sl = qi % 2
        WD = WD_[sl]
        g0, N, tiles = groups[gi]
        MT = MT_[gi % 2]
        last_q = qi == len(quarters) - 1
        yield
        yield
        for jt, t in enumerate(tiles):
            for hf in range(2):
                pd = psn()
                for ci in range(nq):
                    k.op("pe", lambda e, ci=ci, pd=pd, jt=jt, hf=hf: e.matmul(pd.f[:, :], lhsT=MT[:, ci, jt * 128:(jt + 1) * 128], rhs=WD[:, ci, hf * 512:(hf + 1) * 512], start=(ci == 0), stop=(ci == nq - 1)),
                         reads=[MT.b, WD.b], writes=[pd.b])
                k.op("dve", lambda e, pd=pd, t=t, hf=hf: e.tensor_tensor(out=H[:, t, hf * 512:(hf + 1) * 512], in0=pd.f[:, :], in1=H[:, t, hf * 512:(hf + 1) * 512], op=ALU.add),
                     reads=[pd.b, Hb[t]], writes=[Hb[t]])
                if last_q:
                    advance_pending()
                yield
                if last_q:
                    advance_pending()
            if last_q:
                pend.append(final_norm(t))
        if last_q and gi == len(groups) - 1:
            while pend:
                advance_pending()
                yield

    seq = [(qi, gi) for qi in range(len(quarters)) for gi in range(1, len(groups))]
    prev = None
    for (qi, gi) in seq:
        if gi == 1:
            interleave(("C", ffn_up(qi, 0)))
        interleave(("C", ffn_up(qi, gi)), ("D", ffn_down(*prev)) if prev is not None else None)
        if prev is not None and prev[1] == len(groups) - 1 and prev[0] + 2 < len(quarters):
            load_quarter(prev[0] + 2)
        prev = (qi, gi)
    interleave(("D", ffn_down(*prev)))

    finals = list(ods)
    if debug:
        finals.append(dd)
    k.generate(final_waits=finals)
    return nc


def _consts():
    ident = np.eye(128, dtype=np.float32)
    kk = np.arange(128)[:, None]
    qq = np.arange(128)[None, :]
    m_cur = (kk <= qq).astype(np.float32)
    m_prev = (kk > qq).astype(np.float32)
    m_p01 = (kk > qq - 112).astype(np.float32)
    masks = np.stack([m_cur, m_prev, m_p01]).astype(np.float32)
    utri = (kk <= qq).astype(np.float32) * np.float32(-1.0 / 16.0)
    half = 32
    inv_freq = (10000.0 ** (-np.arange(half, dtype=np.float32) / half)).astype(np.float32)
    rope = np.zeros((128, NT, 128), np.float32)
    for t in range(NT):
        P = tsz(t)
        pos = (tok0(t) + np.arange(P)).astype(np.float32)
        ang = pos[:, None] * inv_freq[None, :]
        c = np.cos(ang).astype(np.float32)
        s = np.sin(ang).astype(np.float32)
        rope[:P, t, 0:32] = c
        rope[:P, t, 32:64] = c
        rope[:P, t, 64:96] = s
        rope[:P, t, 96:128] = -s
    return dict(c_ident=ident, c_masks=masks, c_utri=utri, c_rope=rope)


_CACHE = {}


def kernel(**inputs):
    debug = inputs.pop("_debug", None)
    key = debug
    if key not in _CACHE:
        _CACHE[key] = build(debug)
    nc = _CACHE[key]
    consts = _consts()
    x = np.ascontiguousarray(inputs["x"], dtype=np.float32)
    B = x.shape[0]
    shared = {n: np.ascontiguousarray(np.asarray(v, dtype=np.float32)) for n, v in inputs.items() if n != "x"}
    shared.update(consts)
    in_maps = []
    for b in range(B):
        m = dict(shared)
        m["x"] = x[b]
        in_maps.append(m)
    res = run_bass_kernel_spmd(nc, in_maps, core_ids=list(range(B)))
    out = np.stack([np.asarray(r["y"], dtype=np.float32) for r in res.results], axis=0)
    if debug:
        return out, np.stack([np.asarray(r["dbg"]) for r in res.results], axis=0)
    return out
```
